# Optimizing a Trainium2 kernel written in Bass

```python
import math
import jax, jax.numpy as jnp
from jax import lax
import numpy as np

D_MODEL = 2048
BATCH = 4
SEQ = 2048
DEPTH = 1
DEC_BATCH = 32
DEC_SEQ = 4
PAST_LEN = 8192
PAGE_SIZE = 128

SSM_WIDTH = D_MODEL // 2
SSM_GROUP = 16
SSM_GROUPS = SSM_WIDTH // SSM_GROUP
SSM_STATE = 64
DT_MIN = 0.001
DT_MAX = 0.1
N_HEADS = 8
HEAD_DIM = 128
N_KV_HEADS = 2
N_REP = N_HEADS // N_KV_HEADS
ATTN_WIDTH = N_HEADS * HEAD_DIM
KV_WIDTH = N_KV_HEADS * HEAD_DIM
ROT_DIM = HEAD_DIM // 4
ROPE_THETA = 500000.0
IDX_HEADS = 16
IDX_DIM = 64
IDX_ROT_DIM = IDX_DIM // 4
TOPK_MAX = 256
Q_BLOCK = 128
D_FF = 5632
FFN_RES = 0.5
EPS = 1e-6

IN_SIZES = [SSM_WIDTH, ATTN_WIDTH, KV_WIDTH, KV_WIDTH, IDX_HEADS * IDX_DIM, IDX_DIM, IDX_HEADS, D_MODEL, D_MODEL]
IN_WIDTH = sum(IN_SIZES)
IN_SPLITS = [sum(IN_SIZES[: i + 1]) for i in range(len(IN_SIZES) - 1)]

kernel_name = "hybrid_s5_dsa_macaron_step"


def _rms_norm(x, g):
    xf = x.astype(jnp.float32)
    y = xf * lax.rsqrt(jnp.mean(xf * xf, axis=-1, keepdims=True) + EPS)
    return (y * g.astype(jnp.float32)).astype(x.dtype)


def _swiglu(x, w_gate, w_up, w_down):
    return (jax.nn.silu(x @ w_gate) * (x @ w_up)) @ w_down


def _rope_partial(x, pos, rot_dim):
    half = rot_dim // 2
    freqs = ROPE_THETA ** (-jnp.arange(half, dtype=jnp.float32) * 2.0 / rot_dim)
    ang = pos.astype(jnp.float32)[:, None] * freqs[None, :]
    cos = jnp.cos(ang)[None, :, None, :]
    sin = jnp.sin(ang)[None, :, None, :]
    xr = x[..., :rot_dim].astype(jnp.float32)
    x1, x2 = xr[..., :half], xr[..., half:]
    rot = jnp.concatenate([x1 * cos - x2 * sin, x2 * cos + x1 * sin], axis=-1).astype(x.dtype)
    return jnp.concatenate([rot, x[..., rot_dim:]], axis=-1)


def _take_rows(t, idx):
    return jax.vmap(lambda tb, ib: tb[ib])(t, idx)


def _ssm_discretize(lam_re, lam_im, b_re, b_im, log_dt):
    f32 = jnp.float32
    lam = lax.complex(lam_re.astype(f32), lam_im.astype(f32))
    dt = jnp.exp(log_dt.astype(f32))[:, None]
    lam_bar = jnp.exp(lam * dt)
    b = lax.complex(b_re.astype(f32), b_im.astype(f32))
    b_bar = ((lam_bar - 1.0) / lam)[..., None] * b
    return lam_bar, b_bar


def _ssm_scan(u, h0, lam_bar, b_bar, c, d):
    uf = u.astype(jnp.float32)
    bu = jnp.einsum('gph,bsgh->bsgp', b_bar, uf.astype(jnp.complex64))
    bu = bu.at[:, 0].add(lam_bar[None] * h0)
    a = jnp.broadcast_to(lam_bar, bu.shape)

    def combine(e1, e2):
        a1, b1 = e1
        a2, b2 = e2
        return a1 * a2, a2 * b1 + b2

    _, h = lax.associative_scan(combine, (a, bu), axis=1)
    y = jnp.real(jnp.einsum('ghp,bsgp->bsgh', c, h)) + d.astype(jnp.float32) * uf
    return y.astype(u.dtype), h[:, -1]


def _indexer_select(qi, wi, ki, q_pos, top_k):
    s = jnp.einsum('bqhd,bld->bqhl', qi, ki).astype(jnp.float32) * (IDX_DIM ** -0.5)
    s = jnp.einsum('bqhl,bqh->bql', jax.nn.relu(s), wi.astype(jnp.float32))
    k_pos = jnp.arange(ki.shape[1])
    allowed = k_pos[None, None, :] <= q_pos[None, :, None]
    s = jnp.where(allowed, s, -jnp.inf)
    _, sel = lax.top_k(s, top_k)
    valid = sel <= q_pos[None, :, None]
    return sel, valid


def _sparse_attend(q, k_sel, v_sel, valid):
    b_, q_ = q.shape[:2]
    qg = q.reshape(b_, q_, N_KV_HEADS, N_REP, HEAD_DIM)
    logits = jnp.einsum('bqgrd,bqkgd->bqgrk', qg, k_sel).astype(jnp.float32) * (HEAD_DIM ** -0.5)
    logits = jnp.where(valid[:, :, None, None, :], logits, -jnp.inf)
    prob = jax.nn.softmax(logits, axis=-1).astype(v_sel.dtype)
    out = jnp.einsum('bqgrk,bqkgd->bqgrd', prob, v_sel)
    return out.reshape(b_, q_, ATTN_WIDTH)


def _prompt_attention(q, k, v, qi, wi, ki):
    b_, s_ = q.shape[:2]
    top_k = min(TOPK_MAX, s_ // 4)
    nb = s_ // Q_BLOCK

    def to_blocks(t):
        return jnp.moveaxis(t.reshape(b_, nb, Q_BLOCK, *t.shape[2:]), 1, 0)

    pos_blocks = jnp.arange(s_).reshape(nb, Q_BLOCK)

    def one_block(args):
        qb, qib, wib, pb = args
        sel, valid = _indexer_select(qib, wib, ki, pb, top_k)
        return _sparse_attend(qb, _take_rows(k, sel), _take_rows(v, sel), valid)

    out = lax.map(one_block, (to_blocks(q), to_blocks(qi), to_blocks(wi), pos_blocks))
    return jnp.moveaxis(out, 0, 1).reshape(b_, s_, ATTN_WIDTH)


def _sample_attention(q, k_new, v_new, qi, wi, ki_new, cache_k, cache_v, cache_idx_k, page_table):
    db, t_ = q.shape[:2]
    n_pages = page_table.shape[1]
    past = n_pages * PAGE_SIZE
    top_k = min(TOPK_MAX, (past + t_) // 4)
    ki_past = cache_idx_k[page_table].reshape(db, past, IDX_DIM)
    ki_all = jnp.concatenate([ki_past, ki_new.astype(ki_past.dtype)], axis=1)
    q_pos = past + jnp.arange(t_)
    sel, valid = _indexer_select(qi, wi, ki_all, q_pos, top_k)
    in_past = (sel < past)[..., None, None]
    page_ix = jnp.minimum(sel // PAGE_SIZE, n_pages - 1)
    phys = _take_rows(page_table, page_ix)
    off = sel % PAGE_SIZE
    new_ix = jnp.clip(sel - past, 0, t_ - 1)
    k_sel = jnp.where(in_past, cache_k[phys, off], _take_rows(k_new, new_ix).astype(cache_k.dtype))
    v_sel = jnp.where(in_past, cache_v[phys, off], _take_rows(v_new, new_ix).astype(cache_v.dtype))
    return _sparse_attend(q, k_sel.astype(q.dtype), v_sel.astype(q.dtype), valid)


def _token_mix(x, pos, h0, attend_fn, p, lam_bar, b_bar, c_ssm):
    b_, s_ = x.shape[:2]
    h = _rms_norm(x, p['mix_norm'])
    z = h @ p['w_in']
    u, q, k, v, qi, ki, wi, ga, gb = jnp.split(z, IN_SPLITS, axis=-1)
    u = u.reshape(b_, s_, SSM_GROUPS, SSM_GROUP)
    y_ssm, h_last = _ssm_scan(u, h0, lam_bar, b_bar, c_ssm, p['ssm_d'])
    za = jax.nn.gelu(y_ssm.reshape(b_, s_, SSM_WIDTH))
    a_out = za * jax.nn.sigmoid(za @ p['glu_w'] + p['glu_b'])
    q = _rope_partial(_rms_norm(q.reshape(b_, s_, N_HEADS, HEAD_DIM), p['q_norm']), pos, ROT_DIM)
    k = _rope_partial(_rms_norm(k.reshape(b_, s_, N_KV_HEADS, HEAD_DIM), p['k_norm']), pos, ROT_DIM)
    v = v.reshape(b_, s_, N_KV_HEADS, HEAD_DIM)
    qi = _rope_partial(qi.reshape(b_, s_, IDX_HEADS, IDX_DIM), pos, IDX_ROT_DIM)
    ki = _rope_partial(ki[:, :, None, :], pos, IDX_ROT_DIM)[:, :, 0, :]
    wi = wi * (IDX_HEADS ** -0.5)
    b_out = attend_fn(q, k, v, qi, wi, ki)
    merged = jax.nn.sigmoid(ga) * (a_out @ p['w_branch_a']) + jax.nn.sigmoid(gb) * (b_out @ p['w_branch_b'])
    return x + merged @ p['w_out'], (k, v, ki, h_last)


def _layer(x_p, x_s, ck, cv, cik, s_re, s_im, page_table, p):
    f32 = jnp.float32
    past = page_table.shape[1] * PAGE_SIZE
    s_len, t_len = x_p.shape[1], x_s.shape[1]
    x_p = x_p + FFN_RES * _swiglu(_rms_norm(x_p, p['ffn1_norm']), p['ffn1_w_gate'], p['ffn1_w_up'], p['ffn1_w_down'])
    x_s = x_s + FFN_RES * _swiglu(_rms_norm(x_s, p['ffn1_norm']), p['ffn1_w_gate'], p['ffn1_w_up'], p['ffn1_w_down'])
    lam_bar, b_bar = _ssm_discretize(p['ssm_lambda_re'], p['ssm_lambda_im'], p['ssm_b_re'], p['ssm_b_im'], p['ssm_log_dt'])
    c_ssm = lax.complex(p['ssm_c_re'].astype(f32), p['ssm_c_im'].astype(f32))
    h0_p = jnp.zeros((x_p.shape[0], SSM_GROUPS, SSM_STATE), jnp.complex64)
    h0_s = lax.complex(s_re.astype(f32), s_im.astype(f32))

    def attn_s(q, k, v, qi, wi, ki):
        return _sample_attention(q, k, v, qi, wi, ki, ck, cv, cik, page_table)

    x_p, (kp, vp, kip, hp) = _token_mix(x_p, jnp.arange(s_len), h0_p, _prompt_attention, p, lam_bar, b_bar, c_ssm)
    x_s, (ks, vs, kis, hs) = _token_mix(x_s, past + jnp.arange(t_len), h0_s, attn_s, p, lam_bar, b_bar, c_ssm)
    x_p = x_p + FFN_RES * _swiglu(_rms_norm(x_p, p['ffn2_norm']), p['ffn2_w_gate'], p['ffn2_w_up'], p['ffn2_w_down'])
    x_s = x_s + FFN_RES * _swiglu(_rms_norm(x_s, p['ffn2_norm']), p['ffn2_w_gate'], p['ffn2_w_up'], p['ffn2_w_down'])
    rows = (kp, vp, kip, jnp.real(hp), jnp.imag(hp), ks, vs, kis, jnp.real(hs), jnp.imag(hs))
    return x_p, x_s, rows


def setup_inputs(seed: int = 0) -> dict:
    key = jax.random.key(seed)
    k = jax.random.split(key, 33)
    f32 = jnp.float32
    n_pages = PAST_LEN // PAGE_SIZE
    n_phys = (DEC_BATCH * n_pages * 5) // 4

    def nrm(kk, shape, scale):
        return jax.random.normal(kk, shape, f32) * scale

    def gain(kk, n):
        return 1.0 + 0.02 * jax.random.normal(kk, (DEPTH, n), f32)

    page_table = jax.random.permutation(k[7], n_phys)[: DEC_BATCH * n_pages].reshape(DEC_BATCH, n_pages).astype(jnp.int32)
    lam_re = -0.5 + 0.01 * jax.random.normal(k[16], (DEPTH, SSM_GROUPS, SSM_STATE), f32)
    lam_im = math.pi * jnp.arange(SSM_STATE, dtype=f32)[None, None, :] + 0.01 * jax.random.normal(k[17], (DEPTH, SSM_GROUPS, SSM_STATE), f32)
    log_dt = jax.random.uniform(k[23], (DEPTH, SSM_GROUPS), f32, math.log(DT_MIN), math.log(DT_MAX))
    return {
        'x_prompt': nrm(k[0], (BATCH, SEQ, D_MODEL), 1.0),
        'x_sample': nrm(k[1], (DEC_BATCH, DEC_SEQ, D_MODEL), 1.0),
        'cache_k': nrm(k[2], (DEPTH, n_phys, PAGE_SIZE, N_KV_HEADS, HEAD_DIM), 1.0),
        'cache_v': nrm(k[3], (DEPTH, n_phys, PAGE_SIZE, N_KV_HEADS, HEAD_DIM), 1.0),
        'cache_idx_k': nrm(k[4], (DEPTH, n_phys, PAGE_SIZE, IDX_DIM), 1.0),
        'state_ssm_re': nrm(k[5], (DEPTH, DEC_BATCH, SSM_GROUPS, SSM_STATE), 0.5),
        'state_ssm_im': nrm(k[6], (DEPTH, DEC_BATCH, SSM_GROUPS, SSM_STATE), 0.5),
        'page_table': page_table,
        'ffn1_norm': gain(k[8], D_MODEL),
        'ffn1_w_gate': nrm(k[9], (DEPTH, D_MODEL, D_FF), D_MODEL ** -0.5),
        'ffn1_w_up': nrm(k[10], (DEPTH, D_MODEL, D_FF), D_MODEL ** -0.5),
        'ffn1_w_down': nrm(k[11], (DEPTH, D_FF, D_MODEL), D_FF ** -0.5),
        'mix_norm': gain(k[12], D_MODEL),
        'w_in': nrm(k[13], (DEPTH, D_MODEL, IN_WIDTH), D_MODEL ** -0.5),
        'q_norm': gain(k[14], HEAD_DIM),
        'k_norm': gain(k[15], HEAD_DIM),
        'ssm_lambda_re': lam_re,
        'ssm_lambda_im': lam_im,
        'ssm_b_re': nrm(k[18], (DEPTH, SSM_GROUPS, SSM_STATE, SSM_GROUP), (2 * SSM_GROUP) ** -0.5),
        'ssm_b_im': nrm(k[19], (DEPTH, SSM_GROUPS, SSM_STATE, SSM_GROUP), (2 * SSM_GROUP) ** -0.5),
        'ssm_c_re': nrm(k[20], (DEPTH, SSM_GROUPS, SSM_GROUP, SSM_STATE), SSM_STATE ** -0.5),
        'ssm_c_im': nrm(k[21], (DEPTH, SSM_GROUPS, SSM_GROUP, SSM_STATE), SSM_STATE ** -0.5),
        'ssm_d': nrm(k[22], (DEPTH, SSM_GROUPS, SSM_GROUP), 0.5),
        'ssm_log_dt': log_dt,
        'glu_w': nrm(k[24], (DEPTH, SSM_WIDTH, SSM_WIDTH), SSM_WIDTH ** -0.5),
        'glu_b': nrm(k[25], (DEPTH, SSM_WIDTH), 0.01),
        'w_branch_a': nrm(k[26], (DEPTH, SSM_WIDTH, D_MODEL), SSM_WIDTH ** -0.5),
        'w_branch_b': nrm(k[27], (DEPTH, ATTN_WIDTH, D_MODEL), ATTN_WIDTH ** -0.5),
        'w_out': nrm(k[28], (DEPTH, D_MODEL, D_MODEL), D_MODEL ** -0.5),
        'ffn2_norm': gain(k[29], D_MODEL),
        'ffn2_w_gate': nrm(k[30], (DEPTH, D_MODEL, D_FF), D_MODEL ** -0.5),
        'ffn2_w_up': nrm(k[31], (DEPTH, D_MODEL, D_FF), D_MODEL ** -0.5),
        'ffn2_w_down': nrm(k[32], (DEPTH, D_FF, D_MODEL), D_FF ** -0.5),
    }


def reference(x_prompt, x_sample, cache_k, cache_v, cache_idx_k, state_ssm_re, state_ssm_im, page_table,
              ffn1_norm, ffn1_w_gate, ffn1_w_up, ffn1_w_down, mix_norm, w_in, q_norm, k_norm,
              ssm_lambda_re, ssm_lambda_im, ssm_b_re, ssm_b_im, ssm_c_re, ssm_c_im, ssm_d, ssm_log_dt,
              glu_w, glu_b, w_branch_a, w_branch_b, w_out, ffn2_norm, ffn2_w_gate, ffn2_w_up, ffn2_w_down):
    y_p, y_s = x_prompt, x_sample
    new = [[] for _ in range(10)]
    for l in range(DEPTH):
        p = dict(
            ffn1_norm=ffn1_norm[l], ffn1_w_gate=ffn1_w_gate[l], ffn1_w_up=ffn1_w_up[l], ffn1_w_down=ffn1_w_down[l],
            mix_norm=mix_norm[l], w_in=w_in[l], q_norm=q_norm[l], k_norm=k_norm[l],
            ssm_lambda_re=ssm_lambda_re[l], ssm_lambda_im=ssm_lambda_im[l],
            ssm_b_re=ssm_b_re[l], ssm_b_im=ssm_b_im[l], ssm_c_re=ssm_c_re[l], ssm_c_im=ssm_c_im[l],
            ssm_d=ssm_d[l], ssm_log_dt=ssm_log_dt[l], glu_w=glu_w[l], glu_b=glu_b[l],
            w_branch_a=w_branch_a[l], w_branch_b=w_branch_b[l], w_out=w_out[l],
            ffn2_norm=ffn2_norm[l], ffn2_w_gate=ffn2_w_gate[l], ffn2_w_up=ffn2_w_up[l], ffn2_w_down=ffn2_w_down[l],
        )
        y_p, y_s, rows = _layer(y_p, y_s, cache_k[l], cache_v[l], cache_idx_k[l],
                                state_ssm_re[l], state_ssm_im[l], page_table, p)
        for lst, r in zip(new, rows):
            lst.append(r)
    k_p, v_p, ik_p, re_p, im_p, k_s, v_s, ik_s, re_s, im_s = [jnp.stack(lst) for lst in new]
    return (y_p, y_s, k_p, v_p, ik_p, re_p, im_p, k_s, v_s, ik_s, re_s, im_s)
```

```python
import contextlib
import numpy as np
import ml_dtypes
import concourse.bass as bass
import concourse.mybir as mybir
from concourse.bass_utils import run_bass_kernel_spmd

F32 = mybir.dt.float32
BF16 = mybir.dt.bfloat16
I32 = mybir.dt.int32
ALU = mybir.AluOpType
AF = mybir.ActivationFunctionType
AX = mybir.AxisListType

D = 2048
DFF = 5632
NKC = D // 128
NFF = DFF // 128
SEQ = 2048
NB = 512
NBLK = SEQ // NB
NS = 16
NTOK = SEQ + NS
EPS = 1e-6
NEG = -1.0e30
TOPK = 256
PAST = 8192
NPAGE = 64
N_CORES = 8


class V:
    __slots__ = ("ap", "keys")

    def __init__(self, ap, keys):
        self.ap = ap
        self.keys = keys


class Buf:
    def __init__(self, handle, name):
        self.h = handle
        self.name = name

    def __getitem__(self, idx):
        return V(self.h[idx], ((self.name, None),))

    def sub(self, k, idx):
        return V(self.h[idx], ((self.name, k),))

    def multi(self, ks, idx):
        return V(self.h[idx], tuple((self.name, k) for k in ks))


def vv(ap, *views):
    keys = tuple(k for v in views for k in v.keys)
    return V(ap, keys)


class Op:
    __slots__ = ("idx", "eng", "emit", "dma", "deps", "inc", "cnt", "sem", "val", "prev_val")

    def __init__(self, idx, eng, emit, dma):
        self.idx = idx
        self.eng = eng
        self.emit = emit
        self.dma = dma
        self.deps = set()
        self.inc = False
        self.cnt = 0
        self.sem = None
        self.val = 0
        self.prev_val = 0


class Sched:
    def __init__(self):
        self.ops = []
        self.last_writer = {}
        self.readers = {}

    @staticmethod
    def _conf(d, name, sub):
        e = d.get(name)
        if not e:
            return []
        if sub is None:
            return list(e.values())
        out = []
        if sub in e:
            out.append(e[sub])
        if None in e:
            out.append(e[None])
        return out

    def add(self, eng, emit, reads=(), writes=(), dma=False):
        op = Op(len(self.ops), eng, emit, dma)
        rk = [k for v in reads if isinstance(v, V) for k in v.keys]
        wk = [k for v in writes if isinstance(v, V) for k in v.keys]
        for (name, sub) in rk:
            for w in self._conf(self.last_writer, name, sub):
                op.deps.add(w)
            if name.startswith("ps"):
                for rs in self._conf(self.readers, name, sub):
                    op.deps |= {r for r in rs if r.eng != eng}
        for (name, sub) in wk:
            for w in self._conf(self.last_writer, name, sub):
                op.deps.add(w)
            for rs in self._conf(self.readers, name, sub):
                op.deps |= rs
        for (name, sub) in rk:
            self.readers.setdefault(name, {}).setdefault(sub, set()).add(op)
        for (name, sub) in wk:
            lw = self.last_writer.setdefault(name, {})
            rd = self.readers.setdefault(name, {})
            if sub is None:
                lw.clear()
                rd.clear()
            lw[sub] = op
            rd[sub] = set()
        op.deps.discard(op)
        self.ops.append(op)
        return op

    def emit_all(self, nc, nsem_dma=14):
        engs = ["pe", "act", "dve", "pool", "sp"]
        per = {e: [] for e in engs}
        for op in self.ops:
            per[op.eng].append(op)
        for op in self.ops:
            for d in op.deps:
                if d.dma:
                    continue
                if d.eng == "pe" and op.eng == "pe" and not op.dma:
                    continue
                d.inc = True
        cnt = {e: 0 for e in engs}
        for op in self.ops:
            if not op.dma and op.inc:
                cnt[op.eng] += 1
                op.cnt = cnt[op.eng]
        with contextlib.ExitStack() as st:
            csem = {e: st.enter_context(nc.semaphore("c_" + e)) for e in ["pe", "act", "dve", "pool"]}
            dsem = {e: [st.enter_context(nc.semaphore("d_%s%d" % (e, i))) for i in range(nsem_dma)]
                    for e in ["pool", "sp"]}
            duse = {e: [0] * nsem_dma for e in dsem}
            dnext = {e: 0 for e in dsem}
            for op in self.ops:
                if op.dma:
                    i = dnext[op.eng]
                    dnext[op.eng] = (i + 1) % nsem_dma
                    op.sem = dsem[op.eng][i]
                    op.prev_val = duse[op.eng][i]
                    duse[op.eng][i] += 16
                    op.val = duse[op.eng][i]
            block = st.enter_context(nc.Block())

            def run(engname, e):
                waited = {}

                def w(sem, val):
                    if val <= 0 or waited.get(sem.name, 0) >= val:
                        return
                    waited[sem.name] = val
                    e.wait_ge(sem, val)

                for op in per[engname]:
                    for d in sorted(op.deps, key=lambda o: o.idx):
                        if d.dma:
                            w(d.sem, d.val)
                        else:
                            if d.eng == "pe" and engname == "pe" and not op.dma:
                                continue
                            w(csem[d.eng], d.cnt)
                    if op.dma:
                        w(op.sem, op.prev_val)
                        ins = op.emit(e)
                        ins.then_inc(op.sem, 16)
                    else:
                        ins = op.emit(e)
                        if op.inc:
                            ins.then_inc(csem[engname], 1)
                if engname in dsem:
                    for i, s in enumerate(dsem[engname]):
                        w(s, duse[engname][i])

            @block.tensor
            def _(e):
                run("pe", e)

            @block.scalar
            def _(e):
                run("act", e)

            @block.vector
            def _(e):
                run("dve", e)

            @block.gpsimd
            def _(e):
                run("pool", e)

            @block.sync
            def _(e):
                run("sp", e)


WMIX_TILES = 46


def build_nc(stage_limit=99):
    import os as _os
    KSUB = int(_os.environ.get("KSUB", "99"))
    KATT = int(_os.environ.get("KATT", "1"))
    KV = int(_os.environ.get("KV", "3"))
    nc = bass.Bass("TRN2", target_bir_lowering=False)

    def din(name, shape, dt=F32):
        return nc.dram_tensor(name, list(shape), dt, kind="ExternalInput").ap()

    def dout(name, shape, dt=F32):
        return nc.dram_tensor(name, list(shape), dt, kind="ExternalOutput").ap()

    xT_d = din("xT", [D, NTOK])
    w1g_d = din("w1g", [D, DFF]); w1u_d = din("w1u", [D, DFF]); w1d_d = din("w1d", [DFF, D])
    w2g_d = din("w2g", [D, DFF]); w2u_d = din("w2u", [D, DFF]); w2d_d = din("w2d", [DFF, D])
    wmix_d = din("wmix", [D, WMIX_TILES * 128])
    wtok_d = din("wtok", [D, 512])
    wgab_d = din("wgab", [D, 4096])
    glw_d = din("glw", [1024, 1024]); wa_d = din("wa", [1024, D]); wb_d = din("wb", [1024, D]); wo_d = din("wo", [D, D])
    nrm_d = din("nrm", [128, 3 * 16])
    qkn_d = din("qkn", [128, 4])
    glb_d = din("glb", [128, 8])
    rope_d = din("rope", [4, 128, NTOK])
    ident_d = din("ident", [128, 128]); tri_d = din("tri", [128, 128]); iota_d = din("iota1", [128, NB])
    ssmp_d = din("ssmp", [128, 3 * 32])
    ssmd_d = din("ssmd", [128, 8])
    bsre_d = din("bsre", [128, 32 * 128]); bsim_d = din("bsim", [128, 32 * 128])
    cre_d = din("cre", [128, 32 * 128]); cim_d = din("cim", [128, 32 * 128])
    h0_d = din("h0", [128, 2 * 32 * 4])
    hmask_d = din("hmask", [128, 2])
    ck_d = din("ckh", [40960, 2048]); cv_d = din("cvh", [40960, 2048]); cik_d = din("cikh", [40960, 512])
    ptl_d = din("ptl", [128, 4], I32)
    mnew_d = din("mnew", [16, 16])

    yT_o = dout("yT", [D, NTOK])
    kT_o = dout("kTo", [256, NTOK])
    vT_o = dout("vTo", [256, NTOK])
    kiT_o = dout("kiTo", [128, NTOK])
    sp_o = dout("ssp", [128, 2 * 32])
    ss_o = dout("sss", [128, 2 * 32 * 4])

    S = Sched()
    with contextlib.ExitStack() as st:
        def sb(name, shape, dt):
            return Buf(st.enter_context(nc.sbuf_tensor("s_" + name, list(shape), dt)), name)

        def psb(name, shape, dt):
            return Buf(st.enter_context(nc.psum_tensor("p_" + name, list(shape), dt)), name)

        xT = sb("xTs", [128, NKC, NB], F32)
        hT = sb("hT", [128, NKC, NB], BF16)
        arena = sb("arena", [128, NFF, NB], BF16)
        wsl = [sb("wsl%d" % i, [128, 8192], BF16) for i in range(2)]
        ps = [psb("ps%d" % i, [128, 512], F32) for i in range(8)]
        kTs = sb("kTs", [128, 2, SEQ], BF16)
        vS = sb("vS", [128, 16, 256], BF16)
        kiA = sb("kiA", [128, SEQ], BF16)
        kiB = sb("kiB", [128, SEQ], BF16)
        Dd = sb("Dd", [128, 8, 128], BF16)
        identb = sb("identb", [128, 128], BF16); identf = sb("identf", [128, 128], F32)
        onesb = sb("onesb", [128, 128], BF16)
        tri = sb("tri", [128, 128], F32)
        hmask = sb("hmask", [128, 2], F32)
        iota1 = sb("iota1s", [128, NB], F32)
        nrm = sb("nrm", [128, 48], F32); qkn = sb("qkn", [128, 4], F32); glb = sb("glb", [128, 8], F32)
        ssmp = sb("ssmp", [128, 96], F32); ssmd = sb("ssmd", [128, 8], F32)
        sdec = sb("sdec", [128, 32], F32)
        sfrq = sb("sfrq", [128, 32], F32)
        lbre = sb("lbre", [128, 32], F32); lbim = sb("lbim", [128, 32], F32)
        gre = sb("gre", [128, 32], F32); gim = sb("gim", [128, 32], F32)
        stp = sb("stp", [128, 8, 32], F32)
        car = sb("car", [128, 2, 32], F32)
        h0s = sb("h0s", [128, 2, 32, 4], F32)
        hss = sb("hss", [128, 2, 32, 4], F32)
        sq = sb("sq", [128, 4, NB], BF16)
        rstd = sb("rstd", [128, NB], F32)
        tA = sb("tA", [128, NB], F32); tB = sb("tB", [128, NB], F32); tC = sb("tC", [128, NB], F32)
        tD = sb("tD", [128, NB], F32); tE = sb("tE", [128, NB], F32); tF = sb("tF", [128, NB], F32)
        tI = sb("tI", [128, NB], I32)
        cosT = sb("cosT", [128, NB], F32); sinT = sb("sinT", [128, NB], F32)
        hre = sb("hre", [128, NB], BF16); him = sb("him", [128, NB], BF16)
        mskT = sb("mskT", [128, 16, 128], BF16)
        rl = [sb("rl%d" % i, [128, NB], BF16) for i in range(2)]
        pe_ = [sb("pe%d" % i, [128, NB], BF16) for i in range(2)]
        Dm = sb("Dm", [128, 16, 128], BF16)
        wtk = sb("wtk", [128, 4, 16], F32)
        bs = sb("bs", [128, 8], F32)
        sA = sb("sA", [128, NFF, NS], BF16)
        sQI = sb("sQI", [128, 16, NS], BF16)
        DmS = sb("DmS", [16, 16, 16], BF16)
        ptl = sb("ptl", [128, 4], I32); idxh = sb("idxh", [128, 4], I32); idx8 = sb("idx8", [128, 4, 8], I32)
        mnew = sb("mnew", [16, 16], F32)
        bs2 = sb("bs2", [16, 8], F32)
        kst = sb("kst", [128, NB], F32)

        cur = {"sample": False}

        def slot(i, n=NB):
            if cur["sample"]:
                return sA.sub(i, (slice(None), i, slice(0, n)))
            return arena.sub(i, (slice(None), i, slice(0, n)))

        def slotc(i, a, b):
            return arena.sub(i, (slice(None), i, slice(a, b)))

        IscAP = arena.h[:, 24:32, :].rearrange("p s n -> p (s n)").bitcast(F32)
        mskAP = arena.h[:, 40:44, :].rearrange("p s n -> p (s n)")
        ISC_KEYS = arena.multi(range(24, 32), (slice(None), slice(24, 32), slice(None)))
        MSK_KEYS = arena.multi(range(40, 44), (slice(None), slice(40, 44), slice(None)))

        def Isc(a, b):
            return vv(IscAP[:, a:b], ISC_KEYS)

        def mskv(a, b):
            return vv(mskAP[:, a:b], MSK_KEYS)

        ps6b = ps[7].h[:].bitcast(BF16)

        A_U, A_Q, A_QI, A_ZA, A_BO = 0, 8, 16, 24, 32

        def dma(eng, out, in_, reads=(), writes=()):
            S.add(eng, lambda e: e.dma_start(out=out.ap if isinstance(out, V) else out,
                                             in_=in_.ap if isinstance(in_, V) else in_),
                  reads=list(reads), writes=list(writes), dma=True)

        def mm(out, lhsT, rhs, start, stop):
            S.add("pe", lambda e: e.matmul(out.ap, lhsT=lhsT.ap, rhs=rhs.ap, start=start, stop=stop),
                  reads=[lhsT, rhs], writes=[out])

        def act(out, in_, func, scale=1.0, bias=0.0, accum=None, extra_reads=()):
            kw = {}
            if accum is not None:
                kw["accum_out"] = accum.ap
            b = bias.ap if isinstance(bias, V) else bias
            sc = scale.ap if isinstance(scale, V) else scale
            S.add("act", lambda e: e.activation(out=out.ap, in_=in_.ap, func=func, bias=b, scale=sc, **kw),
                  reads=[in_, bias, scale] + list(extra_reads), writes=[out] + ([accum] if accum is not None else []))

        def tt(out, in0, in1, op, eng="dve"):
            S.add(eng, lambda e: e.tensor_tensor(out=out.ap, in0=in0.ap, in1=in1.ap, op=op),
                  reads=[in0, in1], writes=[out])

        def ts(out, in0, s1, s2, op0, op1=None, accum=None, eng="dve"):
            a1 = s1.ap if isinstance(s1, V) else s1
            a2 = s2.ap if isinstance(s2, V) else s2
            kw = {}
            if op1 is not None:
                kw["op1"] = op1
            if accum is not None:
                kw["accum_out"] = accum.ap
            S.add(eng, lambda e: e.tensor_scalar(out=out.ap, in0=in0.ap, scalar1=a1, scalar2=a2, op0=op0, **kw),
                  reads=[in0, s1, s2], writes=[out] + ([accum] if accum is not None else []))

        def stt(out, in0, scalar, in1, op0, op1, eng="dve"):
            a = scalar.ap if isinstance(scalar, V) else scalar
            S.add(eng, lambda e: e.scalar_tensor_tensor(out=out.ap, in0=in0.ap, scalar=a, in1=in1.ap, op0=op0, op1=op1),
                  reads=[in0, scalar, in1], writes=[out])

        def cp(out, in_, eng="dve"):
            S.add(eng, lambda e: e.tensor_copy(out=out.ap, in_=in_.ap), reads=[in_], writes=[out])

        def memset(out, val, eng="dve"):
            S.add(eng, lambda e: e.memset(out.ap, val), writes=[out])

        def recip(out, in_):
            S.add("dve", lambda e: e.reciprocal(out=out.ap, in_=in_.ap), reads=[in_], writes=[out])

        def transpose(out, in_, ident):
            S.add("pe", lambda e: e.transpose(out.ap, in_.ap, ident.ap), reads=[in_, ident], writes=[out])

        wstate = {"n": 0}

        def wload(parts, extra_reads=(), used=None):
            sl = wsl[wstate["n"] % 2]
            wstate["n"] += 1
            for (c0, ncols, nkc, src) in parts:
                dst = sl.h[:, c0:c0 + nkc * ncols].rearrange("p (k c) -> p k c", k=nkc)
                step = max(1, 1024 // 128 // max(1, 1)) if False else max(1, min(nkc, 8))
                for k0 in range(0, nkc, step):
                    k1 = min(nkc, k0 + step)
                    S.add("pool", lambda e, d=(dst[:, k0:k1, :] if used is None else dst[:, k0:k1, 0:used]), s_=src[:, k0:k1, :]: e.dma_start(out=d, in_=s_),
                          reads=list(extra_reads), writes=[sl[:]], dma=True)
            return sl

        def wview(sl, c0, ncols, kc, a, b):
            o = c0 + kc * ncols
            return sl[:, o + a:o + b]

        def wsrc(w_d, nkc, c0, c1):
            return w_d.rearrange("(k p) c -> p k c", p=128)[:, :, c0:c1]

        dma("sp", identf[:], ident_d, writes=[identf[:]])
        dma("pool", identb[:], ident_d, writes=[identb[:]])
        dma("sp", tri[:], tri_d, writes=[tri[:]])
        dma("sp", iota1[:], iota_d, writes=[iota1[:]])
        dma("sp", nrm[:], nrm_d, writes=[nrm[:]])
        dma("sp", qkn[:], qkn_d, writes=[qkn[:]])
        dma("sp", glb[:], glb_d, writes=[glb[:]])
        dma("sp", hmask[:], hmask_d, writes=[hmask[:]])
        dma("sp", ptl[:], ptl_d, writes=[ptl[:]])
        dma("sp", mnew[:], mnew_d, writes=[mnew[:]])
        ts(idxh[:], ptl[:], 2.0, None, ALU.mult)
        ts(idxh[:], idxh[:], hmask[:, 1:2], None, ALU.add)
        for ch in range(8):
            ts(idx8[:, :, ch], idxh[:], 8.0, float(ch), ALU.mult, ALU.add)
        dma("sp", ssmp[:], ssmp_d, writes=[ssmp[:]])
        dma("sp", ssmd[:], ssmd_d, writes=[ssmd[:]])
        dma("sp", vv(h0s.h[:].rearrange("p a j b -> p (a j b)"), h0s[:]), h0_d, writes=[h0s[:]])
        memset(onesb[:], 1.0)
        memset(hre[:], 0.0)
        memset(him[:], 0.0)
        memset(car[:], 0.0)
        for c in range(8):
            ts(Dd.sub(c, (slice(None), c, slice(None))), identf[:], ssmd[:, c:c + 1], None, ALU.mult)

        lre = ssmp[:, 0:32]; lim = ssmp[:, 32:64]; ldt = ssmp[:, 64:96]

        def T(i):
            return stp.sub(i, (slice(None), i, slice(None)))
        act(T(0), ldt, AF.Exp)
        tt(T(1), lre, T(0), ALU.mult)
        tt(T(2), lim, T(0), ALU.mult)
        act(sdec[:], T(1), AF.Exp)
        ts(T(3), T(2), float(1.0 / (2 * np.pi)), None, ALU.mult)
        stpi = sb("stpi", [128, 32], I32)
        cp(stpi[:], T(3))
        stt(sfrq[:], stpi[:], -1.0, T(3), ALU.mult, ALU.add)
        act(T(4), sfrq[:], AF.Sin, scale=float(2 * np.pi))
        ts(T(5), sfrq[:], 0.25, None, ALU.add)
        cp(stpi[:], T(5))
        stt(T(5), stpi[:], -1.0, T(5), ALU.mult, ALU.add)
        act(T(5), T(5), AF.Sin, scale=float(2 * np.pi))
        tt(lbre[:], sdec[:], T(5), ALU.mult)
        tt(lbim[:], sdec[:], T(4), ALU.mult)
        ts(T(6), lbre[:], -1.0, None, ALU.add)
        tt(T(0), lre, lre, ALU.mult)
        tt(T(1), lim, lim, ALU.mult)
        tt(T(0), T(0), T(1), ALU.add)
        recip(T(0), T(0))
        tt(T(1), T(6), lre, ALU.mult)
        tt(T(2), lbim[:], lim, ALU.mult)
        tt(T(1), T(1), T(2), ALU.add)
        tt(gre[:], T(1), T(0), ALU.mult)
        tt(T(1), lbim[:], lre, ALU.mult)
        tt(T(2), T(6), lim, ALU.mult)
        tt(T(1), T(1), T(2), ALU.subtract)
        tt(gim[:], T(1), T(0), ALU.mult)
        Bscr_d = nc.dram_tensor("Bscr", [128, 2, 32, 128], BF16).ap()
        BSCR = V(None, (("Bscr", None),))
        for ch in range(4):
            js = slice(8 * ch, 8 * ch + 8)

            def f32v(s0):
                ks = list(range(s0, s0 + 4))
                v = arena.multi(ks, (slice(None), slice(s0, s0 + 4), slice(None)))
                return vv(v.ap.rearrange("p s n -> p (s n)").bitcast(F32).rearrange("p (j m) -> p j m", j=8), v)
            bre_raw = f32v(0); bim_raw = f32v(4); o_re = f32v(8); o_im = f32v(12); tmp = f32v(16)
            dma("sp", bre_raw, bsre_d.rearrange("p (j m) -> p j m", j=32)[:, js, :], writes=[bre_raw])
            dma("sp", bim_raw, bsim_d.rearrange("p (j m) -> p j m", j=32)[:, js, :], writes=[bim_raw])
            g_re_b = vv(gre.h[:, js].unsqueeze(2).to_broadcast([128, 8, 128]), gre[:])
            g_im_b = vv(gim.h[:, js].unsqueeze(2).to_broadcast([128, 8, 128]), gim[:])
            tt(o_re, bre_raw, g_re_b, ALU.mult)
            tt(tmp, bim_raw, g_im_b, ALU.mult)
            tt(o_re, o_re, tmp, ALU.subtract)
            tt(o_im, bim_raw, g_re_b, ALU.mult)
            tt(tmp, bre_raw, g_im_b, ALU.mult)
            tt(o_im, o_im, tmp, ALU.add)
            stg = [vv(sq.h[:, 0:2, :].rearrange("p a n -> p (a n)").rearrange("p (j m) -> p j m", j=8), sq[:]),
                   vv(sq.h[:, 2:4, :].rearrange("p a n -> p (a n)").rearrange("p (j m) -> p j m", j=8), sq[:])]
            for jj in range(8):
                for (ri, src, pb) in ((0, o_re, ps[0]), (1, o_im, ps[1])):
                    transpose(pb[:, 0:128], vv(src.ap[:, jj, :], src), identf[:])
                    act(vv(stg[ri].ap[:, jj, :], sq[:]), pb[:, 0:128], AF.Copy)
            for ri in range(2):
                dma("sp", Bscr_d[:, ri, js, :], stg[ri], reads=[sq[:]], writes=[BSCR])

        def rmsnorm(n, gcol0):
            for grp in range(4):
                src = xT.multi(range(4 * grp, 4 * grp + 4), (slice(None), slice(4 * grp, 4 * grp + 4), slice(0, n)))
                act(sq[:, :, 0:n], src, AF.Square)
                for i in range(4):
                    mm(ps[6][:, 0:n], onesb[:], sq[:, i, 0:n], start=(grp == 0 and i == 0), stop=(grp == 3 and i == 3))
            act(rstd[:, 0:n], ps[6][:, 0:n], AF.Sqrt, scale=1.0 / D, bias=EPS)
            recip(rstd[:, 0:n], rstd[:, 0:n])
            for j in range(NKC):
                stt(hT.sub(j, (slice(None), j, slice(0, n))), xT.sub(j, (slice(None), j, slice(0, n))),
                    nrm[:, gcol0 + j:gcol0 + j + 1], rstd[:, 0:n], ALU.mult, ALU.mult)

        def ffn(n, wg_d, wu_d, wd_d):
            hall = lambda kc: hT[:, kc, 0:n]
            for s in range(NFF // 2):
                sl = wload([(0, 256, NKC, wsrc(wg_d, NKC, 256 * s, 256 * s + 256)),
                            (4096, 256, NKC, wsrc(wu_d, NKC, 256 * s, 256 * s + 256))])
                for half in range(2):
                    f = 2 * s + half
                    pg = ps[2 * (f % 2)]; pu = ps[2 * (f % 2) + 1]
                    for kc in range(NKC):
                        mm(pg[:, 0:n], wview(sl, 0, 256, kc, 128 * half, 128 * half + 128), hall(kc), kc == 0, kc == NKC - 1)
                    for kc in range(NKC):
                        mm(pu[:, 0:n], wview(sl, 4096, 256, kc, 128 * half, 128 * half + 128), hall(kc), kc == 0, kc == NKC - 1)
                    tmp = tA if f % 2 == 0 else tB
                    act(tmp[:, 0:n], pg[:, 0:n], AF.Silu)
                    tt(slot(f, n), tmp[:, 0:n], pu[:, 0:n], ALU.mult)
            for j in range(NKC):
                sl = wload([(0, 128, NFF, wsrc(wd_d, NFF, 128 * j, 128 * j + 128))])
                pd = ps[4 + (j % 2)]
                for kc in range(NFF):
                    mm(pd[:, 0:n], wview(sl, 0, 128, kc, 0, 128), slot(kc, n), kc == 0, kc == NFF - 1)
                xj = xT.sub(j, (slice(None), j, slice(0, n)))
                stt(xj, pd[:, 0:n], 0.5, xj, ALU.mult, ALU.add)

        def proj_pair(sl, t0, n, pa, pb):
            for (tix, pp) in ((t0, pa), (t0 + 1, pb)):
                for kc in range(NKC):
                    mm(pp[:, 0:n], wview(sl, 0, 512, kc, 128 * tix, 128 * tix + 128), hT[:, kc, 0:n], kc == 0, kc == NKC - 1)

        def mix_slab(i):
            return wload([(0, 512, NKC, wsrc(wmix_d, NKC, 512 * i, 512 * i + 512))])

        def mix(blk, n, c0):
            sample = blk == NBLK
            rmsnorm(n, 16)
            cosK = tE[:, 0:n]; sinK = tF[:, 0:n]; cosI = tE[:, 0:n]; sinI = tF[:, 0:n]
            dma("sp", tE[:, 0:n], rope_d[0, :, c0:c0 + n], writes=[tE[:]])
            dma("sp", tF[:, 0:n], rope_d[1, :, c0:c0 + n], writes=[tF[:]])
            for i in range(2):
                sl = mix_slab(i)
                for t in range(4):
                    pp = ps[t % 4]
                    for kc in range(NKC):
                        mm(pp[:, 0:n], wview(sl, 0, 512, kc, 128 * t, 128 * t + 128), hT[:, kc, 0:n], kc == 0, kc == NKC - 1)
                    act(slot(A_U + 4 * i + t, n), pp[:, 0:n], AF.Copy)

            def normrope(pa, pb, gcol, out_bf, out_f32=None):
                act(sq[:, 0, 0:n], pa[:, 0:n], AF.Square)
                mm(ps[6][:, 0:n], onesb[:], sq[:, 0, 0:n], True, True)
                act(rstd[:, 0:n], ps[6][:, 0:n], AF.Sqrt, scale=1.0 / 128, bias=EPS)
                recip(rstd[:, 0:n], rstd[:, 0:n])
                stt(tC[:, 0:n], pa[:, 0:n], qkn[:, gcol:gcol + 1], rstd[:, 0:n], ALU.mult, ALU.mult)
                tt(tC[:, 0:n], tC[:, 0:n], cosK, ALU.mult)
                stt(tD[:, 0:n], pb[:, 0:n], qkn[:, gcol + 1:gcol + 2], rstd[:, 0:n], ALU.mult, ALU.mult)
                tt(tD[:, 0:n], tD[:, 0:n], sinK, ALU.mult)
                if out_f32 is not None:
                    tt(out_f32, tC[:, 0:n], tD[:, 0:n], ALU.add)
                    act(out_bf, out_f32, AF.Copy)
                else:
                    tt(out_bf, tC[:, 0:n], tD[:, 0:n], ALU.add)

            if KSUB < 2:
                return False
            for i in range(4):
                sl = mix_slab(2 + i)
                for hh in range(2):
                    h = 2 * i + hh
                    pa, pb = ps[2 * hh], ps[2 * hh + 1]
                    proj_pair(sl, 2 * hh, n, pa, pb)
                    normrope(pa, pb, 0, slot(A_Q + h, n))
            if KSUB < 3:
                return False
            sl = mix_slab(6)
            for g in range(2):
                pa, pb = ps[2 * g], ps[2 * g + 1]
                proj_pair(sl, 2 * g, n, pa, pb)
                if not sample:
                    kdst = kTs.sub(blk, (slice(None), g, slice(c0, c0 + n)))
                else:
                    kdst = knew.sub(g, (slice(None), g, slice(0, n)))
                normrope(pa, pb, 2, kdst, out_f32=kst[:, 0:n])
                dma("sp", kT_o[128 * g:128 * g + 128, c0:c0 + n], kst[:, 0:n], reads=[kst[:]])
            if KSUB < 4:
                return False
            dma("sp", tE[:, 0:n], rope_d[2, :, c0:c0 + n], writes=[tE[:]])
            dma("sp", tF[:, 0:n], rope_d[3, :, c0:c0 + n], writes=[tF[:]])
            for i in range(4):
                sl = mix_slab(7 + i)
                for hh in range(2):
                    t = 2 * i + hh
                    pa, pb = ps[2 * hh], ps[2 * hh + 1]
                    if not sample:
                        proj_pair(sl, 2 * hh, n, pa, pb)
                        tt(tC[:, 0:n], pa[:, 0:n], cosI, ALU.mult)
                        tt(tD[:, 0:n], pb[:, 0:n], sinI, ALU.mult)
                        tt(slot(A_QI + t, n), tC[:, 0:n], tD[:, 0:n], ALU.add)
                    else:
                        for a in range(2):
                            for (tix, pp) in ((2 * hh, pa), (2 * hh + 1, pb)):
                                for kc in range(NKC):
                                    mm(pp[0:64, 0:n], wview(sl, 0, 512, kc, 128 * tix + 64 * a, 128 * tix + 64 * a + 64),
                                       hT[:, kc, 0:n], kc == 0, kc == NKC - 1)
                            tt(tC[0:64, 0:n], pa[0:64, 0:n], tE[0:64, 0:n], ALU.mult)
                            tt(tD[0:64, 0:n], pb[0:64, 0:n], tF[0:64, 0:n], ALU.mult)
                            tt(sQI[0:64, 2 * t + a, 0:n], tC[0:64, 0:n], tD[0:64, 0:n], ALU.add)
            if KSUB < 5:
                return False
            sl = wload([(0, 256, NKC, wsrc(wmix_d, NKC, 44 * 128, 46 * 128))])
            pa, pb = ps[0], ps[1]
            for (tix, pp) in ((0, pa), (1, pb)):
                for kc in range(NKC):
                    mm(pp[:, 0:n], wview(sl, 0, 256, kc, 128 * tix, 128 * tix + 128), hT[:, kc, 0:n], kc == 0, kc == NKC - 1)
            tt(tC[:, 0:n], pa[:, 0:n], cosI, ALU.mult)
            tt(tD[:, 0:n], pb[:, 0:n], sinI, ALU.mult)
            tt(kst[:, 0:n], tC[:, 0:n], tD[:, 0:n], ALU.add)
            dma("sp", kiT_o[:, c0:c0 + n], kst[:, 0:n], reads=[kst[:]])
            if not sample:
                ts(kiA.sub(blk, (slice(None), slice(c0, c0 + n))), kst[:, 0:n], hmask[:, 0:1], None, ALU.mult)
                ts(kiB.sub(blk, (slice(None), slice(c0, c0 + n))), kst[:, 0:n], hmask[:, 1:2], None, ALU.mult)
            else:
                ts(kinA[:, 0:n], kst[:, 0:n], hmask[:, 0:1], None, ALU.mult)
                ts(kinB[:, 0:n], kst[:, 0:n], hmask[:, 1:2], None, ALU.mult)
            if KSUB < 6:
                return False
            sl = wload([(0, 512, NKC, wsrc(wtok_d, NKC, 0, 512))])
            ntt = (n + 127) // 128
            for g in range(2):
                pp = ps[g]
                for kc in range(NKC):
                    mm(pp[:, 0:n], wview(sl, 0, 512, kc, 128 * g, 128 * g + 128), hT[:, kc, 0:n], kc == 0, kc == NKC - 1)
                cp(kst[:, 0:n], pp[:, 0:n])
                dma("sp", vT_o[128 * g:128 * g + 128, c0:c0 + n], kst[:, 0:n], reads=[kst[:]])
                if KATT and (KV & 1):
                    act(hre[:, 0:n], pp[:, 0:n], AF.Copy)
                    for tti in range(ntt):
                        m = max(32, min(128, n - 128 * tti))
                        transpose(vv(ps6b[0:m, 128 * tti:128 * tti + 128], ps[7][:]), hre[:, 128 * tti:128 * tti + m], identb[:])
                    if not sample:
                        for tti in range(ntt):
                            gt = blk * 4 + tti
                            act(vS.sub(gt, (slice(None), gt, slice(128 * g, 128 * g + 128))),
                                vv(ps6b[:, 128 * tti:128 * tti + 128], ps[7][:]), AF.Copy)
                    else:
                        act(vnew[0:n, 128 * g:128 * g + 128], vv(ps6b[0:n, 0:128], ps[7][:]), AF.Copy)
            if KATT and (KV & 2):
                pw = ps[2]
                if n < 32:
                    memset(tC[0:32, 0:32], 0.0)
                for kc in range(NKC):
                    mm(pw[0:32, 0:n], wview(sl, 0, 512, kc, 256, 288), hT[:, kc, 0:n], kc == 0, kc == NKC - 1)
                ts(tC[0:32, 0:n], pw[0:32, 0:n], 0.25, None, ALU.mult)
                for tti in range(ntt):
                    m = max(32, min(128, n - 128 * tti))
                    transpose(ps[3][0:m, 32 * tti:32 * tti + 32], tC[0:32, 128 * tti:128 * tti + m], identf[0:32, 0:32])
                for tti in range(ntt):
                    m = min(128, n - 128 * tti)
                    cp(wtk.sub(tti, (slice(0, m), tti, slice(None))), ps[3][0:m, 32 * tti:32 * tti + 16])
            if stage_limit < 3:
                return False
            if not sample:
                ssm_prompt(n)
            else:
                ssm_sample()
            if stage_limit < 4:
                return False
            glu(n)
            if stage_limit < 5:
                return False
            if not sample and KATT:
                for qt in range(4):
                    attn_prompt(blk, qt)
            else:
                attn_sample()
            if stage_limit < 6:
                return False
            merge_out(n)
            return True

        def glu(n):
            sl = wload([(0, 1024, 8, wsrc(glw_d, 8, 0, 1024))])
            for j in range(8):
                pp = ps[j % 4]
                for kc in range(8):
                    mm(pp[:, 0:n], wview(sl, 0, 1024, kc, 128 * j, 128 * j + 128), slot(A_ZA + kc, n), kc == 0, kc == 7)
                tmp = tA if j % 2 == 0 else tB
                act(tmp[:, 0:n], pp[:, 0:n], AF.Sigmoid, bias=glb[:, j:j + 1])
                tt(slot(A_U + j, n), tmp[:, 0:n], slot(A_ZA + j, n), ALU.mult)

        def merge_out(n):
            for j in range(16):
                sl = wload([(0, 128, 8, wsrc(wa_d, 8, 128 * j, 128 * j + 128)),
                            (1024, 128, 8, wsrc(wb_d, 8, 128 * j, 128 * j + 128)),
                            (2048, 128, 16, wsrc(wgab_d, 16, 128 * j, 128 * j + 128)),
                            (4096, 128, 16, wsrc(wgab_d, 16, 2048 + 128 * j, 2048 + 128 * j + 128))])
                b0 = 0
                pA, pB, pga, pgb = ps[b0], ps[b0 + 1], ps[b0 + 2], ps[b0 + 3]
                for kc in range(8):
                    mm(pA[:, 0:n], wview(sl, 0, 128, kc, 0, 128), slot(A_U + kc, n), kc == 0, kc == 7)
                for kc in range(8):
                    mm(pB[:, 0:n], wview(sl, 1024, 128, kc, 0, 128), slot(A_BO + kc, n), kc == 0, kc == 7)
                for kc in range(16):
                    mm(pga[:, 0:n], wview(sl, 2048, 128, kc, 0, 128), hT[:, kc, 0:n], kc == 0, kc == 15)
                for kc in range(16):
                    mm(pgb[:, 0:n], wview(sl, 4096, 128, kc, 0, 128), hT[:, kc, 0:n], kc == 0, kc == 15)
                act(tA[:, 0:n], pga[:, 0:n], AF.Sigmoid)
                act(tB[:, 0:n], pgb[:, 0:n], AF.Sigmoid)
                tt(tC[:, 0:n], tA[:, 0:n], pA[:, 0:n], ALU.mult)
                tt(tD[:, 0:n], tB[:, 0:n], pB[:, 0:n], ALU.mult)
                tt(slot(A_Q + j, n), tC[:, 0:n], tD[:, 0:n], ALU.add)
            for i in range(4):
                sl = wload([(0, 512, NKC, wsrc(wo_d, NKC, 512 * i, 512 * i + 512))])
                for t in range(4):
                    j = 4 * i + t
                    pp = ps[j % 4]
                    for kc in range(NKC):
                        mm(pp[:, 0:n], wview(sl, 0, 512, kc, 128 * t, 128 * t + 128), slot(A_Q + kc, n), kc == 0, kc == NKC - 1)
                    xj = xT.sub(j, (slice(None), j, slice(0, n)))
                    tt(xj, pp[:, 0:n], xj, ALU.add)

        NIT = 22

        def bisect(L, lo, W, mid, cnt, gsel):
            pr = lo.ap.shape[0]
            for it in range(NIT):
                sc = float(2.0 ** -(it + 1))
                stt(mid, W, sc, lo, ALU.mult, ALU.add)
                ts(vv(mskAP[0:pr, 0:L], MSK_KEYS), vv(IscAP[0:pr, 0:L], ISC_KEYS), mid, None, ALU.is_ge, ALU.add, accum=cnt)
                ts(gsel, cnt, TOPK - 0.5, sc, ALU.is_ge, ALU.mult)
                stt(lo, gsel, W, lo, ALU.mult, ALU.add)

        def attn_prompt(blk, qt):
            G = 4 * blk + qt
            L = 128 * (G + 1)
            nkb = G + 1
            q0 = 128 * qt
            for h in range(16):
                ts(Dm.sub(h, (slice(None), h, slice(None))), identf[:], wtk[:, qt, h:h + 1], None, ALU.mult)
            for ch in range((L + 511) // 512):
                w = min(512, L - 512 * ch)
                pI = ps[2]
                for h in range(16):
                    psc = ps[h % 2]
                    kx = kiA if h % 2 == 0 else kiB
                    mm(psc[:, 0:w], slotc(A_QI + h // 2, q0, q0 + 128), kx[:, 512 * ch:512 * ch + w], True, True)
                    act(rl[h % 2][:, 0:w], psc[:, 0:w], AF.Relu, scale=0.125)
                    mm(pI[:, 0:w], Dm.sub(h, (slice(None), h, slice(None))), rl[h % 2][:, 0:w], h == 0, h == 15)
                act(Isc(512 * ch, 512 * ch + w), pI[:, 0:w], AF.Copy)
            hi, lo, W, mid, cnt, gsel = (bs[:, i:i + 1] for i in range(6))
            S.add("dve", lambda e: e.tensor_reduce(out=bs.h[:, 0:1], in_=IscAP[:, 0:L], axis=AX.X, op=ALU.max),
                  reads=[Isc(0, L)], writes=[hi])
            S.add("dve", lambda e: e.tensor_reduce(out=bs.h[:, 1:2], in_=IscAP[:, 0:L], axis=AX.X, op=ALU.min),
                  reads=[Isc(0, L)], writes=[lo])
            tt(Isc(128 * G, 128 * G + 128), Isc(128 * G, 128 * G + 128), tri[:], ALU.add)
            ts(lo, lo, -1.0, None, ALU.add)
            tt(W, hi, lo, ALU.subtract)
            ts(W, W, 1.0, None, ALU.add)
            bisect(L, lo, W, mid, cnt, gsel)
            ts(mskv(0, L), Isc(0, L), lo, None, ALU.is_ge)
            for k8 in range(0, nkb, 8):
                c8 = min(8, nkb - k8)
                for kb in range(k8, k8 + c8):
                    transpose(vv(ps6b[:, (kb - k8) * 128:(kb - k8) * 128 + 128], ps[7][:]), mskv(128 * kb, 128 * kb + 128), identb[:])
                act(vv(mskT.h[:, k8:k8 + c8, :].rearrange("p a b -> p (a b)"), mskT[:]),
                    vv(ps6b[:, 0:c8 * 128], ps[7][:]), AF.Copy)
            for hh in range(8):
                g = hh // 4
                pO, pD = ps[6], ps[3]
                for c4 in range(0, nkb, 4):
                    c = min(4, nkb - c4)
                    pS = ps[4 + ((c4 // 4) % 2)]
                    pex = pe_[(c4 // 4) % 2]
                    for kb in range(c4, c4 + c):
                        mm(pS[:, (kb - c4) * 128:(kb - c4) * 128 + 128], kTs[:, g, 128 * kb:128 * kb + 128],
                           slotc(A_Q + hh, q0, q0 + 128), True, True)
                    act(pex[:, 0:c * 128], pS[:, 0:c * 128], AF.Exp, scale=float(128 ** -0.5))
                    tt(pex[:, 0:c * 128], pex[:, 0:c * 128],
                       vv(mskT.h[:, c4:c4 + c, :].rearrange("p a b -> p (a b)"), mskT[:]), ALU.mult)
                    for kb in range(c4, c4 + c):
                        mm(pO[:, 0:128], vS[:, kb, 128 * g:128 * g + 128], pex[:, (kb - c4) * 128:(kb - c4) * 128 + 128],
                           kb == 0, kb == nkb - 1)
                        mm(pD[:, 0:128], onesb[:], pex[:, (kb - c4) * 128:(kb - c4) * 128 + 128], kb == 0, kb == nkb - 1)
                recip(tA[:, 0:128], pD[:, 0:128])
                tt(slotc(A_BO + hh, q0, q0 + 128), pO[:, 0:128], tA[:, 0:128], ALU.mult)

        def attn_sample():
            LS = PAST + NS
            arf = arena.h[:].rearrange("p s n -> p (s n)").bitcast(F32)
            ARK = arena[:]

            def IS(a, b_):
                return vv(arf[0:16, a:b_], ARK)
            junk = vv(kTs.h[:].rearrange("p a n -> p (a n)")[0:16, 0:2052], kTs[:])
            kic = vv(rl[0].h[:, :].rearrange("p (t d) -> p t d", t=8), rl[0][:])
            kiTc = vv(mskT.h[:].rearrange("p a b -> p (a b)")[0:64, 0:1024], mskT[:])
            Kc = vv(kiA.h[:, :].rearrange("p (t d) -> p t d", t=8), kiA[:])
            Vc = vv(kiB.h[:, :].rearrange("p (t d) -> p t d", t=8), kiB[:])
            KTc = vv(Dm.h[:].rearrange("p a b -> p (a b)").rearrange("p (g k) -> p g k", g=2), Dm[:])
            mTc = vv(pe_[0].h[:, 0:128].rearrange("p (t q) -> p t q", t=8), pe_[0][:])
            mskc = vv(vS.h[:].rearrange("p a b -> p (a b)")[0:16, 0:1024], vS[:])
            pexv = vv(pe_[1].h[:, 0:256].rearrange("p (g t c) -> p g t c", g=2, t=8), pe_[1][:])
            qs = vv(hre.h[:, 0:32].rearrange("p (h q) -> p h q", h=8), hre[:])
            for h in range(16):
                ts(DmS[:, h, :], identf[0:16, 0:16], wtk[0:16, 0, h:h + 1], None, ALU.mult)
            hi, lo, W, mid, cnt, gsel, c2 = (bs2[:, i:i + 1] for i in range(7))
            for bi in range(4):
                for ch in range(8):
                    S.add("pool", lambda e, ch=ch, bi=bi: e.indirect_dma_start(
                        out=rl[0].h[:, :], out_offset=None, in_=cik_d[:, :],
                        in_offset=bass.IndirectOffsetOnAxis(ap=idx8.h[:, bi, ch:ch + 1], axis=0)),
                        reads=[idx8[:]], writes=[rl[0][:]], dma=True)
                    for t in range(8):
                        transpose(vv(ps6b[0:64, 128 * t:128 * t + 128], ps[7][:]), vv(kic.ap[:, t, :], kic), identb[:])
                    act(kiTc, vv(ps6b[0:64, 0:1024], ps[7][:]), AF.Copy)
                    for sc in range(2):
                        pI = ps[2]
                        for h in range(16):
                            psc = ps[h % 2]
                            mm(psc[0:16, 0:512], sQI[0:64, h, 0:16], vv(kiTc.ap[:, 512 * sc:512 * sc + 512], kiTc), True, True)
                            act(rl[1][0:16, 0:512], psc[0:16, 0:512], AF.Relu, scale=0.125)
                            mm(pI[0:16, 0:512], DmS[:, h, :], rl[1][0:16, 0:512], h == 0, h == 15)
                        c0_ = 1024 * ch + 512 * sc
                        act(IS(c0_, c0_ + 512), pI[0:16, 0:512], AF.Copy)
                pI = ps[2]
                for h in range(16):
                    psc = ps[h % 2]
                    mm(psc[0:16, 0:16], sQI[0:64, h, 0:16], kinA[0:64, 0:16], True, True)
                    act(rl[1][0:16, 0:16], psc[0:16, 0:16], AF.Relu, scale=0.125)
                    mm(pI[0:16, 0:16], DmS[:, h, :], rl[1][0:16, 0:16], h == 0, h == 15)
                S.add("dve", lambda e: e.tensor_reduce(out=bs2.h[:, 0:1], in_=arf[0:16, 0:PAST], axis=AX.X, op=ALU.max),
                      reads=[IS(0, PAST)], writes=[hi])
                S.add("dve", lambda e: e.tensor_reduce(out=bs2.h[:, 1:2], in_=arf[0:16, 0:PAST], axis=AX.X, op=ALU.min),
                      reads=[IS(0, PAST)], writes=[lo])
                tt(IS(PAST, LS), pI[0:16, 0:16], mnew[:], ALU.add)
                ts(lo, lo, -64.0, None, ALU.add)
                ts(hi, hi, 64.0, None, ALU.add)
                tt(W, hi, lo, ALU.subtract)
                for it in range(NIT + 4):
                    scv = float(2.0 ** -(it + 1))
                    stt(mid, W, scv, lo, ALU.mult, ALU.add)
                    for q4 in range(4):
                        ts(junk, IS(2052 * q4, 2052 * q4 + 2052), mid, None, ALU.is_ge, ALU.add, accum=(cnt if q4 == 0 else c2))
                        if q4 > 0:
                            tt(cnt, cnt, c2, ALU.add)
                    ts(gsel, cnt, TOPK - 0.5, scv, ALU.is_ge, ALU.mult)
                    stt(lo, gsel, W, lo, ALU.mult, ALU.add)
                for hh in range(8):
                    cp(vv(qs.ap[:, hh, :], qs), slotc_s(A_Q + hh, 4 * bi, 4 * bi + 4))
                pO = [ps[2], ps[3]]
                pD = [ps[4], ps[5]]
                nblk_tot = 65
                for ch in range(9):
                    if ch < 8:
                        for (dst, src_d) in ((kiA, ck_d), (kiB, cv_d)):
                            S.add("pool", lambda e, ch=ch, bi=bi, dst=dst, src_d=src_d: e.indirect_dma_start(
                                out=dst.h[:, :], out_offset=None, in_=src_d[:, :],
                                in_offset=bass.IndirectOffsetOnAxis(ap=idx8.h[:, bi, ch:ch + 1], axis=0)),
                                reads=[idx8[:]], writes=[dst[:]], dma=True)
                        ts(mskc, IS(1024 * ch, 1024 * ch + 1024), lo, None, ALU.is_ge)
                        for t in range(8):
                            transpose(vv(ps6b[:, 16 * t:16 * t + 16], ps[7][:]), vv(mskc.ap[:, 128 * t:128 * t + 128], mskc), identb[0:16, 0:16])
                        act(vv(pe_[0].h[:, 0:128], pe_[0][:]), vv(ps6b[:, 0:128], ps[7][:]), AF.Copy)
                        for g in range(2):
                            for t in range(8):
                                transpose(vv(ps6b[:, 128 * t:128 * t + 128], ps[7][:]), vv(Kc.ap[:, t, 128 * g:128 * g + 128], Kc), identb[:])
                            act(vv(KTc.ap[:, g, :], KTc), vv(ps6b[:, 0:1024], ps[7][:]), AF.Copy)
                        pS = ps[0]
                        for g in range(2):
                            for t in range(8):
                                mm(pS[:, 128 * g + 16 * t:128 * g + 16 * t + 16], vv(KTc.ap[:, g, 128 * t:128 * t + 128], KTc),
                                   vv(qs.ap[:, 4 * g:4 * g + 4, :].rearrange("p h q -> p (h q)"), qs), True, True)
                        act(vv(pe_[1].h[:, 0:256], pe_[1][:]), pS[:, 0:256], AF.Exp, scale=float(128 ** -0.5))
                        for g in range(2):
                            pg_ = vv(pexv.ap[:, g, :, :].rearrange("p t (h q) -> p t h q", h=4), pexv)
                            mb_ = vv(mTc.ap[:, :, 4 * bi:4 * bi + 4].unsqueeze(2).to_broadcast([128, 8, 4, 4]), mTc)
                            tt(pg_, pg_, mb_, ALU.mult)
                        for g in range(2):
                            for t in range(8):
                                kb = 8 * ch + t
                                rhs_ = vv(pexv.ap[:, g, t, :], pexv)
                                mm(pO[g][:, 0:16], vv(Vc.ap[:, t, 128 * g:128 * g + 128], Vc), rhs_, kb == 0, False)
                                mm(pD[g][:, 0:16], onesb[:], rhs_, kb == 0, False)
                    else:
                        ts(vv(mskc.ap[:, 0:16], mskc), IS(PAST, LS), lo, None, ALU.is_ge)
                        transpose(vv(ps6b[0:16, 0:16], ps[7][:]), vv(mskc.ap[:, 0:16], mskc), identb[0:16, 0:16])
                        act(vv(pe_[0].h[0:16, 0:16], pe_[0][:]), vv(ps6b[0:16, 0:16], ps[7][:]), AF.Copy)
                        pS = ps[0]
                        for g in range(2):
                            mm(pS[0:16, 16 * g:16 * g + 16], knew[:, g, 0:16],
                               vv(qs.ap[:, 4 * g:4 * g + 4, :].rearrange("p h q -> p (h q)"), qs), True, True)
                        act(vv(pe_[1].h[0:16, 0:32], pe_[1][:]), pS[0:16, 0:32], AF.Exp, scale=float(128 ** -0.5))
                        for g in range(2):
                            pg_ = vv(pe_[1].h[0:16, 16 * g:16 * g + 16].rearrange("p (h q) -> p h q", h=4), pe_[1][:])
                            mb_ = vv(pe_[0].h[0:16, 4 * bi:4 * bi + 4].unsqueeze(1).to_broadcast([16, 4, 4]), pe_[0][:])
                            tt(pg_, pg_, mb_, ALU.mult)
                        for g in range(2):
                            rhs_ = vv(pe_[1].h[0:16, 16 * g:16 * g + 16], pe_[1][:])
                            mm(pO[g][:, 0:16], vnew[0:16, 128 * g:128 * g + 128], rhs_, False, True)
                            mm(pD[g][:, 0:16], onesb[0:16, :], rhs_, False, True)
                for g in range(2):
                    recip(tA[:, 0:16], pD[g][:, 0:16])
                    tt(tB[:, 0:16], pO[g][:, 0:16], tA[:, 0:16], ALU.mult)
                    outv = sA.multi(range(A_BO + 4 * g, A_BO + 4 * g + 4),
                                    (slice(None), slice(A_BO + 4 * g, A_BO + 4 * g + 4), slice(4 * bi, 4 * bi + 4)))
                    cp(outv, vv(tB.h[:, 0:16].rearrange("p (h q) -> p h q", h=4), tB[:]))

        def slotc_s(i, a, b_):
            return sA.sub(i, (slice(None), i, slice(a, b_)))

        def ssm_prompt(n):
            slB = wload([(0, 128, 32, Bscr_d[:, 0, :, :]), (4096, 128, 32, Bscr_d[:, 1, :, :])], extra_reads=[BSCR])
            slC = wload([(0, 128, 32, cre_d.rearrange("p (j m) -> p j m", j=32)),
                         (4096, 128, 32, cim_d.rearrange("p (j m) -> p j m", j=32))])
            for c in range(8):
                py = ps[4 + (c % 2)]
                for jj in range(4):
                    j = 4 * c + jj
                    pre, pim = ps[2 * (j % 2)], ps[2 * (j % 2) + 1]
                    ub = slot(A_U + c, n)
                    mm(pre[:, 0:n], wview(slB, 0, 128, j, 0, 128), ub, True, True)
                    mm(pim[:, 0:n], wview(slB, 4096, 128, j, 0, 128), ub, True, True)
                    fj = sfrq[:, j:j + 1]
                    ts(tA[:, 0:n], iota1[:, 0:n], fj, None, ALU.mult)
                    cp(tI[:, 0:n], tA[:, 0:n])
                    stt(tB[:, 0:n], tI[:, 0:n], -1.0, tA[:, 0:n], ALU.mult, ALU.add)
                    act(sinT[:, 0:n], tB[:, 0:n], AF.Sin, scale=float(2 * np.pi))
                    ts(tA[:, 0:n], tA[:, 0:n], 0.25, None, ALU.add)
                    cp(tI[:, 0:n], tA[:, 0:n])
                    stt(tB[:, 0:n], tI[:, 0:n], -1.0, tA[:, 0:n], ALU.mult, ALU.add)
                    act(cosT[:, 0:n], tB[:, 0:n], AF.Sin, scale=float(2 * np.pi))
                    tt(tA[:, 0:n], cosT[:, 0:n], pre[:, 0:n], ALU.mult)
                    tt(tB[:, 0:n], sinT[:, 0:n], pim[:, 0:n], ALU.mult)
                    tt(tC[:, 0:n], tA[:, 0:n], tB[:, 0:n], ALU.add)
                    tt(tA[:, 0:n], cosT[:, 0:n], pim[:, 0:n], ALU.mult)
                    tt(tB[:, 0:n], sinT[:, 0:n], pre[:, 0:n], ALU.mult)
                    tt(tD[:, 0:n], tA[:, 0:n], tB[:, 0:n], ALU.subtract)
                    dec = vv(sdec.h[:, j:j + 1].to_broadcast([128, n]), sdec[:])
                    for (cc, qq, ci) in ((tC, tE, 0), (tD, tF, 1)):
                        init = car[:, ci, j:j + 1]
                        S.add("dve", lambda e, cc=cc, qq=qq, init=init, dec=dec: e.tensor_tensor_scan(
                            out=qq.h[:, 0:n], data0=dec.ap, data1=cc.h[:, 0:n], initial=init.ap,
                            op0=ALU.mult, op1=ALU.add), reads=[cc[:], init, dec], writes=[qq[:]])
                    tt(tA[:, 0:n], cosT[:, 0:n], tE[:, 0:n], ALU.mult)
                    tt(tB[:, 0:n], sinT[:, 0:n], tF[:, 0:n], ALU.mult)
                    tt(tC[:, 0:n], tA[:, 0:n], tB[:, 0:n], ALU.subtract)
                    tt(tA[:, 0:n], sinT[:, 0:n], tE[:, 0:n], ALU.mult)
                    tt(tB[:, 0:n], cosT[:, 0:n], tF[:, 0:n], ALU.mult)
                    tt(tD[:, 0:n], tA[:, 0:n], tB[:, 0:n], ALU.add)
                    act(hre[:, 0:n], tC[:, 0:n], AF.Copy)
                    act(him[:, 0:n], tD[:, 0:n], AF.Copy, scale=-1.0)
                    cp(car[:, 0, j:j + 1], tC[:, n - 1:n])
                    cp(car[:, 1, j:j + 1], tD[:, n - 1:n])
                    mm(py[:, 0:n], wview(slC, 0, 128, j, 0, 128), hre[:, 0:n], jj == 0, False)
                    mm(py[:, 0:n], wview(slC, 4096, 128, j, 0, 128), him[:, 0:n], False, False)
                mm(py[:, 0:n], Dd.sub(c, (slice(None), c, slice(None))), slot(A_U + c, n), False, True)
                gelu_to(py, slot(A_ZA + c, n), n)

        def gelu_to(py, out, n):
            act(tA[:, 0:n], py[:, 0:n], AF.Square)
            ts(tA[:, 0:n], tA[:, 0:n], 0.044715, 1.0, ALU.mult, ALU.add)
            tt(tA[:, 0:n], tA[:, 0:n], py[:, 0:n], ALU.mult)
            act(tB[:, 0:n], tA[:, 0:n], AF.Sigmoid, scale=1.5957691216057308)
            tt(out, tB[:, 0:n], py[:, 0:n], ALU.mult)

        def ssm_sample():
            n = NS
            slB = wload([(0, 128, 32, Bscr_d[:, 0, :, :]), (4096, 128, 32, Bscr_d[:, 1, :, :])], extra_reads=[BSCR])
            slC = wload([(0, 128, 32, cre_d.rearrange("p (j m) -> p j m", j=32)),
                         (4096, 128, 32, cim_d.rearrange("p (j m) -> p j m", j=32))])
            for j in range(32):
                c = j // 4
                pre, pim = ps[2 * (j % 2)], ps[2 * (j % 2) + 1]
                mm(pre[:, 0:n], wview(slB, 0, 128, j, 0, 128), slot(A_U + c, n), True, True)
                mm(pim[:, 0:n], wview(slB, 4096, 128, j, 0, 128), slot(A_U + c, n), True, True)
                cp(tE[:, 16 * j:16 * j + 16], pre[:, 0:n])
                act(tF[:, 16 * j:16 * j + 16], pim[:, 0:n], AF.Copy)
            cp(hss[:], h0s[:])

            def v4(tX, t):
                return vv(tX.h[:].rearrange("p (j b t) -> p j b t", j=32, b=4)[:, :, :, t], tX[:])

            def v3(tX, a):
                return vv(tX.h[:, a:a + 128].rearrange("p (j b) -> p j b", j=32), tX[:])
            lr = vv(lbre.h[:, :].unsqueeze(2).to_broadcast([128, 32, 4]), lbre[:])
            li = vv(lbim.h[:, :].unsqueeze(2).to_broadcast([128, 32, 4]), lbim[:])
            A1, A2, A3, A4 = v3(tA, 0), v3(tA, 128), v3(tB, 0), v3(tB, 128)
            for t in range(4):
                hr = hss[:, 0, :, :]
                hi_ = hss[:, 1, :, :]
                tt(A1, hr, lr, ALU.mult)
                tt(A2, hi_, li, ALU.mult)
                tt(A1, A1, A2, ALU.subtract)
                tt(A3, hi_, lr, ALU.mult)
                tt(A4, hr, li, ALU.mult)
                tt(A3, A3, A4, ALU.add)
                tt(hr, A1, v4(tE, t), ALU.add)
                tt(hi_, A3, v4(tF, t), ALU.add)
                cp(v4(tC, t), hr)
                cp(v4(tD, t), hi_)
            act(hre[:], tC[:], AF.Copy)
            act(him[:], tD[:], AF.Copy, scale=-1.0)
            for c in range(8):
                py = ps[4 + (c % 2)]
                for jj in range(4):
                    j = 4 * c + jj
                    mm(py[:, 0:n], wview(slC, 0, 128, j, 0, 128), hre[:, 16 * j:16 * j + 16], jj == 0, False)
                    mm(py[:, 0:n], wview(slC, 4096, 128, j, 0, 128), him[:, 16 * j:16 * j + 16], False, False)
                mm(py[:, 0:n], Dd.sub(c, (slice(None), c, slice(None))), slot(A_U + c, n), False, True)
                gelu_to(py, slot(A_ZA + c, n), n)

        knew = sb("knew", [128, 2, NS], BF16)
        kinA = sb("kinA", [128, NS], BF16); kinB = sb("kinB", [128, NS], BF16)
        vnew = sb("vnew", [NS, 256], BF16)

        for blk in range(NBLK + 1):
            n = NB if blk < NBLK else NS
            cur["sample"] = (blk == NBLK)
            c0 = blk * NB
            for j in range(NKC):
                dma("sp", xT.sub(j, (slice(None), j, slice(0, n))), xT_d[128 * j:128 * j + 128, c0:c0 + n],
                    writes=[xT.sub(j, (slice(None), j, slice(0, n)))])
            if stage_limit >= 1:
                rmsnorm(n, 0)
                ffn(n, w1g_d, w1u_d, w1d_d)
            if stage_limit >= 2:
                full = mix(blk, n, c0)
                if full and stage_limit >= 7:
                    rmsnorm(n, 32)
                    ffn(n, w2g_d, w2u_d, w2d_d)
            for j in range(NKC):
                dma("sp", yT_o[128 * j:128 * j + 128, c0:c0 + n], xT.sub(j, (slice(None), j, slice(0, n))),
                    reads=[xT.sub(j, (slice(None), j, slice(0, n)))])
        dma("sp", sp_o, vv(car.h[:].rearrange("p a j -> p (a j)"), car[:]), reads=[car[:]])
        dma("sp", ss_o, vv(hss.h[:].rearrange("p a j b -> p (a j b)"), hss[:]), reads=[hss[:]])

        S.emit_all(nc)
    return nc


def _rope_tables(pos):
    pos = np.asarray(pos, np.float32)
    T = pos.shape[0]
    out = np.zeros((4, 128, T), np.float32)
    out[0] = 1.0
    out[2] = 1.0
    half = 16
    fr = (np.float32(500000.0) ** (-np.arange(half, dtype=np.float32) * np.float32(2.0) / np.float32(32))).astype(np.float32)
    ang = (pos[:, None] * fr[None, :]).astype(np.float32)
    c, s = np.cos(ang).T.astype(np.float32), np.sin(ang).T.astype(np.float32)
    out[0, 0:16] = c; out[0, 16:32] = c
    out[1, 0:16] = -s; out[1, 16:32] = s
    half = 8
    fr = (np.float32(500000.0) ** (-np.arange(half, dtype=np.float32) * np.float32(2.0) / np.float32(16))).astype(np.float32)
    ang = (pos[:, None] * fr[None, :]).astype(np.float32)
    c, s = np.cos(ang).T.astype(np.float32), np.sin(ang).T.astype(np.float32)
    for b in (0, 64):
        out[2, b:b + 8] = c; out[2, b + 8:b + 16] = c
        out[3, b:b + 8] = -s; out[3, b + 8:b + 16] = s
    return out


def _perm_cols(w, width, half):
    n = w.shape[1]
    idx = np.arange(n)
    loc = idx % width
    src = idx.copy()
    src[loc < half] += half
    m = (loc >= half) & (loc < 2 * half)
    src[m] -= half
    return w[:, src]


_NC_CACHE = {}


def kernel(x_prompt, x_sample, cache_k, cache_v, cache_idx_k, state_ssm_re, state_ssm_im, page_table,
           ffn1_norm, ffn1_w_gate, ffn1_w_up, ffn1_w_down, mix_norm, w_in, q_norm, k_norm,
           ssm_lambda_re, ssm_lambda_im, ssm_b_re, ssm_b_im, ssm_c_re, ssm_c_im, ssm_d, ssm_log_dt,
           glu_w, glu_b, w_branch_a, w_branch_b, w_out, ffn2_norm, ffn2_w_gate, ffn2_w_up, ffn2_w_down):
    import os
    stage_limit = int(os.environ.get("KSTAGE", "99"))
    f32 = np.float32
    A = lambda a: np.ascontiguousarray(np.asarray(a), dtype=f32)
    win = A(w_in)[0]
    u_w = win[:, 0:1024]; q_w = win[:, 1024:2048]; k_w = win[:, 2048:2304]; v_w = win[:, 2304:2560]
    qi_w = win[:, 2560:3584]; ki_w = win[:, 3584:3648]; wi_w = win[:, 3648:3664]
    q_r = _perm_cols(q_w, 128, 16); k_r = _perm_cols(k_w, 128, 16)
    qi_r = _perm_cols(qi_w, 64, 8); ki_r = _perm_cols(ki_w, 64, 8)
    tiles = [u_w[:, 128 * i:128 * i + 128] for i in range(8)]
    for h in range(8):
        tiles += [q_w[:, 128 * h:128 * h + 128], q_r[:, 128 * h:128 * h + 128]]
    for g in range(2):
        tiles += [k_w[:, 128 * g:128 * g + 128], k_r[:, 128 * g:128 * g + 128]]
    for t in range(8):
        tiles += [qi_w[:, 128 * t:128 * t + 128], qi_r[:, 128 * t:128 * t + 128]]
    tiles += [np.concatenate([ki_w, ki_w], 1), np.concatenate([ki_r, ki_r], 1)]
    wmix = np.ascontiguousarray(np.concatenate(tiles, 1))
    assert wmix.shape[1] == WMIX_TILES * 128
    wtok = np.ascontiguousarray(np.concatenate([v_w, wi_w, np.zeros((D, 512 - 272), f32)], 1))
    wgab = np.ascontiguousarray(win[:, 3664:7760])
    pl = lambda g: np.ascontiguousarray(A(g).reshape(-1, 128).T)
    nrm = np.concatenate([pl(ffn1_norm[0]), pl(mix_norm[0]), pl(ffn2_norm[0])], 1)
    qn = A(q_norm)[0]; kn = A(k_norm)[0]
    qkn = np.stack([qn, _perm_cols(qn[None], 128, 16)[0], kn, _perm_cols(kn[None], 128, 16)[0]], 1)
    glb = pl(glu_b[0])
    ident = np.eye(128, dtype=f32)
    tri = np.where(np.arange(128)[None, :] <= np.arange(128)[:, None], 0.0, NEG).astype(f32)
    iota1 = np.broadcast_to(np.arange(1, NB + 1, dtype=f32)[None, :], (128, NB)).copy()
    def sl_(a):
        return np.ascontiguousarray(A(a).reshape(32, 2, 64).transpose(1, 2, 0).reshape(128, 32))
    ldt = np.repeat(A(ssm_log_dt)[0][:, None], 64, 1)
    ssmp = np.concatenate([sl_(ssm_lambda_re[0]), sl_(ssm_lambda_im[0]), sl_(ldt)], 1)
    ssmd = np.ascontiguousarray(A(ssm_d)[0].reshape(8, 128).T)
    def bs_(b):
        b = A(b).reshape(32, 2, 64, 16)
        o = np.zeros((2, 64, 32, 128), f32)
        for j in range(32):
            for g2 in range(2):
                k0 = 32 * (j % 4) + 16 * g2
                o[g2, :, j, k0:k0 + 16] = b[j, g2]
        return np.ascontiguousarray(o.reshape(128, 32 * 128))
    def cs_(c):
        c = A(c).reshape(32, 2, 16, 64)
        o = np.zeros((2, 64, 32, 128), f32)
        for j in range(32):
            for g2 in range(2):
                m0 = 32 * (j % 4) + 16 * g2
                o[g2, :, j, m0:m0 + 16] = c[j, g2].T
        return np.ascontiguousarray(o.reshape(128, 32 * 128))
    bsre = bs_(ssm_b_re[0]); bsim = bs_(ssm_b_im[0]); cre = cs_(ssm_c_re[0]); cim = cs_(ssm_c_im[0])

    xp = A(x_prompt); xs = A(x_sample)
    sre = A(state_ssm_re)[0]; sim = A(state_ssm_im)[0]
    shared = dict(w1g=A(ffn1_w_gate)[0], w1u=A(ffn1_w_up)[0], w1d=A(ffn1_w_down)[0],
                  w2g=A(ffn2_w_gate)[0], w2u=A(ffn2_w_up)[0], w2d=A(ffn2_w_down)[0],
                  wmix=wmix, wtok=wtok, wgab=wgab, glw=A(glu_w)[0], wa=A(w_branch_a)[0], wb=A(w_branch_b)[0],
                  wo=A(w_out)[0], nrm=np.ascontiguousarray(nrm), qkn=np.ascontiguousarray(qkn), glb=glb,
                  ident=ident, tri=tri, iota1=iota1, hmask=np.stack([(np.arange(128) < 64), (np.arange(128) >= 64)], 1).astype(f32), ssmp=np.ascontiguousarray(ssmp), ssmd=ssmd,
                  bsre=bsre, bsim=bsim, cre=cre, cim=cim)
    ckh = A(cache_k)[0].reshape(40960, 2048); cvh = A(cache_v)[0].reshape(40960, 2048)
    cikh = A(cache_idx_k)[0].reshape(40960, 512)
    shared.update(ckh=ckh, cvh=cvh, cikh=cikh)
    tq = np.arange(16)
    mnew = np.where((tq[None, :] // 4 == tq[:, None] // 4) & (tq[None, :] % 4 <= tq[:, None] % 4), 0.0, NEG).astype(f32)
    shared.update(mnew=mnew)
    pt_np = np.asarray(page_table).astype(np.int32)
    pos_s = np.tile(PAST + np.arange(4), 4)
    rope = _rope_tables(np.concatenate([np.arange(SEQ), pos_s]))
    in_maps = []
    for c in range(N_CORES):
        pb = c // 2
        sbs = slice(4 * c, 4 * c + 4)
        xT = np.concatenate([xp[pb].T, xs[sbs].reshape(NS, D).T], 1)
        def h0l(s):
            return s.reshape(4, 32, 2, 64).transpose(2, 3, 1, 0).reshape(128, 32 * 4)
        h0 = np.concatenate([h0l(sre[sbs]), h0l(sim[sbs])], 1)
        m = dict(shared)
        ptl_c = np.ascontiguousarray(np.concatenate([pt_np[sbs].T, pt_np[sbs].T], 0).astype(np.int32))
        m.update(xT=np.ascontiguousarray(xT), rope=rope, h0=np.ascontiguousarray(h0), ptl=ptl_c)
        in_maps.append(m)

    import time as _t
    _t0 = _t.time()
    key = stage_limit
    if key not in _NC_CACHE:
        _NC_CACHE[key] = build_nc(stage_limit)
    nc = _NC_CACHE[key]
    _t1 = _t.time()
    res = run_bass_kernel_spmd(nc, in_maps, core_ids=list(range(N_CORES)))
    if os.environ.get("KTIME"):
        print("[kernel] build %.1fs run %.1fs" % (_t1 - _t0, _t.time() - _t1), flush=True)
    R = res.results
    y_p = np.stack([R[2 * b]["yT"][:, :SEQ].T for b in range(4)])
    y_s = np.concatenate([R[c]["yT"][:, SEQ:].T.reshape(4, 4, D) for c in range(N_CORES)])
    k_p = np.stack([R[2 * b]["kTo"][:, :SEQ].T.reshape(SEQ, 2, 128) for b in range(4)])[None]
    v_p = np.stack([R[2 * b]["vTo"][:, :SEQ].T.reshape(SEQ, 2, 128) for b in range(4)])[None]
    ik_p = np.stack([R[2 * b]["kiTo"][:64, :SEQ].T for b in range(4)])[None]
    def st_p(r, a):
        return r["ssp"][:, 32 * a:32 * a + 32].reshape(2, 64, 32).transpose(2, 0, 1).reshape(64, 64)
    re_p = np.stack([st_p(R[2 * b], 0) for b in range(4)])[None]
    im_p = np.stack([st_p(R[2 * b], 1) for b in range(4)])[None]
    k_s = np.concatenate([R[c]["kTo"][:, SEQ:].T.reshape(4, 4, 2, 128) for c in range(N_CORES)])[None]
    v_s = np.concatenate([R[c]["vTo"][:, SEQ:].T.reshape(4, 4, 2, 128) for c in range(N_CORES)])[None]
    ik_s = np.concatenate([R[c]["kiTo"][:64, SEQ:].T.reshape(4, 4, 64) for c in range(N_CORES)])[None]
    def st_s(r, a):
        return r["sss"][:, 128 * a:128 * a + 128].reshape(2, 64, 32, 4).transpose(3, 2, 0, 1).reshape(4, 64, 64)
    re_s = np.concatenate([st_s(R[c], 0) for c in range(N_CORES)])[None]
    im_s = np.concatenate([st_s(R[c], 1) for c in range(N_CORES)])[None]
    outs = (y_p, y_s, k_p, v_p, ik_p, re_p, im_p, k_s, v_s, ik_s, re_s, im_s)
    return tuple(np.ascontiguousarray(o, dtype=np.float32) for o in outs)
```

```python
import contextlib
import numpy as np
import ml_dtypes
import concourse.bass as bass
import concourse.mybir as mybir
from concourse.bass_utils import run_bass_kernel_spmd

F32 = mybir.dt.float32
BF16 = mybir.dt.bfloat16
I32 = mybir.dt.int32
ALU = mybir.AluOpType
AF = mybir.ActivationFunctionType
AX = mybir.AxisListType

D = 2048
DFF = 5632
NKC = D // 128
NFF = DFF // 128
SEQ = 2048
NB = 512
NBLK = SEQ // NB
NS = 16
NTOK = SEQ + NS
EPS = 1e-6
NEG = -1.0e30
TOPK = 256
PAST = 8192
NPAGE = 64
N_CORES = 8


class V:
    __slots__ = ("ap", "keys")

    def __init__(self, ap, keys):
        self.ap = ap
        self.keys = keys


class Buf:
    def __init__(self, handle, name):
        self.h = handle
        self.name = name

    def __getitem__(self, idx):
        return V(self.h[idx], ((self.name, None),))

    def sub(self, k, idx):
        return V(self.h[idx], ((self.name, k),))

    def multi(self, ks, idx):
        return V(self.h[idx], tuple((self.name, k) for k in ks))


def vv(ap, *views):
    keys = tuple(k for v in views for k in v.keys)
    return V(ap, keys)


class Op:
    __slots__ = ("idx", "eng", "emit", "dma", "deps", "inc", "cnt", "sem", "val", "prev_val")

    def __init__(self, idx, eng, emit, dma):
        self.idx = idx
        self.eng = eng
        self.emit = emit
        self.dma = dma
        self.deps = set()
        self.inc = False
        self.cnt = 0
        self.sem = None
        self.val = 0
        self.prev_val = 0


class Sched:
    def __init__(self):
        self.ops = []
        self.last_writer = {}
        self.readers = {}

    @staticmethod
    def _conf(d, name, sub):
        e = d.get(name)
        if not e:
            return []
        if sub is None:
            return list(e.values())
        out = []
        if sub in e:
            out.append(e[sub])
        if None in e:
            out.append(e[None])
        return out

    def add(self, eng, emit, reads=(), writes=(), dma=False):
        op = Op(len(self.ops), eng, emit, dma)
        rk = [k for v in reads if isinstance(v, V) for k in v.keys]
        wk = [k for v in writes if isinstance(v, V) for k in v.keys]
        for (name, sub) in rk:
            for w in self._conf(self.last_writer, name, sub):
                op.deps.add(w)
            if name.startswith("ps"):
                for rs in self._conf(self.readers, name, sub):
                    op.deps |= {r for r in rs if r.eng != eng}
        for (name, sub) in wk:
            for w in self._conf(self.last_writer, name, sub):
                op.deps.add(w)
            for rs in self._conf(self.readers, name, sub):
                op.deps |= rs
        for (name, sub) in rk:
            self.readers.setdefault(name, {}).setdefault(sub, set()).add(op)
        for (name, sub) in wk:
            lw = self.last_writer.setdefault(name, {})
            rd = self.readers.setdefault(name, {})
            if sub is None:
                lw.clear()
                rd.clear()
            lw[sub] = op
            rd[sub] = set()
        op.deps.discard(op)
        self.ops.append(op)
        return op

    def emit_all(self, nc, nsem_dma=14):
        engs = ["pe", "act", "dve", "pool", "sp"]
        per = {e: [] for e in engs}
        for op in self.ops:
            per[op.eng].append(op)
        for op in self.ops:
            for d in op.deps:
                if d.dma:
                    continue
                if d.eng == "pe" and op.eng == "pe" and not op.dma:
                    continue
                d.inc = True
        cnt = {e: 0 for e in engs}
        for op in self.ops:
            if not op.dma and op.inc:
                cnt[op.eng] += 1
                op.cnt = cnt[op.eng]
        with contextlib.ExitStack() as st:
            csem = {e: st.enter_context(nc.semaphore("c_" + e)) for e in ["pe", "act", "dve", "pool"]}
            dsem = {e: [st.enter_context(nc.semaphore("d_%s%d" % (e, i))) for i in range(nsem_dma)]
                    for e in ["pool", "sp"]}
            duse = {e: [0] * nsem_dma for e in dsem}
            dnext = {e: 0 for e in dsem}
            for op in self.ops:
                if op.dma:
                    i = dnext[op.eng]
                    dnext[op.eng] = (i + 1) % nsem_dma
                    op.sem = dsem[op.eng][i]
                    op.prev_val = duse[op.eng][i]
                    duse[op.eng][i] += 16
                    op.val = duse[op.eng][i]
            block = st.enter_context(nc.Block())

            def run(engname, e):
                waited = {}

                def w(sem, val):
                    if val <= 0 or waited.get(sem.name, 0) >= val:
                        return
                    waited[sem.name] = val
                    e.wait_ge(sem, val)

                for op in per[engname]:
                    for d in sorted(op.deps, key=lambda o: o.idx):
                        if d.dma:
                            w(d.sem, d.val)
                        else:
                            if d.eng == "pe" and engname == "pe" and not op.dma:
                                continue
                            w(csem[d.eng], d.cnt)
                    if op.dma:
                        w(op.sem, op.prev_val)
                        ins = op.emit(e)
                        ins.then_inc(op.sem, 16)
                    else:
                        ins = op.emit(e)
                        if op.inc:
                            ins.then_inc(csem[engname], 1)
                if engname in dsem:
                    for i, s in enumerate(dsem[engname]):
                        w(s, duse[engname][i])

            @block.tensor
            def _(e):
                run("pe", e)

            @block.scalar
            def _(e):
                run("act", e)

            @block.vector
            def _(e):
                run("dve", e)

            @block.gpsimd
            def _(e):
                run("pool", e)

            @block.sync
            def _(e):
                run("sp", e)


WMIX_TILES = 46


def build_nc(stage_limit=99):
    import os as _os
    KSUB = int(_os.environ.get("KSUB", "99"))
    KATT = int(_os.environ.get("KATT", "1"))
    KV = int(_os.environ.get("KV", "3"))
    nc = bass.Bass("TRN2", target_bir_lowering=False)

    def din(name, shape, dt=F32):
        return nc.dram_tensor(name, list(shape), dt, kind="ExternalInput").ap()

    def dout(name, shape, dt=F32):
        return nc.dram_tensor(name, list(shape), dt, kind="ExternalOutput").ap()

    xT_d = din("xT", [D, NTOK])
    w1g_d = din("w1g", [D, DFF]); w1u_d = din("w1u", [D, DFF]); w1d_d = din("w1d", [DFF, D])
    w2g_d = din("w2g", [D, DFF]); w2u_d = din("w2u", [D, DFF]); w2d_d = din("w2d", [DFF, D])
    wmix_d = din("wmix", [D, WMIX_TILES * 128])
    wtok_d = din("wtok", [D, 512])
    wgab_d = din("wgab", [D, 4096])
    glw_d = din("glw", [1024, 1024]); wa_d = din("wa", [1024, D]); wb_d = din("wb", [1024, D]); wo_d = din("wo", [D, D])
    nrm_d = din("nrm", [128, 3 * 16])
    qkn_d = din("qkn", [128, 4])
    glb_d = din("glb", [128, 8])
    rope_d = din("rope", [4, 128, NTOK])
    ident_d = din("ident", [128, 128]); tri_d = din("tri", [128, 128]); iota_d = din("iota1", [128, NB])
    ssmp_d = din("ssmp", [128, 3 * 32])
    ssmd_d = din("ssmd", [128, 8])
    bsre_d = din("bsre", [128, 32 * 128]); bsim_d = din("bsim", [128, 32 * 128])
    cre_d = din("cre", [128, 32 * 128]); cim_d = din("cim", [128, 32 * 128])
    h0_d = din("h0", [128, 2 * 32 * 4])
    hmask_d = din("hmask", [128, 2])
    ck_d = din("ckh", [40960, 2048]); cv_d = din("cvh", [40960, 2048]); cik_d = din("cikh", [40960, 512])
    ptl_d = din("ptl", [128, 4], I32)
    mnew_d = din("mnew", [16, 16])

    yT_o = dout("yT", [D, NTOK])
    kT_o = dout("kTo", [256, NTOK])
    vT_o = dout("vTo", [256, NTOK])
    kiT_o = dout("kiTo", [128, NTOK])
    sp_o = dout("ssp", [128, 2 * 32])
    ss_o = dout("sss", [128, 2 * 32 * 4])

    S = Sched()
    with contextlib.ExitStack() as st:
        def sb(name, shape, dt):
            return Buf(st.enter_context(nc.sbuf_tensor("s_" + name, list(shape), dt)), name)

        def psb(name, shape, dt):
            return Buf(st.enter_context(nc.psum_tensor("p_" + name, list(shape), dt)), name)

        xT = sb("xTs", [128, NKC, NB], F32)
        hT = sb("hT", [128, NKC, NB], BF16)
        arena = sb("arena", [128, NFF, NB], BF16)
        wsl = [sb("wsl%d" % i, [128, 8192], BF16) for i in range(2)]
        ps = [psb("ps%d" % i, [128, 512], F32) for i in range(8)]
        kTs = sb("kTs", [128, 2, SEQ], BF16)
        vS = sb("vS", [128, 16, 256], BF16)
        kiA = sb("kiA", [128, SEQ], BF16)
        kiB = sb("kiB", [128, SEQ], BF16)
        Dd = sb("Dd", [128, 8, 128], BF16)
        identb = sb("identb", [128, 128], BF16); identf = sb("identf", [128, 128], F32)
        onesb = sb("onesb", [128, 128], BF16)
        tri = sb("tri", [128, 128], F32)
        hmask = sb("hmask", [128, 2], F32)
        iota1 = sb("iota1s", [128, NB], F32)
        nrm = sb("nrm", [128, 48], F32); qkn = sb("qkn", [128, 4], F32); glb = sb("glb", [128, 8], F32)
        ssmp = sb("ssmp", [128, 96], F32); ssmd = sb("ssmd", [128, 8], F32)
        sdec = sb("sdec", [128, 32], F32)
        sfrq = sb("sfrq", [128, 32], F32)
        lbre = sb("lbre", [128, 32], F32); lbim = sb("lbim", [128, 32], F32)
        gre = sb("gre", [128, 32], F32); gim = sb("gim", [128, 32], F32)
        stp = sb("stp", [128, 8, 32], F32)
        car = sb("car", [128, 2, 32], F32)
        h0s = sb("h0s", [128, 2, 32, 4], F32)
        hss = sb("hss", [128, 2, 32, 4], F32)
        sq = sb("sq", [128, 4, NB], BF16)
        rstd = sb("rstd", [128, NB], F32)
        tA = sb("tA", [128, NB], F32); tB = sb("tB", [128, NB], F32); tC = sb("tC", [128, NB], F32)
        tD = sb("tD", [128, NB], F32); tE = sb("tE", [128, NB], F32); tF = sb("tF", [128, NB], F32)
        tI = sb("tI", [128, NB], I32)
        cosT = sb("cosT", [128, NB], F32); sinT = sb("sinT", [128, NB], F32)
        hre = sb("hre", [128, NB], BF16); him = sb("him", [128, NB], BF16)
        mskT = sb("mskT", [128, 16, 128], BF16)
        rl = [sb("rl%d" % i, [128, NB], BF16) for i in range(2)]
        pe_ = [sb("pe%d" % i, [128, NB], BF16) for i in range(2)]
        Dm = sb("Dm", [128, 16, 128], BF16)
        wtk = sb("wtk", [128, 4, 16], F32)
        bs = sb("bs", [128, 8], F32)
        sA = sb("sA", [128, NFF, NS], BF16)
        sQI = sb("sQI", [128, 16, NS], BF16)
        DmS = sb("DmS", [16, 16, 16], BF16)
        ptl = sb("ptl", [128, 4], I32); idxh = sb("idxh", [128, 4], I32); idx8 = sb("idx8", [128, 4, 8], I32)
        mnew = sb("mnew", [16, 16], F32)
        bs2 = sb("bs2", [16, 8], F32)
        kst = sb("kst", [128, NB], F32)

        cur = {"sample": False}

        def slot(i, n=NB):
            if cur["sample"]:
                return sA.sub(i, (slice(None), i, slice(0, n)))
            return arena.sub(i, (slice(None), i, slice(0, n)))

        def slotc(i, a, b):
            return arena.sub(i, (slice(None), i, slice(a, b)))

        IscAP = arena.h[:, 24:32, :].rearrange("p s n -> p (s n)").bitcast(F32)
        mskAP = arena.h[:, 40:44, :].rearrange("p s n -> p (s n)")
        ISC_KEYS = arena.multi(range(24, 32), (slice(None), slice(24, 32), slice(None)))
        MSK_KEYS = arena.multi(range(40, 44), (slice(None), slice(40, 44), slice(None)))

        def Isc(a, b):
            return vv(IscAP[:, a:b], ISC_KEYS)

        def mskv(a, b):
            return vv(mskAP[:, a:b], MSK_KEYS)

        ps6b = ps[7].h[:].bitcast(BF16)

        A_U, A_Q, A_QI, A_ZA, A_BO = 0, 8, 16, 24, 32

        def dma(eng, out, in_, reads=(), writes=()):
            S.add(eng, lambda e: e.dma_start(out=out.ap if isinstance(out, V) else out,
                                             in_=in_.ap if isinstance(in_, V) else in_),
                  reads=list(reads), writes=list(writes), dma=True)

        def mm(out, lhsT, rhs, start, stop):
            S.add("pe", lambda e: e.matmul(out.ap, lhsT=lhsT.ap, rhs=rhs.ap, start=start, stop=stop),
                  reads=[lhsT, rhs], writes=[out])

        def act(out, in_, func, scale=1.0, bias=0.0, accum=None, extra_reads=()):
            kw = {}
            if accum is not None:
                kw["accum_out"] = accum.ap
            b = bias.ap if isinstance(bias, V) else bias
            sc = scale.ap if isinstance(scale, V) else scale
            S.add("act", lambda e: e.activation(out=out.ap, in_=in_.ap, func=func, bias=b, scale=sc, **kw),
                  reads=[in_, bias, scale] + list(extra_reads), writes=[out] + ([accum] if accum is not None else []))

        def tt(out, in0, in1, op, eng="dve"):
            S.add(eng, lambda e: e.tensor_tensor(out=out.ap, in0=in0.ap, in1=in1.ap, op=op),
                  reads=[in0, in1], writes=[out])

        def ts(out, in0, s1, s2, op0, op1=None, accum=None, eng="dve"):
            a1 = s1.ap if isinstance(s1, V) else s1
            a2 = s2.ap if isinstance(s2, V) else s2
            kw = {}
            if op1 is not None:
                kw["op1"] = op1
            if accum is not None:
                kw["accum_out"] = accum.ap
            S.add(eng, lambda e: e.tensor_scalar(out=out.ap, in0=in0.ap, scalar1=a1, scalar2=a2, op0=op0, **kw),
                  reads=[in0, s1, s2], writes=[out] + ([accum] if accum is not None else []))

        def stt(out, in0, scalar, in1, op0, op1, eng="dve"):
            a = scalar.ap if isinstance(scalar, V) else scalar
            S.add(eng, lambda e: e.scalar_tensor_tensor(out=out.ap, in0=in0.ap, scalar=a, in1=in1.ap, op0=op0, op1=op1),
                  reads=[in0, scalar, in1], writes=[out])

        def cp(out, in_, eng="dve"):
            S.add(eng, lambda e: e.tensor_copy(out=out.ap, in_=in_.ap), reads=[in_], writes=[out])

        def memset(out, val, eng="dve"):
            S.add(eng, lambda e: e.memset(out.ap, val), writes=[out])

        def recip(out, in_):
            S.add("dve", lambda e: e.reciprocal(out=out.ap, in_=in_.ap), reads=[in_], writes=[out])

        def transpose(out, in_, ident):
            S.add("pe", lambda e: e.transpose(out.ap, in_.ap, ident.ap), reads=[in_, ident], writes=[out])

        wstate = {"n": 0}

        NSLAB = 128
        Wscr_d = nc.dram_tensor("Wscr", [NSLAB, 128, 8192], BF16).ap()
        wblk = {"blk": 0, "k": 0}

        def wload(parts, extra_reads=(), used=None):
            sl = wsl[wstate["n"] % 2]
            wstate["n"] += 1
            k = wblk["k"]
            wblk["k"] += 1
            assert k < NSLAB
            skey = V(None, (("Wscr", k),))
            if wblk["blk"] == 0:
                for (c0, ncols, nkc, src) in parts:
                    dst = sl.h[:, c0:c0 + nkc * ncols].rearrange("p (k c) -> p k c", k=nkc)
                    step = max(1, min(nkc, 8))
                    for k0 in range(0, nkc, step):
                        k1 = min(nkc, k0 + step)
                        S.add("pool", lambda e, d=(dst[:, k0:k1, :] if used is None else dst[:, k0:k1, 0:used]), s_=src[:, k0:k1, :]: e.dma_start(out=d, in_=s_),
                              reads=list(extra_reads), writes=[sl[:]], dma=True)
                S.add("sp", lambda e, sl=sl, k=k: e.dma_start(out=Wscr_d[k, :, :], in_=sl.h[:, :]), reads=[sl[:]], writes=[skey], dma=True)
            else:
                S.add("sp", lambda e, sl=sl, k=k: e.dma_start(out=sl.h[:, :], in_=Wscr_d[k, :, :]), reads=[skey], writes=[sl[:]], dma=True)
            return sl

        def wview(sl, c0, ncols, kc, a, b):
            o = c0 + kc * ncols
            return sl[:, o + a:o + b]

        def wsrc(w_d, nkc, c0, c1):
            return w_d.rearrange("(k p) c -> p k c", p=128)[:, :, c0:c1]

        dma("sp", identf[:], ident_d, writes=[identf[:]])
        dma("pool", identb[:], ident_d, writes=[identb[:]])
        dma("sp", tri[:], tri_d, writes=[tri[:]])
        dma("sp", iota1[:], iota_d, writes=[iota1[:]])
        dma("sp", nrm[:], nrm_d, writes=[nrm[:]])
        dma("sp", qkn[:], qkn_d, writes=[qkn[:]])
        dma("sp", glb[:], glb_d, writes=[glb[:]])
        dma("sp", hmask[:], hmask_d, writes=[hmask[:]])
        dma("sp", ptl[:], ptl_d, writes=[ptl[:]])
        dma("sp", mnew[:], mnew_d, writes=[mnew[:]])
        ts(idxh[:], ptl[:], 2.0, None, ALU.mult)
        ts(idxh[:], idxh[:], hmask[:, 1:2], None, ALU.add)
        for ch in range(8):
            ts(idx8[:, :, ch], idxh[:], 8.0, float(ch), ALU.mult, ALU.add)
        dma("sp", ssmp[:], ssmp_d, writes=[ssmp[:]])
        dma("sp", ssmd[:], ssmd_d, writes=[ssmd[:]])
        dma("sp", vv(h0s.h[:].rearrange("p a j b -> p (a j b)"), h0s[:]), h0_d, writes=[h0s[:]])
        memset(onesb[:], 1.0)
        memset(wsl[0][:], 0.0)
        memset(wsl[1][:], 0.0)
        memset(hre[:], 0.0)
        memset(him[:], 0.0)
        memset(car[:], 0.0)
        for c in range(8):
            ts(Dd.sub(c, (slice(None), c, slice(None))), identf[:], ssmd[:, c:c + 1], None, ALU.mult)

        lre = ssmp[:, 0:32]; lim = ssmp[:, 32:64]; ldt = ssmp[:, 64:96]

        def T(i):
            return stp.sub(i, (slice(None), i, slice(None)))
        act(T(0), ldt, AF.Exp)
        tt(T(1), lre, T(0), ALU.mult)
        tt(T(2), lim, T(0), ALU.mult)
        act(sdec[:], T(1), AF.Exp)
        ts(T(3), T(2), float(1.0 / (2 * np.pi)), None, ALU.mult)
        stpi = sb("stpi", [128, 32], I32)
        cp(stpi[:], T(3))
        stt(sfrq[:], stpi[:], -1.0, T(3), ALU.mult, ALU.add)
        act(T(4), sfrq[:], AF.Sin, scale=float(2 * np.pi))
        ts(T(5), sfrq[:], 0.25, None, ALU.add)
        cp(stpi[:], T(5))
        stt(T(5), stpi[:], -1.0, T(5), ALU.mult, ALU.add)
        act(T(5), T(5), AF.Sin, scale=float(2 * np.pi))
        tt(lbre[:], sdec[:], T(5), ALU.mult)
        tt(lbim[:], sdec[:], T(4), ALU.mult)
        ts(T(6), lbre[:], -1.0, None, ALU.add)
        tt(T(0), lre, lre, ALU.mult)
        tt(T(1), lim, lim, ALU.mult)
        tt(T(0), T(0), T(1), ALU.add)
        recip(T(0), T(0))
        tt(T(1), T(6), lre, ALU.mult)
        tt(T(2), lbim[:], lim, ALU.mult)
        tt(T(1), T(1), T(2), ALU.add)
        tt(gre[:], T(1), T(0), ALU.mult)
        tt(T(1), lbim[:], lre, ALU.mult)
        tt(T(2), T(6), lim, ALU.mult)
        tt(T(1), T(1), T(2), ALU.subtract)
        tt(gim[:], T(1), T(0), ALU.mult)
        Bscr_d = nc.dram_tensor("Bscr", [128, 2, 32, 128], BF16).ap()
        BSCR = V(None, (("Bscr", None),))
        for ch in range(4):
            js = slice(8 * ch, 8 * ch + 8)

            def f32v(s0):
                ks = list(range(s0, s0 + 4))
                v = arena.multi(ks, (slice(None), slice(s0, s0 + 4), slice(None)))
                return vv(v.ap.rearrange("p s n -> p (s n)").bitcast(F32).rearrange("p (j m) -> p j m", j=8), v)
            bre_raw = f32v(0); bim_raw = f32v(4); o_re = f32v(8); o_im = f32v(12); tmp = f32v(16)
            dma("sp", bre_raw, bsre_d.rearrange("p (j m) -> p j m", j=32)[:, js, :], writes=[bre_raw])
            dma("sp", bim_raw, bsim_d.rearrange("p (j m) -> p j m", j=32)[:, js, :], writes=[bim_raw])
            g_re_b = vv(gre.h[:, js].unsqueeze(2).to_broadcast([128, 8, 128]), gre[:])
            g_im_b = vv(gim.h[:, js].unsqueeze(2).to_broadcast([128, 8, 128]), gim[:])
            tt(o_re, bre_raw, g_re_b, ALU.mult)
            tt(tmp, bim_raw, g_im_b, ALU.mult)
            tt(o_re, o_re, tmp, ALU.subtract)
            tt(o_im, bim_raw, g_re_b, ALU.mult)
            tt(tmp, bre_raw, g_im_b, ALU.mult)
            tt(o_im, o_im, tmp, ALU.add)
            stg = [vv(sq.h[:, 0:2, :].rearrange("p a n -> p (a n)").rearrange("p (j m) -> p j m", j=8), sq[:]),
                   vv(sq.h[:, 2:4, :].rearrange("p a n -> p (a n)").rearrange("p (j m) -> p j m", j=8), sq[:])]
            for jj in range(8):
                for (ri, src, pb) in ((0, o_re, ps[0]), (1, o_im, ps[1])):
                    transpose(pb[:, 0:128], vv(src.ap[:, jj, :], src), identf[:])
                    act(vv(stg[ri].ap[:, jj, :], sq[:]), pb[:, 0:128], AF.Copy)
            for ri in range(2):
                dma("sp", Bscr_d[:, ri, js, :], stg[ri], reads=[sq[:]], writes=[BSCR])

        def rmsnorm(n, gcol0):
            for grp in range(4):
                src = xT.multi(range(4 * grp, 4 * grp + 4), (slice(None), slice(4 * grp, 4 * grp + 4), slice(0, n)))
                act(sq[:, :, 0:n], src, AF.Square)
                for i in range(4):
                    mm(ps[6][:, 0:n], onesb[:], sq[:, i, 0:n], start=(grp == 0 and i == 0), stop=(grp == 3 and i == 3))
            act(rstd[:, 0:n], ps[6][:, 0:n], AF.Sqrt, scale=1.0 / D, bias=EPS)
            recip(rstd[:, 0:n], rstd[:, 0:n])
            for j in range(NKC):
                stt(hT.sub(j, (slice(None), j, slice(0, n))), xT.sub(j, (slice(None), j, slice(0, n))),
                    nrm[:, gcol0 + j:gcol0 + j + 1], rstd[:, 0:n], ALU.mult, ALU.mult)

        def ffn(n, wg_d, wu_d, wd_d):
            hall = lambda kc: hT[:, kc, 0:n]
            for s in range(NFF // 2):
                sl = wload([(0, 256, NKC, wsrc(wg_d, NKC, 256 * s, 256 * s + 256)),
                            (4096, 256, NKC, wsrc(wu_d, NKC, 256 * s, 256 * s + 256))])
                for half in range(2):
                    f = 2 * s + half
                    pg = ps[2 * (f % 2)]; pu = ps[2 * (f % 2) + 1]
                    for kc in range(NKC):
                        mm(pg[:, 0:n], wview(sl, 0, 256, kc, 128 * half, 128 * half + 128), hall(kc), kc == 0, kc == NKC - 1)
                    for kc in range(NKC):
                        mm(pu[:, 0:n], wview(sl, 4096, 256, kc, 128 * half, 128 * half + 128), hall(kc), kc == 0, kc == NKC - 1)
                    tmp = tA if f % 2 == 0 else tB
                    act(tmp[:, 0:n], pg[:, 0:n], AF.Silu)
                    tt(slot(f, n), tmp[:, 0:n], pu[:, 0:n], ALU.mult)
            for j in range(NKC):
                sl = wload([(0, 128, NFF, wsrc(wd_d, NFF, 128 * j, 128 * j + 128))])
                pd = ps[4 + (j % 2)]
                for kc in range(NFF):
                    mm(pd[:, 0:n], wview(sl, 0, 128, kc, 0, 128), slot(kc, n), kc == 0, kc == NFF - 1)
                xj = xT.sub(j, (slice(None), j, slice(0, n)))
                stt(xj, pd[:, 0:n], 0.5, xj, ALU.mult, ALU.add)

        def proj_pair(sl, t0, n, pa, pb):
            for (tix, pp) in ((t0, pa), (t0 + 1, pb)):
                for kc in range(NKC):
                    mm(pp[:, 0:n], wview(sl, 0, 512, kc, 128 * tix, 128 * tix + 128), hT[:, kc, 0:n], kc == 0, kc == NKC - 1)

        def mix_slab(i):
            return wload([(0, 512, NKC, wsrc(wmix_d, NKC, 512 * i, 512 * i + 512))])

        def mix(blk, n, c0):
            sample = blk == NBLK
            rmsnorm(n, 16)
            cosK = tE[:, 0:n]; sinK = tF[:, 0:n]; cosI = tE[:, 0:n]; sinI = tF[:, 0:n]
            dma("sp", tE[:, 0:n], rope_d[0, :, c0:c0 + n], writes=[tE[:]])
            dma("sp", tF[:, 0:n], rope_d[1, :, c0:c0 + n], writes=[tF[:]])
            for i in range(2):
                sl = mix_slab(i)
                for t in range(4):
                    pp = ps[t % 4]
                    for kc in range(NKC):
                        mm(pp[:, 0:n], wview(sl, 0, 512, kc, 128 * t, 128 * t + 128), hT[:, kc, 0:n], kc == 0, kc == NKC - 1)
                    act(slot(A_U + 4 * i + t, n), pp[:, 0:n], AF.Copy)

            def normrope(pa, pb, gcol, out_bf, out_f32=None):
                act(sq[:, 0, 0:n], pa[:, 0:n], AF.Square)
                mm(ps[6][:, 0:n], onesb[:], sq[:, 0, 0:n], True, True)
                act(rstd[:, 0:n], ps[6][:, 0:n], AF.Sqrt, scale=1.0 / 128, bias=EPS)
                recip(rstd[:, 0:n], rstd[:, 0:n])
                stt(tC[:, 0:n], pa[:, 0:n], qkn[:, gcol:gcol + 1], rstd[:, 0:n], ALU.mult, ALU.mult)
                tt(tC[:, 0:n], tC[:, 0:n], cosK, ALU.mult)
                stt(tD[:, 0:n], pb[:, 0:n], qkn[:, gcol + 1:gcol + 2], rstd[:, 0:n], ALU.mult, ALU.mult)
                tt(tD[:, 0:n], tD[:, 0:n], sinK, ALU.mult)
                if out_f32 is not None:
                    tt(out_f32, tC[:, 0:n], tD[:, 0:n], ALU.add)
                    act(out_bf, out_f32, AF.Copy)
                else:
                    tt(out_bf, tC[:, 0:n], tD[:, 0:n], ALU.add)

            if KSUB < 2:
                return False
            for i in range(4):
                sl = mix_slab(2 + i)
                for hh in range(2):
                    h = 2 * i + hh
                    pa, pb = ps[2 * hh], ps[2 * hh + 1]
                    proj_pair(sl, 2 * hh, n, pa, pb)
                    normrope(pa, pb, 0, slot(A_Q + h, n))
            if KSUB < 3:
                return False
            sl = mix_slab(6)
            for g in range(2):
                pa, pb = ps[2 * g], ps[2 * g + 1]
                proj_pair(sl, 2 * g, n, pa, pb)
                if not sample:
                    kdst = kTs.sub(blk, (slice(None), g, slice(c0, c0 + n)))
                else:
                    kdst = knew.sub(g, (slice(None), g, slice(0, n)))
                normrope(pa, pb, 2, kdst, out_f32=kst[:, 0:n])
                dma("sp", kT_o[128 * g:128 * g + 128, c0:c0 + n], kst[:, 0:n], reads=[kst[:]])
            if KSUB < 4:
                return False
            dma("sp", tE[:, 0:n], rope_d[2, :, c0:c0 + n], writes=[tE[:]])
            dma("sp", tF[:, 0:n], rope_d[3, :, c0:c0 + n], writes=[tF[:]])
            for i in range(4):
                sl = mix_slab(7 + i)
                for hh in range(2):
                    t = 2 * i + hh
                    pa, pb = ps[2 * hh], ps[2 * hh + 1]
                    if not sample:
                        proj_pair(sl, 2 * hh, n, pa, pb)
                        tt(tC[:, 0:n], pa[:, 0:n], cosI, ALU.mult)
                        tt(tD[:, 0:n], pb[:, 0:n], sinI, ALU.mult)
                        tt(slot(A_QI + t, n), tC[:, 0:n], tD[:, 0:n], ALU.add)
                    else:
                        for a in range(2):
                            for (tix, pp) in ((2 * hh, pa), (2 * hh + 1, pb)):
                                for kc in range(NKC):
                                    mm(pp[0:64, 0:n], wview(sl, 0, 512, kc, 128 * tix + 64 * a, 128 * tix + 64 * a + 64),
                                       hT[:, kc, 0:n], kc == 0, kc == NKC - 1)
                            tt(tC[0:64, 0:n], pa[0:64, 0:n], tE[0:64, 0:n], ALU.mult)
                            tt(tD[0:64, 0:n], pb[0:64, 0:n], tF[0:64, 0:n], ALU.mult)
                            tt(sQI[0:64, 2 * t + a, 0:n], tC[0:64, 0:n], tD[0:64, 0:n], ALU.add)
            if KSUB < 5:
                return False
            sl = wload([(0, 256, NKC, wsrc(wmix_d, NKC, 44 * 128, 46 * 128))])
            pa, pb = ps[0], ps[1]
            for (tix, pp) in ((0, pa), (1, pb)):
                for kc in range(NKC):
                    mm(pp[:, 0:n], wview(sl, 0, 256, kc, 128 * tix, 128 * tix + 128), hT[:, kc, 0:n], kc == 0, kc == NKC - 1)
            tt(tC[:, 0:n], pa[:, 0:n], cosI, ALU.mult)
            tt(tD[:, 0:n], pb[:, 0:n], sinI, ALU.mult)
            tt(kst[:, 0:n], tC[:, 0:n], tD[:, 0:n], ALU.add)
            dma("sp", kiT_o[:, c0:c0 + n], kst[:, 0:n], reads=[kst[:]])
            if not sample:
                ts(kiA.sub(blk, (slice(None), slice(c0, c0 + n))), kst[:, 0:n], hmask[:, 0:1], None, ALU.mult)
                ts(kiB.sub(blk, (slice(None), slice(c0, c0 + n))), kst[:, 0:n], hmask[:, 1:2], None, ALU.mult)
            else:
                ts(kinA[:, 0:n], kst[:, 0:n], hmask[:, 0:1], None, ALU.mult)
                ts(kinB[:, 0:n], kst[:, 0:n], hmask[:, 1:2], None, ALU.mult)
            if KSUB < 6:
                return False
            sl = wload([(0, 512, NKC, wsrc(wtok_d, NKC, 0, 512))])
            ntt = (n + 127) // 128
            for g in range(2):
                pp = ps[g]
                for kc in range(NKC):
                    mm(pp[:, 0:n], wview(sl, 0, 512, kc, 128 * g, 128 * g + 128), hT[:, kc, 0:n], kc == 0, kc == NKC - 1)
                cp(kst[:, 0:n], pp[:, 0:n])
                dma("sp", vT_o[128 * g:128 * g + 128, c0:c0 + n], kst[:, 0:n], reads=[kst[:]])
                if KATT and (KV & 1):
                    act(hre[:, 0:n], pp[:, 0:n], AF.Copy)
                    for tti in range(ntt):
                        m = max(32, min(128, n - 128 * tti))
                        transpose(vv(ps6b[0:m, 128 * tti:128 * tti + 128], ps[7][:]), hre[:, 128 * tti:128 * tti + m], identb[:])
                    if not sample:
                        for tti in range(ntt):
                            gt = blk * 4 + tti
                            act(vS.sub(gt, (slice(None), gt, slice(128 * g, 128 * g + 128))),
                                vv(ps6b[:, 128 * tti:128 * tti + 128], ps[7][:]), AF.Copy)
                    else:
                        act(vnew[0:n, 128 * g:128 * g + 128], vv(ps6b[0:n, 0:128], ps[7][:]), AF.Copy)
            if KATT and (KV & 2):
                pw = ps[2]
                if n < 32:
                    memset(tC[0:32, 0:32], 0.0)
                for kc in range(NKC):
                    mm(pw[0:32, 0:n], wview(sl, 0, 512, kc, 256, 288), hT[:, kc, 0:n], kc == 0, kc == NKC - 1)
                ts(tC[0:32, 0:n], pw[0:32, 0:n], 0.25, None, ALU.mult)
                for tti in range(ntt):
                    m = max(32, min(128, n - 128 * tti))
                    transpose(ps[3][0:m, 32 * tti:32 * tti + 32], tC[0:32, 128 * tti:128 * tti + m], identf[0:32, 0:32])
                for tti in range(ntt):
                    m = min(128, n - 128 * tti)
                    cp(wtk.sub(tti, (slice(0, m), tti, slice(None))), ps[3][0:m, 32 * tti:32 * tti + 16])
            if stage_limit < 3:
                return False
            if not sample:
                ssm_prompt(n)
            else:
                ssm_sample()
            if stage_limit < 4:
                return False
            glu(n)
            if stage_limit < 5:
                return False
            if not sample and KATT:
                for qt in range(4):
                    attn_prompt(blk, qt)
            else:
                attn_sample()
            if stage_limit < 6:
                return False
            merge_out(n)
            return True

        def glu(n):
            sl = wload([(0, 1024, 8, wsrc(glw_d, 8, 0, 1024))])
            for j in range(8):
                pp = ps[j % 4]
                for kc in range(8):
                    mm(pp[:, 0:n], wview(sl, 0, 1024, kc, 128 * j, 128 * j + 128), slot(A_ZA + kc, n), kc == 0, kc == 7)
                tmp = tA if j % 2 == 0 else tB
                act(tmp[:, 0:n], pp[:, 0:n], AF.Sigmoid, bias=glb[:, j:j + 1])
                tt(slot(A_U + j, n), tmp[:, 0:n], slot(A_ZA + j, n), ALU.mult)

        def merge_out(n):
            for j in range(16):
                sl = wload([(0, 128, 8, wsrc(wa_d, 8, 128 * j, 128 * j + 128)),
                            (1024, 128, 8, wsrc(wb_d, 8, 128 * j, 128 * j + 128)),
                            (2048, 128, 16, wsrc(wgab_d, 16, 128 * j, 128 * j + 128)),
                            (4096, 128, 16, wsrc(wgab_d, 16, 2048 + 128 * j, 2048 + 128 * j + 128))])
                b0 = 0
                pA, pB, pga, pgb = ps[b0], ps[b0 + 1], ps[b0 + 2], ps[b0 + 3]
                for kc in range(8):
                    mm(pA[:, 0:n], wview(sl, 0, 128, kc, 0, 128), slot(A_U + kc, n), kc == 0, kc == 7)
                for kc in range(8):
                    mm(pB[:, 0:n], wview(sl, 1024, 128, kc, 0, 128), slot(A_BO + kc, n), kc == 0, kc == 7)
                for kc in range(16):
                    mm(pga[:, 0:n], wview(sl, 2048, 128, kc, 0, 128), hT[:, kc, 0:n], kc == 0, kc == 15)
                for kc in range(16):
                    mm(pgb[:, 0:n], wview(sl, 4096, 128, kc, 0, 128), hT[:, kc, 0:n], kc == 0, kc == 15)
                act(tA[:, 0:n], pga[:, 0:n], AF.Sigmoid)
                act(tB[:, 0:n], pgb[:, 0:n], AF.Sigmoid)
                tt(tC[:, 0:n], tA[:, 0:n], pA[:, 0:n], ALU.mult)
                tt(tD[:, 0:n], tB[:, 0:n], pB[:, 0:n], ALU.mult)
                tt(slot(A_Q + j, n), tC[:, 0:n], tD[:, 0:n], ALU.add)
            for i in range(4):
                sl = wload([(0, 512, NKC, wsrc(wo_d, NKC, 512 * i, 512 * i + 512))])
                for t in range(4):
                    j = 4 * i + t
                    pp = ps[j % 4]
                    for kc in range(NKC):
                        mm(pp[:, 0:n], wview(sl, 0, 512, kc, 128 * t, 128 * t + 128), slot(A_Q + kc, n), kc == 0, kc == NKC - 1)
                    xj = xT.sub(j, (slice(None), j, slice(0, n)))
                    tt(xj, pp[:, 0:n], xj, ALU.add)

        NIT = 22

        def bisect(L, lo, W, mid, cnt, gsel):
            pr = lo.ap.shape[0]
            for it in range(NIT):
                sc = float(2.0 ** -(it + 1))
                stt(mid, W, sc, lo, ALU.mult, ALU.add)
                ts(vv(mskAP[0:pr, 0:L], MSK_KEYS), vv(IscAP[0:pr, 0:L], ISC_KEYS), mid, None, ALU.is_ge, ALU.add, accum=cnt)
                ts(gsel, cnt, TOPK - 0.5, sc, ALU.is_ge, ALU.mult)
                stt(lo, gsel, W, lo, ALU.mult, ALU.add)

        def attn_prompt(blk, qt):
            G = 4 * blk + qt
            L = 128 * (G + 1)
            nkb = G + 1
            q0 = 128 * qt
            for h in range(16):
                ts(Dm.sub(h, (slice(None), h, slice(None))), identf[:], wtk[:, qt, h:h + 1], None, ALU.mult)
            for ch in range((L + 511) // 512):
                w = min(512, L - 512 * ch)
                pI = ps[2]
                for h in range(16):
                    psc = ps[h % 2]
                    kx = kiA if h % 2 == 0 else kiB
                    mm(psc[:, 0:w], slotc(A_QI + h // 2, q0, q0 + 128), kx[:, 512 * ch:512 * ch + w], True, True)
                    act(rl[h % 2][:, 0:w], psc[:, 0:w], AF.Relu, scale=0.125)
                    mm(pI[:, 0:w], Dm.sub(h, (slice(None), h, slice(None))), rl[h % 2][:, 0:w], h == 0, h == 15)
                act(Isc(512 * ch, 512 * ch + w), pI[:, 0:w], AF.Copy)
            hi, lo, W, mid, cnt, gsel = (bs[:, i:i + 1] for i in range(6))
            S.add("dve", lambda e: e.tensor_reduce(out=bs.h[:, 0:1], in_=IscAP[:, 0:L], axis=AX.X, op=ALU.max),
                  reads=[Isc(0, L)], writes=[hi])
            S.add("dve", lambda e: e.tensor_reduce(out=bs.h[:, 1:2], in_=IscAP[:, 0:L], axis=AX.X, op=ALU.min),
                  reads=[Isc(0, L)], writes=[lo])
            tt(Isc(128 * G, 128 * G + 128), Isc(128 * G, 128 * G + 128), tri[:], ALU.add)
            ts(lo, lo, -1.0, None, ALU.add)
            tt(W, hi, lo, ALU.subtract)
            ts(W, W, 1.0, None, ALU.add)
            bisect(L, lo, W, mid, cnt, gsel)
            ts(mskv(0, L), Isc(0, L), lo, None, ALU.is_ge)
            for k8 in range(0, nkb, 8):
                c8 = min(8, nkb - k8)
                for kb in range(k8, k8 + c8):
                    transpose(vv(ps6b[:, (kb - k8) * 128:(kb - k8) * 128 + 128], ps[7][:]), mskv(128 * kb, 128 * kb + 128), identb[:])
                act(vv(mskT.h[:, k8:k8 + c8, :].rearrange("p a b -> p (a b)"), mskT[:]),
                    vv(ps6b[:, 0:c8 * 128], ps[7][:]), AF.Copy)
            for hh in range(8):
                g = hh // 4
                pO, pD = ps[6], ps[3]
                for c4 in range(0, nkb, 4):
                    c = min(4, nkb - c4)
                    pS = ps[4 + ((c4 // 4) % 2)]
                    pex = pe_[(c4 // 4) % 2]
                    for kb in range(c4, c4 + c):
                        mm(pS[:, (kb - c4) * 128:(kb - c4) * 128 + 128], kTs[:, g, 128 * kb:128 * kb + 128],
                           slotc(A_Q + hh, q0, q0 + 128), True, True)
                    act(pex[:, 0:c * 128], pS[:, 0:c * 128], AF.Exp, scale=float(128 ** -0.5))
                    tt(pex[:, 0:c * 128], pex[:, 0:c * 128],
                       vv(mskT.h[:, c4:c4 + c, :].rearrange("p a b -> p (a b)"), mskT[:]), ALU.mult)
                    for kb in range(c4, c4 + c):
                        mm(pO[:, 0:128], vS[:, kb, 128 * g:128 * g + 128], pex[:, (kb - c4) * 128:(kb - c4) * 128 + 128],
                           kb == 0, kb == nkb - 1)
                        mm(pD[:, 0:128], onesb[:], pex[:, (kb - c4) * 128:(kb - c4) * 128 + 128], kb == 0, kb == nkb - 1)
                recip(tA[:, 0:128], pD[:, 0:128])
                tt(slotc(A_BO + hh, q0, q0 + 128), pO[:, 0:128], tA[:, 0:128], ALU.mult)

        def attn_sample():
            LS = PAST + NS
            arf = arena.h[:].rearrange("p s n -> p (s n)").bitcast(F32)
            ARK = arena[:]

            def IS(a, b_):
                return vv(arf[0:16, a:b_], ARK)
            junk = vv(kTs.h[:].rearrange("p a n -> p (a n)")[0:16, 0:2052], kTs[:])
            kic = vv(rl[0].h[:, :].rearrange("p (t d) -> p t d", t=8), rl[0][:])
            kiTc = vv(mskT.h[:].rearrange("p a b -> p (a b)")[0:64, 0:1024], mskT[:])
            Kc = vv(kiA.h[:, :].rearrange("p (t d) -> p t d", t=8), kiA[:])
            Vc = vv(kiB.h[:, :].rearrange("p (t d) -> p t d", t=8), kiB[:])
            KTc = vv(Dm.h[:].rearrange("p a b -> p (a b)").rearrange("p (g k) -> p g k", g=2), Dm[:])
            mTc = vv(pe_[0].h[:, 0:128].rearrange("p (t q) -> p t q", t=8), pe_[0][:])
            mskc = vv(vS.h[:].rearrange("p a b -> p (a b)")[0:16, 0:1024], vS[:])
            pexv = vv(pe_[1].h[:, 0:256].rearrange("p (g t c) -> p g t c", g=2, t=8), pe_[1][:])
            qs = vv(hre.h[:, 0:32].rearrange("p (h q) -> p h q", h=8), hre[:])
            for h in range(16):
                ts(DmS[:, h, :], identf[0:16, 0:16], wtk[0:16, 0, h:h + 1], None, ALU.mult)
            hi, lo, W, mid, cnt, gsel, c2 = (bs2[:, i:i + 1] for i in range(7))
            for bi in range(4):
                for ch in range(8):
                    S.add("pool", lambda e, ch=ch, bi=bi: e.indirect_dma_start(
                        out=rl[0].h[:, :], out_offset=None, in_=cik_d[:, :],
                        in_offset=bass.IndirectOffsetOnAxis(ap=idx8.h[:, bi, ch:ch + 1], axis=0)),
                        reads=[idx8[:]], writes=[rl[0][:]], dma=True)
                    for t in range(8):
                        transpose(vv(ps6b[0:64, 128 * t:128 * t + 128], ps[7][:]), vv(kic.ap[:, t, :], kic), identb[:])
                    act(kiTc, vv(ps6b[0:64, 0:1024], ps[7][:]), AF.Copy)
                    for sc in range(2):
                        pI = ps[2]
                        for h in range(16):
                            psc = ps[h % 2]
                            mm(psc[0:16, 0:512], sQI[0:64, h, 0:16], vv(kiTc.ap[:, 512 * sc:512 * sc + 512], kiTc), True, True)
                            act(rl[1][0:16, 0:512], psc[0:16, 0:512], AF.Relu, scale=0.125)
                            mm(pI[0:16, 0:512], DmS[:, h, :], rl[1][0:16, 0:512], h == 0, h == 15)
                        c0_ = 1024 * ch + 512 * sc
                        act(IS(c0_, c0_ + 512), pI[0:16, 0:512], AF.Copy)
                pI = ps[2]
                for h in range(16):
                    psc = ps[h % 2]
                    mm(psc[0:16, 0:16], sQI[0:64, h, 0:16], kinA[0:64, 0:16], True, True)
                    act(rl[1][0:16, 0:16], psc[0:16, 0:16], AF.Relu, scale=0.125)
                    mm(pI[0:16, 0:16], DmS[:, h, :], rl[1][0:16, 0:16], h == 0, h == 15)
                S.add("dve", lambda e: e.tensor_reduce(out=bs2.h[:, 0:1], in_=arf[0:16, 0:PAST], axis=AX.X, op=ALU.max),
                      reads=[IS(0, PAST)], writes=[hi])
                S.add("dve", lambda e: e.tensor_reduce(out=bs2.h[:, 1:2], in_=arf[0:16, 0:PAST], axis=AX.X, op=ALU.min),
                      reads=[IS(0, PAST)], writes=[lo])
                tt(IS(PAST, LS), pI[0:16, 0:16], mnew[:], ALU.add)
                ts(lo, lo, -64.0, None, ALU.add)
                ts(hi, hi, 64.0, None, ALU.add)
                tt(W, hi, lo, ALU.subtract)
                for it in range(NIT + 4):
                    scv = float(2.0 ** -(it + 1))
                    stt(mid, W, scv, lo, ALU.mult, ALU.add)
                    for q4 in range(4):
                        ts(junk, IS(2052 * q4, 2052 * q4 + 2052), mid, None, ALU.is_ge, ALU.add, accum=(cnt if q4 == 0 else c2))
                        if q4 > 0:
                            tt(cnt, cnt, c2, ALU.add)
                    ts(gsel, cnt, TOPK - 0.5, scv, ALU.is_ge, ALU.mult)
                    stt(lo, gsel, W, lo, ALU.mult, ALU.add)
                for hh in range(8):
                    cp(vv(qs.ap[:, hh, :], qs), slotc_s(A_Q + hh, 4 * bi, 4 * bi + 4))
                pO = [ps[2], ps[3]]
                pD = [ps[4], ps[5]]
                nblk_tot = 65
                for ch in range(9):
                    if ch < 8:
                        for (dst, src_d) in ((kiA, ck_d), (kiB, cv_d)):
                            S.add("pool", lambda e, ch=ch, bi=bi, dst=dst, src_d=src_d: e.indirect_dma_start(
                                out=dst.h[:, :], out_offset=None, in_=src_d[:, :],
                                in_offset=bass.IndirectOffsetOnAxis(ap=idx8.h[:, bi, ch:ch + 1], axis=0)),
                                reads=[idx8[:]], writes=[dst[:]], dma=True)
                        ts(mskc, IS(1024 * ch, 1024 * ch + 1024), lo, None, ALU.is_ge)
                        for t in range(8):
                            transpose(vv(ps6b[:, 16 * t:16 * t + 16], ps[7][:]), vv(mskc.ap[:, 128 * t:128 * t + 128], mskc), identb[0:16, 0:16])
                        act(vv(pe_[0].h[:, 0:128], pe_[0][:]), vv(ps6b[:, 0:128], ps[7][:]), AF.Copy)
                        for g in range(2):
                            for t in range(8):
                                transpose(vv(ps6b[:, 128 * t:128 * t + 128], ps[7][:]), vv(Kc.ap[:, t, 128 * g:128 * g + 128], Kc), identb[:])
                            act(vv(KTc.ap[:, g, :], KTc), vv(ps6b[:, 0:1024], ps[7][:]), AF.Copy)
                        pS = ps[0]
                        for g in range(2):
                            for t in range(8):
                                mm(pS[:, 128 * g + 16 * t:128 * g + 16 * t + 16], vv(KTc.ap[:, g, 128 * t:128 * t + 128], KTc),
                                   vv(qs.ap[:, 4 * g:4 * g + 4, :].rearrange("p h q -> p (h q)"), qs), True, True)
                        act(vv(pe_[1].h[:, 0:256], pe_[1][:]), pS[:, 0:256], AF.Exp, scale=float(128 ** -0.5))
                        for g in range(2):
                            pg_ = vv(pexv.ap[:, g, :, :].rearrange("p t (h q) -> p t h q", h=4), pexv)
                            mb_ = vv(mTc.ap[:, :, 4 * bi:4 * bi + 4].unsqueeze(2).to_broadcast([128, 8, 4, 4]), mTc)
                            tt(pg_, pg_, mb_, ALU.mult)
                        for g in range(2):
                            for t in range(8):
                                kb = 8 * ch + t
                                rhs_ = vv(pexv.ap[:, g, t, :], pexv)
                                mm(pO[g][:, 0:16], vv(Vc.ap[:, t, 128 * g:128 * g + 128], Vc), rhs_, kb == 0, False)
                                mm(pD[g][:, 0:16], onesb[:], rhs_, kb == 0, False)
                    else:
                        ts(vv(mskc.ap[:, 0:16], mskc), IS(PAST, LS), lo, None, ALU.is_ge)
                        transpose(vv(ps6b[0:16, 0:16], ps[7][:]), vv(mskc.ap[:, 0:16], mskc), identb[0:16, 0:16])
                        act(vv(pe_[0].h[0:16, 0:16], pe_[0][:]), vv(ps6b[0:16, 0:16], ps[7][:]), AF.Copy)
                        pS = ps[0]
                        for g in range(2):
                            mm(pS[0:16, 16 * g:16 * g + 16], knew[:, g, 0:16],
                               vv(qs.ap[:, 4 * g:4 * g + 4, :].rearrange("p h q -> p (h q)"), qs), True, True)
                        act(vv(pe_[1].h[0:16, 0:32], pe_[1][:]), pS[0:16, 0:32], AF.Exp, scale=float(128 ** -0.5))
                        for g in range(2):
                            pg_ = vv(pe_[1].h[0:16, 16 * g:16 * g + 16].rearrange("p (h q) -> p h q", h=4), pe_[1][:])
                            mb_ = vv(pe_[0].h[0:16, 4 * bi:4 * bi + 4].unsqueeze(1).to_broadcast([16, 4, 4]), pe_[0][:])
                            tt(pg_, pg_, mb_, ALU.mult)
                        for g in range(2):
                            rhs_ = vv(pe_[1].h[0:16, 16 * g:16 * g + 16], pe_[1][:])
                            mm(pO[g][:, 0:16], vnew[0:16, 128 * g:128 * g + 128], rhs_, False, True)
                            mm(pD[g][:, 0:16], onesb[0:16, :], rhs_, False, True)
                for g in range(2):
                    recip(tA[:, 0:16], pD[g][:, 0:16])
                    tt(tB[:, 0:16], pO[g][:, 0:16], tA[:, 0:16], ALU.mult)
                    outv = sA.multi(range(A_BO + 4 * g, A_BO + 4 * g + 4),
                                    (slice(None), slice(A_BO + 4 * g, A_BO + 4 * g + 4), slice(4 * bi, 4 * bi + 4)))
                    cp(outv, vv(tB.h[:, 0:16].rearrange("p (h q) -> p h q", h=4), tB[:]))

        def slotc_s(i, a, b_):
            return sA.sub(i, (slice(None), i, slice(a, b_)))

        def ssm_prompt(n):
            slB = wload([(0, 128, 32, Bscr_d[:, 0, :, :]), (4096, 128, 32, Bscr_d[:, 1, :, :])], extra_reads=[BSCR])
            slC = wload([(0, 128, 32, cre_d.rearrange("p (j m) -> p j m", j=32)),
                         (4096, 128, 32, cim_d.rearrange("p (j m) -> p j m", j=32))])
            for c in range(8):
                py = ps[4 + (c % 2)]
                for jj in range(4):
                    j = 4 * c + jj
                    pre, pim = ps[2 * (j % 2)], ps[2 * (j % 2) + 1]
                    ub = slot(A_U + c, n)
                    mm(pre[:, 0:n], wview(slB, 0, 128, j, 0, 128), ub, True, True)
                    mm(pim[:, 0:n], wview(slB, 4096, 128, j, 0, 128), ub, True, True)
                    fj = sfrq[:, j:j + 1]
                    ts(tA[:, 0:n], iota1[:, 0:n], fj, None, ALU.mult)
                    cp(tI[:, 0:n], tA[:, 0:n])
                    stt(tB[:, 0:n], tI[:, 0:n], -1.0, tA[:, 0:n], ALU.mult, ALU.add)
                    act(sinT[:, 0:n], tB[:, 0:n], AF.Sin, scale=float(2 * np.pi))
                    ts(tA[:, 0:n], tA[:, 0:n], 0.25, None, ALU.add)
                    cp(tI[:, 0:n], tA[:, 0:n])
                    stt(tB[:, 0:n], tI[:, 0:n], -1.0, tA[:, 0:n], ALU.mult, ALU.add)
                    act(cosT[:, 0:n], tB[:, 0:n], AF.Sin, scale=float(2 * np.pi))
                    tt(tA[:, 0:n], cosT[:, 0:n], pre[:, 0:n], ALU.mult)
                    tt(tB[:, 0:n], sinT[:, 0:n], pim[:, 0:n], ALU.mult)
                    tt(tC[:, 0:n], tA[:, 0:n], tB[:, 0:n], ALU.add)
                    tt(tA[:, 0:n], cosT[:, 0:n], pim[:, 0:n], ALU.mult)
                    tt(tB[:, 0:n], sinT[:, 0:n], pre[:, 0:n], ALU.mult)
                    tt(tD[:, 0:n], tA[:, 0:n], tB[:, 0:n], ALU.subtract)
                    dec = vv(sdec.h[:, j:j + 1].to_broadcast([128, n]), sdec[:])
                    for (cc, qq, ci) in ((tC, tE, 0), (tD, tF, 1)):
                        init = car[:, ci, j:j + 1]
                        S.add("dve", lambda e, cc=cc, qq=qq, init=init, dec=dec: e.tensor_tensor_scan(
                            out=qq.h[:, 0:n], data0=dec.ap, data1=cc.h[:, 0:n], initial=init.ap,
                            op0=ALU.mult, op1=ALU.add), reads=[cc[:], init, dec], writes=[qq[:]])
                    tt(tA[:, 0:n], cosT[:, 0:n], tE[:, 0:n], ALU.mult)
                    tt(tB[:, 0:n], sinT[:, 0:n], tF[:, 0:n], ALU.mult)
                    tt(tC[:, 0:n], tA[:, 0:n], tB[:, 0:n], ALU.subtract)
                    tt(tA[:, 0:n], sinT[:, 0:n], tE[:, 0:n], ALU.mult)
                    tt(tB[:, 0:n], cosT[:, 0:n], tF[:, 0:n], ALU.mult)
                    tt(tD[:, 0:n], tA[:, 0:n], tB[:, 0:n], ALU.add)
                    act(hre[:, 0:n], tC[:, 0:n], AF.Copy)
                    act(him[:, 0:n], tD[:, 0:n], AF.Copy, scale=-1.0)
                    cp(car[:, 0, j:j + 1], tC[:, n - 1:n])
                    cp(car[:, 1, j:j + 1], tD[:, n - 1:n])
                    mm(py[:, 0:n], wview(slC, 0, 128, j, 0, 128), hre[:, 0:n], jj == 0, False)
                    mm(py[:, 0:n], wview(slC, 4096, 128, j, 0, 128), him[:, 0:n], False, False)
                mm(py[:, 0:n], Dd.sub(c, (slice(None), c, slice(None))), slot(A_U + c, n), False, True)
                gelu_to(py, slot(A_ZA + c, n), n)

        def gelu_to(py, out, n):
            act(tA[:, 0:n], py[:, 0:n], AF.Square)
            ts(tA[:, 0:n], tA[:, 0:n], 0.044715, 1.0, ALU.mult, ALU.add)
            tt(tA[:, 0:n], tA[:, 0:n], py[:, 0:n], ALU.mult)
            act(tB[:, 0:n], tA[:, 0:n], AF.Sigmoid, scale=1.5957691216057308)
            tt(out, tB[:, 0:n], py[:, 0:n], ALU.mult)

        def ssm_sample():
            n = NS
            slB = wload([(0, 128, 32, Bscr_d[:, 0, :, :]), (4096, 128, 32, Bscr_d[:, 1, :, :])], extra_reads=[BSCR])
            slC = wload([(0, 128, 32, cre_d.rearrange("p (j m) -> p j m", j=32)),
                         (4096, 128, 32, cim_d.rearrange("p (j m) -> p j m", j=32))])
            for j in range(32):
                c = j // 4
                pre, pim = ps[2 * (j % 2)], ps[2 * (j % 2) + 1]
                mm(pre[:, 0:n], wview(slB, 0, 128, j, 0, 128), slot(A_U + c, n), True, True)
                mm(pim[:, 0:n], wview(slB, 4096, 128, j, 0, 128), slot(A_U + c, n), True, True)
                cp(tE[:, 16 * j:16 * j + 16], pre[:, 0:n])
                act(tF[:, 16 * j:16 * j + 16], pim[:, 0:n], AF.Copy)
            cp(hss[:], h0s[:])

            def v4(tX, t):
                return vv(tX.h[:].rearrange("p (j b t) -> p j b t", j=32, b=4)[:, :, :, t], tX[:])

            def v3(tX, a):
                return vv(tX.h[:, a:a + 128].rearrange("p (j b) -> p j b", j=32), tX[:])
            lr = vv(lbre.h[:, :].unsqueeze(2).to_broadcast([128, 32, 4]), lbre[:])
            li = vv(lbim.h[:, :].unsqueeze(2).to_broadcast([128, 32, 4]), lbim[:])
            A1, A2, A3, A4 = v3(tA, 0), v3(tA, 128), v3(tB, 0), v3(tB, 128)
            for t in range(4):
                hr = hss[:, 0, :, :]
                hi_ = hss[:, 1, :, :]
                tt(A1, hr, lr, ALU.mult)
                tt(A2, hi_, li, ALU.mult)
                tt(A1, A1, A2, ALU.subtract)
                tt(A3, hi_, lr, ALU.mult)
                tt(A4, hr, li, ALU.mult)
                tt(A3, A3, A4, ALU.add)
                tt(hr, A1, v4(tE, t), ALU.add)
                tt(hi_, A3, v4(tF, t), ALU.add)
                cp(v4(tC, t), hr)
                cp(v4(tD, t), hi_)
            act(hre[:], tC[:], AF.Copy)
            act(him[:], tD[:], AF.Copy, scale=-1.0)
            for c in range(8):
                py = ps[4 + (c % 2)]
                for jj in range(4):
                    j = 4 * c + jj
                    mm(py[:, 0:n], wview(slC, 0, 128, j, 0, 128), hre[:, 16 * j:16 * j + 16], jj == 0, False)
                    mm(py[:, 0:n], wview(slC, 4096, 128, j, 0, 128), him[:, 16 * j:16 * j + 16], False, False)
                mm(py[:, 0:n], Dd.sub(c, (slice(None), c, slice(None))), slot(A_U + c, n), False, True)
                gelu_to(py, slot(A_ZA + c, n), n)

        knew = sb("knew", [128, 2, NS], BF16)
        kinA = sb("kinA", [128, NS], BF16); kinB = sb("kinB", [128, NS], BF16)
        vnew = sb("vnew", [NS, 256], BF16)

        for blk in range(NBLK + 1):
            n = NB if blk < NBLK else NS
            cur["sample"] = (blk == NBLK)
            wblk["blk"] = blk
            wblk["k"] = 0
            c0 = blk * NB
            for j in range(NKC):
                dma("sp", xT.sub(j, (slice(None), j, slice(0, n))), xT_d[128 * j:128 * j + 128, c0:c0 + n],
                    writes=[xT.sub(j, (slice(None), j, slice(0, n)))])
            if stage_limit >= 1:
                rmsnorm(n, 0)
                ffn(n, w1g_d, w1u_d, w1d_d)
            if stage_limit >= 2:
                full = mix(blk, n, c0)
                if full and stage_limit >= 7:
                    rmsnorm(n, 32)
                    ffn(n, w2g_d, w2u_d, w2d_d)
            for j in range(NKC):
                dma("sp", yT_o[128 * j:128 * j + 128, c0:c0 + n], xT.sub(j, (slice(None), j, slice(0, n))),
                    reads=[xT.sub(j, (slice(None), j, slice(0, n)))])
        dma("sp", sp_o, vv(car.h[:].rearrange("p a j -> p (a j)"), car[:]), reads=[car[:]])
        dma("sp", ss_o, vv(hss.h[:].rearrange("p a j b -> p (a j b)"), hss[:]), reads=[hss[:]])

        S.emit_all(nc)
    return nc


def _rope_tables(pos):
    pos = np.asarray(pos, np.float32)
    T = pos.shape[0]
    out = np.zeros((4, 128, T), np.float32)
    out[0] = 1.0
    out[2] = 1.0
    half = 16
    fr = (np.float32(500000.0) ** (-np.arange(half, dtype=np.float32) * np.float32(2.0) / np.float32(32))).astype(np.float32)
    ang = (pos[:, None] * fr[None, :]).astype(np.float32)
    c, s = np.cos(ang).T.astype(np.float32), np.sin(ang).T.astype(np.float32)
    out[0, 0:16] = c; out[0, 16:32] = c
    out[1, 0:16] = -s; out[1, 16:32] = s
    half = 8
    fr = (np.float32(500000.0) ** (-np.arange(half, dtype=np.float32) * np.float32(2.0) / np.float32(16))).astype(np.float32)
    ang = (pos[:, None] * fr[None, :]).astype(np.float32)
    c, s = np.cos(ang).T.astype(np.float32), np.sin(ang).T.astype(np.float32)
    for b in (0, 64):
        out[2, b:b + 8] = c; out[2, b + 8:b + 16] = c
        out[3, b:b + 8] = -s; out[3, b + 8:b + 16] = s
    return out


def _perm_cols(w, width, half):
    n = w.shape[1]
    idx = np.arange(n)
    loc = idx % width
    src = idx.copy()
    src[loc < half] += half
    m = (loc >= half) & (loc < 2 * half)
    src[m] -= half
    return w[:, src]


_NC_CACHE = {}


def kernel(x_prompt, x_sample, cache_k, cache_v, cache_idx_k, state_ssm_re, state_ssm_im, page_table,
           ffn1_norm, ffn1_w_gate, ffn1_w_up, ffn1_w_down, mix_norm, w_in, q_norm, k_norm,
           ssm_lambda_re, ssm_lambda_im, ssm_b_re, ssm_b_im, ssm_c_re, ssm_c_im, ssm_d, ssm_log_dt,
           glu_w, glu_b, w_branch_a, w_branch_b, w_out, ffn2_norm, ffn2_w_gate, ffn2_w_up, ffn2_w_down):
    import os
    stage_limit = int(os.environ.get("KSTAGE", "99"))
    f32 = np.float32
    A = lambda a: np.ascontiguousarray(np.asarray(a), dtype=f32)
    win = A(w_in)[0]
    u_w = win[:, 0:1024]; q_w = win[:, 1024:2048]; k_w = win[:, 2048:2304]; v_w = win[:, 2304:2560]
    qi_w = win[:, 2560:3584]; ki_w = win[:, 3584:3648]; wi_w = win[:, 3648:3664]
    q_r = _perm_cols(q_w, 128, 16); k_r = _perm_cols(k_w, 128, 16)
    qi_r = _perm_cols(qi_w, 64, 8); ki_r = _perm_cols(ki_w, 64, 8)
    tiles = [u_w[:, 128 * i:128 * i + 128] for i in range(8)]
    for h in range(8):
        tiles += [q_w[:, 128 * h:128 * h + 128], q_r[:, 128 * h:128 * h + 128]]
    for g in range(2):
        tiles += [k_w[:, 128 * g:128 * g + 128], k_r[:, 128 * g:128 * g + 128]]
    for t in range(8):
        tiles += [qi_w[:, 128 * t:128 * t + 128], qi_r[:, 128 * t:128 * t + 128]]
    tiles += [np.concatenate([ki_w, ki_w], 1), np.concatenate([ki_r, ki_r], 1)]
    wmix = np.ascontiguousarray(np.concatenate(tiles, 1))
    assert wmix.shape[1] == WMIX_TILES * 128
    wtok = np.ascontiguousarray(np.concatenate([v_w, wi_w, np.zeros((D, 512 - 272), f32)], 1))
    wgab = np.ascontiguousarray(win[:, 3664:7760])
    pl = lambda g: np.ascontiguousarray(A(g).reshape(-1, 128).T)
    nrm = np.concatenate([pl(ffn1_norm[0]), pl(mix_norm[0]), pl(ffn2_norm[0])], 1)
    qn = A(q_norm)[0]; kn = A(k_norm)[0]
    qkn = np.stack([qn, _perm_cols(qn[None], 128, 16)[0], kn, _perm_cols(kn[None], 128, 16)[0]], 1)
    glb = pl(glu_b[0])
    ident = np.eye(128, dtype=f32)
    tri = np.where(np.arange(128)[None, :] <= np.arange(128)[:, None], 0.0, NEG).astype(f32)
    iota1 = np.broadcast_to(np.arange(1, NB + 1, dtype=f32)[None, :], (128, NB)).copy()
    def sl_(a):
        return np.ascontiguousarray(A(a).reshape(32, 2, 64).transpose(1, 2, 0).reshape(128, 32))
    ldt = np.repeat(A(ssm_log_dt)[0][:, None], 64, 1)
    ssmp = np.concatenate([sl_(ssm_lambda_re[0]), sl_(ssm_lambda_im[0]), sl_(ldt)], 1)
    ssmd = np.ascontiguousarray(A(ssm_d)[0].reshape(8, 128).T)
    def bs_(b):
        b = A(b).reshape(32, 2, 64, 16)
        o = np.zeros((2, 64, 32, 128), f32)
        for j in range(32):
            for g2 in range(2):
                k0 = 32 * (j % 4) + 16 * g2
                o[g2, :, j, k0:k0 + 16] = b[j, g2]
        return np.ascontiguousarray(o.reshape(128, 32 * 128))
    def cs_(c):
        c = A(c).reshape(32, 2, 16, 64)
        o = np.zeros((2, 64, 32, 128), f32)
        for j in range(32):
            for g2 in range(2):
                m0 = 32 * (j % 4) + 16 * g2
                o[g2, :, j, m0:m0 + 16] = c[j, g2].T
        return np.ascontiguousarray(o.reshape(128, 32 * 128))
    bsre = bs_(ssm_b_re[0]); bsim = bs_(ssm_b_im[0]); cre = cs_(ssm_c_re[0]); cim = cs_(ssm_c_im[0])

    xp = A(x_prompt); xs = A(x_sample)
    sre = A(state_ssm_re)[0]; sim = A(state_ssm_im)[0]
    shared = dict(w1g=A(ffn1_w_gate)[0], w1u=A(ffn1_w_up)[0], w1d=A(ffn1_w_down)[0],
                  w2g=A(ffn2_w_gate)[0], w2u=A(ffn2_w_up)[0], w2d=A(ffn2_w_down)[0],
                  wmix=wmix, wtok=wtok, wgab=wgab, glw=A(glu_w)[0], wa=A(w_branch_a)[0], wb=A(w_branch_b)[0],
                  wo=A(w_out)[0], nrm=np.ascontiguousarray(nrm), qkn=np.ascontiguousarray(qkn), glb=glb,
                  ident=ident, tri=tri, iota1=iota1, hmask=np.stack([(np.arange(128) < 64), (np.arange(128) >= 64)], 1).astype(f32), ssmp=np.ascontiguousarray(ssmp), ssmd=ssmd,
                  bsre=bsre, bsim=bsim, cre=cre, cim=cim)
    ckh = A(cache_k)[0].reshape(40960, 2048); cvh = A(cache_v)[0].reshape(40960, 2048)
    cikh = A(cache_idx_k)[0].reshape(40960, 512)
    shared.update(ckh=ckh, cvh=cvh, cikh=cikh)
    tq = np.arange(16)
    mnew = np.where((tq[None, :] // 4 == tq[:, None] // 4) & (tq[None, :] % 4 <= tq[:, None] % 4), 0.0, NEG).astype(f32)
    shared.update(mnew=mnew)
    pt_np = np.asarray(page_table).astype(np.int32)
    pos_s = np.tile(PAST + np.arange(4), 4)
    rope = _rope_tables(np.concatenate([np.arange(SEQ), pos_s]))
    in_maps = []
    for c in range(N_CORES):
        pb = c // 2
        sbs = slice(4 * c, 4 * c + 4)
        xT = np.concatenate([xp[pb].T, xs[sbs].reshape(NS, D).T], 1)
        def h0l(s):
            return s.reshape(4, 32, 2, 64).transpose(2, 3, 1, 0).reshape(128, 32 * 4)
        h0 = np.concatenate([h0l(sre[sbs]), h0l(sim[sbs])], 1)
        m = dict(shared)
        ptl_c = np.ascontiguousarray(np.concatenate([pt_np[sbs].T, pt_np[sbs].T], 0).astype(np.int32))
        m.update(xT=np.ascontiguousarray(xT), rope=rope, h0=np.ascontiguousarray(h0), ptl=ptl_c)
        in_maps.append(m)

    import time as _t
    _t0 = _t.time()
    key = stage_limit
    if key not in _NC_CACHE:
        _NC_CACHE[key] = build_nc(stage_limit)
    nc = _NC_CACHE[key]
    _t1 = _t.time()
    res = run_bass_kernel_spmd(nc, in_maps, core_ids=list(range(N_CORES)))
    if os.environ.get("KTIME"):
        print("[kernel] build %.1fs run %.1fs" % (_t1 - _t0, _t.time() - _t1), flush=True)
    R = res.results
    y_p = np.stack([R[2 * b]["yT"][:, :SEQ].T for b in range(4)])
    y_s = np.concatenate([R[c]["yT"][:, SEQ:].T.reshape(4, 4, D) for c in range(N_CORES)])
    k_p = np.stack([R[2 * b]["kTo"][:, :SEQ].T.reshape(SEQ, 2, 128) for b in range(4)])[None]
    v_p = np.stack([R[2 * b]["vTo"][:, :SEQ].T.reshape(SEQ, 2, 128) for b in range(4)])[None]
    ik_p = np.stack([R[2 * b]["kiTo"][:64, :SEQ].T for b in range(4)])[None]
    def st_p(r, a):
        return r["ssp"][:, 32 * a:32 * a + 32].reshape(2, 64, 32).transpose(2, 0, 1).reshape(64, 64)
    re_p = np.stack([st_p(R[2 * b], 0) for b in range(4)])[None]
    im_p = np.stack([st_p(R[2 * b], 1) for b in range(4)])[None]
    k_s = np.concatenate([R[c]["kTo"][:, SEQ:].T.reshape(4, 4, 2, 128) for c in range(N_CORES)])[None]
    v_s = np.concatenate([R[c]["vTo"][:, SEQ:].T.reshape(4, 4, 2, 128) for c in range(N_CORES)])[None]
    ik_s = np.concatenate([R[c]["kiTo"][:64, SEQ:].T.reshape(4, 4, 64) for c in range(N_CORES)])[None]
    def st_s(r, a):
        return r["sss"][:, 128 * a:128 * a + 128].reshape(2, 64, 32, 4).transpose(3, 2, 0, 1).reshape(4, 64, 64)
    re_s = np.concatenate([st_s(R[c], 0) for c in range(N_CORES)])[None]
    im_s = np.concatenate([st_s(R[c], 1) for c in range(N_CORES)])[None]
    outs = (y_p, y_s, k_p, v_p, ik_p, re_p, im_p, k_s, v_s, ik_s, re_s, im_s)
    return tuple(np.ascontiguousarray(o, dtype=np.float32) for o in outs)
```

```python
import contextlib
import numpy as np
import ml_dtypes
import concourse.bass as bass
import concourse.mybir as mybir
from concourse.bass_utils import run_bass_kernel_spmd

F32 = mybir.dt.float32
BF16 = mybir.dt.bfloat16
I32 = mybir.dt.int32
ALU = mybir.AluOpType
AF = mybir.ActivationFunctionType
AX = mybir.AxisListType

D = 2048
DFF = 5632
NKC = D // 128
NFF = DFF // 128
SEQ = 2048
NB = 512
NBLK = SEQ // NB
NS = 16
NTOK = SEQ + NS
EPS = 1e-6
NEG = -1.0e30
TOPK = 256
PAST = 8192
NPAGE = 64
N_CORES = 8


class V:
    __slots__ = ("ap", "keys")

    def __init__(self, ap, keys):
        self.ap = ap
        self.keys = keys


class Buf:
    def __init__(self, handle, name):
        self.h = handle
        self.name = name

    def __getitem__(self, idx):
        return V(self.h[idx], ((self.name, None),))

    def sub(self, k, idx):
        return V(self.h[idx], ((self.name, k),))

    def multi(self, ks, idx):
        return V(self.h[idx], tuple((self.name, k) for k in ks))


def vv(ap, *views):
    keys = tuple(k for v in views for k in v.keys)
    return V(ap, keys)


class Op:
    __slots__ = ("idx", "eng", "emit", "dma", "deps", "inc", "cnt", "sem", "val", "prev_val")

    def __init__(self, idx, eng, emit, dma):
        self.idx = idx
        self.eng = eng
        self.emit = emit
        self.dma = dma
        self.deps = set()
        self.inc = False
        self.cnt = 0
        self.sem = None
        self.val = 0
        self.prev_val = 0


class Sched:
    def __init__(self):
        self.ops = []
        self.last_writer = {}
        self.readers = {}

    @staticmethod
    def _conf(d, name, sub):
        e = d.get(name)
        if not e:
            return []
        if sub is None:
            return list(e.values())
        out = []
        if sub in e:
            out.append(e[sub])
        if None in e:
            out.append(e[None])
        return out

    def add(self, eng, emit, reads=(), writes=(), dma=False):
        op = Op(len(self.ops), eng, emit, dma)
        rk = [k for v in reads if isinstance(v, V) for k in v.keys]
        wk = [k for v in writes if isinstance(v, V) for k in v.keys]
        for (name, sub) in rk:
            for w in self._conf(self.last_writer, name, sub):
                op.deps.add(w)
            if name.startswith("ps"):
                for rs in self._conf(self.readers, name, sub):
                    op.deps |= {r for r in rs if r.eng != eng}
        for (name, sub) in wk:
            for w in self._conf(self.last_writer, name, sub):
                op.deps.add(w)
            for rs in self._conf(self.readers, name, sub):
                op.deps |= rs
        for (name, sub) in rk:
            self.readers.setdefault(name, {}).setdefault(sub, set()).add(op)
        for (name, sub) in wk:
            lw = self.last_writer.setdefault(name, {})
            rd = self.readers.setdefault(name, {})
            if sub is None:
                lw.clear()
                rd.clear()
            lw[sub] = op
            rd[sub] = set()
        op.deps.discard(op)
        self.ops.append(op)
        return op

    def emit_all(self, nc, nsem_dma=14):
        engs = ["pe", "act", "dve", "pool", "sp"]
        per = {e: [] for e in engs}
        for op in self.ops:
            per[op.eng].append(op)
        for op in self.ops:
            for d in op.deps:
                if d.dma:
                    continue
                if d.eng == "pe" and op.eng == "pe" and not op.dma:
                    continue
                d.inc = True
        cnt = {e: 0 for e in engs}
        for op in self.ops:
            if not op.dma and op.inc:
                cnt[op.eng] += 1
                op.cnt = cnt[op.eng]
        with contextlib.ExitStack() as st:
            csem = {e: st.enter_context(nc.semaphore("c_" + e)) for e in ["pe", "act", "dve", "pool"]}
            dsem = {e: [st.enter_context(nc.semaphore("d_%s%d" % (e, i))) for i in range(nsem_dma)]
                    for e in ["pool", "sp"]}
            duse = {e: [0] * nsem_dma for e in dsem}
            dnext = {e: 0 for e in dsem}
            for op in self.ops:
                if op.dma:
                    i = dnext[op.eng]
                    dnext[op.eng] = (i + 1) % nsem_dma
                    op.sem = dsem[op.eng][i]
                    op.prev_val = duse[op.eng][i]
                    duse[op.eng][i] += 16
                    op.val = duse[op.eng][i]
            block = st.enter_context(nc.Block())

            def run(engname, e):
                waited = {}

                def w(sem, val):
                    if val <= 0 or waited.get(sem.name, 0) >= val:
                        return
                    waited[sem.name] = val
                    e.wait_ge(sem, val)

                for op in per[engname]:
                    for d in sorted(op.deps, key=lambda o: o.idx):
                        if d.dma:
                            w(d.sem, d.val)
                        else:
                            if d.eng == "pe" and engname == "pe" and not op.dma:
                                continue
                            w(csem[d.eng], d.cnt)
                    if op.dma:
                        w(op.sem, op.prev_val)
                        ins = op.emit(e)
                        ins.then_inc(op.sem, 16)
                    else:
                        ins = op.emit(e)
                        if op.inc:
                            ins.then_inc(csem[engname], 1)
                if engname in dsem:
                    for i, s in enumerate(dsem[engname]):
                        w(s, duse[engname][i])

            @block.tensor
            def _(e):
                run("pe", e)

            @block.scalar
            def _(e):
                run("act", e)

            @block.vector
            def _(e):
                run("dve", e)

            @block.gpsimd
            def _(e):
                run("pool", e)

            @block.sync
            def _(e):
                run("sp", e)


WMIX_TILES = 46


def build_nc(stage_limit=99):
    import os as _os
    KSUB = int(_os.environ.get("KSUB", "99"))
    KATT = int(_os.environ.get("KATT", "1"))
    KV = int(_os.environ.get("KV", "3"))
    nc = bass.Bass("TRN2", target_bir_lowering=False)

    def din(name, shape, dt=F32):
        return nc.dram_tensor(name, list(shape), dt, kind="ExternalInput").ap()

    def dout(name, shape, dt=F32):
        return nc.dram_tensor(name, list(shape), dt, kind="ExternalOutput").ap()

    xT_d = din("xT", [D, NTOK])
    w1g_d = din("w1g", [D, DFF]); w1u_d = din("w1u", [D, DFF]); w1d_d = din("w1d", [DFF, D])
    w2g_d = din("w2g", [D, DFF]); w2u_d = din("w2u", [D, DFF]); w2d_d = din("w2d", [DFF, D])
    wmix_d = din("wmix", [D, WMIX_TILES * 128])
    wtok_d = din("wtok", [D, 512])
    wgab_d = din("wgab", [D, 4096])
    glw_d = din("glw", [1024, 1024]); wa_d = din("wa", [1024, D]); wb_d = din("wb", [1024, D]); wo_d = din("wo", [D, D])
    nrm_d = din("nrm", [128, 3 * 16])
    qkn_d = din("qkn", [128, 4])
    glb_d = din("glb", [128, 8])
    rope_d = din("rope", [4, 128, NTOK])
    ident_d = din("ident", [128, 128]); tri_d = din("tri", [128, 128]); iota_d = din("iota1", [128, NB])
    ssmp_d = din("ssmp", [128, 3 * 32])
    ssmd_d = din("ssmd", [128, 8])
    bsre_d = din("bsre", [128, 32 * 128]); bsim_d = din("bsim", [128, 32 * 128])
    cre_d = din("cre", [128, 32 * 128]); cim_d = din("cim", [128, 32 * 128])
    h0_d = din("h0", [128, 2 * 32 * 4])
    hmask_d = din("hmask", [128, 2])
    ck_d = din("ckh", [40960, 2048]); cv_d = din("cvh", [40960, 2048]); cik_d = din("cikh", [40960, 512])
    ptl_d = din("ptl", [128, 4], I32)
    mnew_d = din("mnew", [16, 16])

    yT_o = dout("yT", [D, NTOK])
    kT_o = dout("kTo", [256, NTOK])
    vT_o = dout("vTo", [256, NTOK])
    kiT_o = dout("kiTo", [128, NTOK])
    sp_o = dout("ssp", [128, 2 * 32])
    ss_o = dout("sss", [128, 2 * 32 * 4])

    S = Sched()
    with contextlib.ExitStack() as st:
        def sb(name, shape, dt):
            return Buf(st.enter_context(nc.sbuf_tensor("s_" + name, list(shape), dt)), name)

        def psb(name, shape, dt):
            return Buf(st.enter_context(nc.psum_tensor("p_" + name, list(shape), dt)), name)

        xT = sb("xTs", [128, NKC, NB], F32)
        hT = sb("hT", [128, NKC, NB], BF16)
        arena = sb("arena", [128, NFF, NB], BF16)
        wsl = [sb("wsl%d" % i, [128, 8192], BF16) for i in range(2)]
        ps = [psb("ps%d" % i, [128, 512], F32) for i in range(8)]
        kTs = sb("kTs", [128, 2, SEQ], BF16)
        vS = sb("vS", [128, 16, 256], BF16)
        kiA = sb("kiA", [128, SEQ], BF16)
        kiB = sb("kiB", [128, SEQ], BF16)
        Dd = sb("Dd", [128, 8, 128], BF16)
        identb = sb("identb", [128, 128], BF16); identf = sb("identf", [128, 128], F32)
        onesb = sb("onesb", [128, 128], BF16)
        tri = sb("tri", [128, 128], F32)
        hmask = sb("hmask", [128, 2], F32)
        iota1 = sb("iota1s", [128, NB], F32)
        nrm = sb("nrm", [128, 48], F32); qkn = sb("qkn", [128, 4], F32); glb = sb("glb", [128, 8], F32)
        ssmp = sb("ssmp", [128, 96], F32); ssmd = sb("ssmd", [128, 8], F32)
        sdec = sb("sdec", [128, 32], F32)
        sfrq = sb("sfrq", [128, 32], F32)
        lbre = sb("lbre", [128, 32], F32); lbim = sb("lbim", [128, 32], F32)
        gre = sb("gre", [128, 32], F32); gim = sb("gim", [128, 32], F32)
        stp = sb("stp", [128, 8, 32], F32)
        car = sb("car", [128, 2, 32], F32)
        h0s = sb("h0s", [128, 2, 32, 4], F32)
        hss = sb("hss", [128, 2, 32, 4], F32)
        sq = sb("sq", [128, 4, NB], BF16)
        rstd = sb("rstd", [128, NB], F32)
        tA = sb("tA", [128, NB], F32); tB = sb("tB", [128, NB], F32); tC = sb("tC", [128, NB], F32)
        tD = sb("tD", [128, NB], F32); tE = sb("tE", [128, NB], F32); tF = sb("tF", [128, NB], F32)
        tI = sb("tI", [128, NB], I32)
        cosT = sb("cosT", [128, NB], F32); sinT = sb("sinT", [128, NB], F32)
        hre = sb("hre", [128, NB], BF16); him = sb("him", [128, NB], BF16)
        mskT = sb("mskT", [128, 16, 128], BF16)
        rl = [sb("rl%d" % i, [128, NB], BF16) for i in range(2)]
        pe_ = [sb("pe%d" % i, [128, NB], BF16) for i in range(2)]
        Dm = sb("Dm", [128, 16, 128], BF16)
        wtk = sb("wtk", [128, 4, 16], F32)
        bs = sb("bs", [128, 8], F32)
        sA = sb("sA", [128, NFF, NS], BF16)
        sQI = sb("sQI", [128, 16, NS], BF16)
        DmS = sb("DmS", [16, 16, 16], BF16)
        ptl = sb("ptl", [128, 4], I32); idxh = sb("idxh", [128, 4], I32); idx8 = sb("idx8", [128, 4, 8], I32)
        mnew = sb("mnew", [16, 16], F32)
        bs2 = sb("bs2", [16, 8], F32)
        kst = sb("kst", [128, NB], F32)

        cur = {"sample": False}

        def slot(i, n=NB):
            if cur["sample"]:
                return sA.sub(i, (slice(None), i, slice(0, n)))
            return arena.sub(i, (slice(None), i, slice(0, n)))

        def slotc(i, a, b):
            return arena.sub(i, (slice(None), i, slice(a, b)))

        IscAP = arena.h[:, 24:32, :].rearrange("p s n -> p (s n)").bitcast(F32)
        mskAP = arena.h[:, 40:44, :].rearrange("p s n -> p (s n)")
        ISC_KEYS = arena.multi(range(24, 32), (slice(None), slice(24, 32), slice(None)))
        MSK_KEYS = arena.multi(range(40, 44), (slice(None), slice(40, 44), slice(None)))

        def Isc(a, b):
            return vv(IscAP[:, a:b], ISC_KEYS)

        def mskv(a, b):
            return vv(mskAP[:, a:b], MSK_KEYS)

        ps6b = ps[7].h[:].bitcast(BF16)

        A_U, A_Q, A_QI, A_ZA, A_BO = 0, 8, 16, 24, 32

        def dma(eng, out, in_, reads=(), writes=()):
            S.add(eng, lambda e: e.dma_start(out=out.ap if isinstance(out, V) else out,
                                             in_=in_.ap if isinstance(in_, V) else in_),
                  reads=list(reads), writes=list(writes), dma=True)

        def mm(out, lhsT, rhs, start, stop):
            S.add("pe", lambda e: e.matmul(out.ap, lhsT=lhsT.ap, rhs=rhs.ap, start=start, stop=stop),
                  reads=[lhsT, rhs], writes=[out])

        def act(out, in_, func, scale=1.0, bias=0.0, accum=None, extra_reads=()):
            kw = {}
            if accum is not None:
                kw["accum_out"] = accum.ap
            b = bias.ap if isinstance(bias, V) else bias
            sc = scale.ap if isinstance(scale, V) else scale
            S.add("act", lambda e: e.activation(out=out.ap, in_=in_.ap, func=func, bias=b, scale=sc, **kw),
                  reads=[in_, bias, scale] + list(extra_reads), writes=[out] + ([accum] if accum is not None else []))

        def tt(out, in0, in1, op, eng="dve"):
            S.add(eng, lambda e: e.tensor_tensor(out=out.ap, in0=in0.ap, in1=in1.ap, op=op),
                  reads=[in0, in1], writes=[out])

        def ts(out, in0, s1, s2, op0, op1=None, accum=None, eng="dve"):
            a1 = s1.ap if isinstance(s1, V) else s1
            a2 = s2.ap if isinstance(s2, V) else s2
            kw = {}
            if op1 is not None:
                kw["op1"] = op1
            if accum is not None:
                kw["accum_out"] = accum.ap
            S.add(eng, lambda e: e.tensor_scalar(out=out.ap, in0=in0.ap, scalar1=a1, scalar2=a2, op0=op0, **kw),
                  reads=[in0, s1, s2], writes=[out] + ([accum] if accum is not None else []))

        def stt(out, in0, scalar, in1, op0, op1, eng="dve"):
            a = scalar.ap if isinstance(scalar, V) else scalar
            S.add(eng, lambda e: e.scalar_tensor_tensor(out=out.ap, in0=in0.ap, scalar=a, in1=in1.ap, op0=op0, op1=op1),
                  reads=[in0, scalar, in1], writes=[out])

        def cp(out, in_, eng="dve"):
            S.add(eng, lambda e: e.tensor_copy(out=out.ap, in_=in_.ap), reads=[in_], writes=[out])

        def memset(out, val, eng="dve"):
            S.add(eng, lambda e: e.memset(out.ap, val), writes=[out])

        def recip(out, in_):
            S.add("dve", lambda e: e.reciprocal(out=out.ap, in_=in_.ap), reads=[in_], writes=[out])

        def transpose(out, in_, ident):
            S.add("pe", lambda e: e.transpose(out.ap, in_.ap, ident.ap), reads=[in_, ident], writes=[out])

        wstate = {"n": 0}

        NSLAB = 128
        Wscr_d = nc.dram_tensor("Wscr", [NSLAB, 128, 8192], BF16).ap()
        wblk = {"blk": 0, "k": 0}

        def wload(parts, extra_reads=(), used=None):
            sl = wsl[wstate["n"] % 2]
            wstate["n"] += 1
            k = wblk["k"]
            wblk["k"] += 1
            assert k < NSLAB
            skey = V(None, (("Wscr", k),))
            if wblk["blk"] == 0:
                for (c0, ncols, nkc, src) in parts:
                    dst = sl.h[:, c0:c0 + nkc * ncols].rearrange("p (k c) -> p k c", k=nkc)
                    step = max(1, min(nkc, 8))
                    for k0 in range(0, nkc, step):
                        k1 = min(nkc, k0 + step)
                        S.add("pool", lambda e, d=(dst[:, k0:k1, :] if used is None else dst[:, k0:k1, 0:used]), s_=src[:, k0:k1, :]: e.dma_start(out=d, in_=s_),
                              reads=list(extra_reads), writes=[sl[:]], dma=True)
                S.add("sp", lambda e, sl=sl, k=k: e.dma_start(out=Wscr_d[k, :, :], in_=sl.h[:, :]), reads=[sl[:]], writes=[skey], dma=True)
            else:
                S.add("sp", lambda e, sl=sl, k=k: e.dma_start(out=sl.h[:, :], in_=Wscr_d[k, :, :]), reads=[skey], writes=[sl[:]], dma=True)
            return sl

        def wview(sl, c0, ncols, kc, a, b):
            o = c0 + kc * ncols
            return sl[:, o + a:o + b]

        def wsrc(w_d, nkc, c0, c1):
            return w_d.rearrange("(k p) c -> p k c", p=128)[:, :, c0:c1]

        dma("sp", identf[:], ident_d, writes=[identf[:]])
        dma("pool", identb[:], ident_d, writes=[identb[:]])
        dma("sp", tri[:], tri_d, writes=[tri[:]])
        dma("sp", iota1[:], iota_d, writes=[iota1[:]])
        dma("sp", nrm[:], nrm_d, writes=[nrm[:]])
        dma("sp", qkn[:], qkn_d, writes=[qkn[:]])
        dma("sp", glb[:], glb_d, writes=[glb[:]])
        dma("sp", hmask[:], hmask_d, writes=[hmask[:]])
        dma("sp", ptl[:], ptl_d, writes=[ptl[:]])
        dma("sp", mnew[:], mnew_d, writes=[mnew[:]])
        ts(idxh[:], ptl[:], 2.0, None, ALU.mult)
        ts(idxh[:], idxh[:], hmask[:, 1:2], None, ALU.add)
        for ch in range(8):
            ts(idx8[:, :, ch], idxh[:], 8.0, float(ch), ALU.mult, ALU.add)
        dma("sp", ssmp[:], ssmp_d, writes=[ssmp[:]])
        dma("sp", ssmd[:], ssmd_d, writes=[ssmd[:]])
        dma("sp", vv(h0s.h[:].rearrange("p a j b -> p (a j b)"), h0s[:]), h0_d, writes=[h0s[:]])
        memset(onesb[:], 1.0)
        memset(wsl[0][:], 0.0)
        memset(wsl[1][:], 0.0)
        memset(hre[:], 0.0)
        memset(him[:], 0.0)
        memset(car[:], 0.0)
        for c in range(8):
            ts(Dd.sub(c, (slice(None), c, slice(None))), identf[:], ssmd[:, c:c + 1], None, ALU.mult)

        lre = ssmp[:, 0:32]; lim = ssmp[:, 32:64]; ldt = ssmp[:, 64:96]

        def T(i):
            return stp.sub(i, (slice(None), i, slice(None)))
        act(T(0), ldt, AF.Exp)
        tt(T(1), lre, T(0), ALU.mult)
        tt(T(2), lim, T(0), ALU.mult)
        act(sdec[:], T(1), AF.Exp)
        ts(T(3), T(2), float(1.0 / (2 * np.pi)), None, ALU.mult)
        stpi = sb("stpi", [128, 32], I32)
        cp(stpi[:], T(3))
        stt(sfrq[:], stpi[:], -1.0, T(3), ALU.mult, ALU.add)
        act(T(4), sfrq[:], AF.Sin, scale=float(2 * np.pi))
        ts(T(5), sfrq[:], 0.25, None, ALU.add)
        cp(stpi[:], T(5))
        stt(T(5), stpi[:], -1.0, T(5), ALU.mult, ALU.add)
        act(T(5), T(5), AF.Sin, scale=float(2 * np.pi))
        tt(lbre[:], sdec[:], T(5), ALU.mult)
        tt(lbim[:], sdec[:], T(4), ALU.mult)
        ts(T(6), lbre[:], -1.0, None, ALU.add)
        tt(T(0), lre, lre, ALU.mult)
        tt(T(1), lim, lim, ALU.mult)
        tt(T(0), T(0), T(1), ALU.add)
        recip(T(0), T(0))
        tt(T(1), T(6), lre, ALU.mult)
        tt(T(2), lbim[:], lim, ALU.mult)
        tt(T(1), T(1), T(2), ALU.add)
        tt(gre[:], T(1), T(0), ALU.mult)
        tt(T(1), lbim[:], lre, ALU.mult)
        tt(T(2), T(6), lim, ALU.mult)
        tt(T(1), T(1), T(2), ALU.subtract)
        tt(gim[:], T(1), T(0), ALU.mult)
        Bscr_d = nc.dram_tensor("Bscr", [128, 2, 32, 128], BF16).ap()
        BSCR = V(None, (("Bscr", None),))
        for ch in range(4):
            js = slice(8 * ch, 8 * ch + 8)

            def f32v(s0):
                ks = list(range(s0, s0 + 4))
                v = arena.multi(ks, (slice(None), slice(s0, s0 + 4), slice(None)))
                return vv(v.ap.rearrange("p s n -> p (s n)").bitcast(F32).rearrange("p (j m) -> p j m", j=8), v)
            bre_raw = f32v(0); bim_raw = f32v(4); o_re = f32v(8); o_im = f32v(12); tmp = f32v(16)
            dma("sp", bre_raw, bsre_d.rearrange("p (j m) -> p j m", j=32)[:, js, :], writes=[bre_raw])
            dma("sp", bim_raw, bsim_d.rearrange("p (j m) -> p j m", j=32)[:, js, :], writes=[bim_raw])
            g_re_b = vv(gre.h[:, js].unsqueeze(2).to_broadcast([128, 8, 128]), gre[:])
            g_im_b = vv(gim.h[:, js].unsqueeze(2).to_broadcast([128, 8, 128]), gim[:])
            tt(o_re, bre_raw, g_re_b, ALU.mult)
            tt(tmp, bim_raw, g_im_b, ALU.mult)
            tt(o_re, o_re, tmp, ALU.subtract)
            tt(o_im, bim_raw, g_re_b, ALU.mult)
            tt(tmp, bre_raw, g_im_b, ALU.mult)
            tt(o_im, o_im, tmp, ALU.add)
            stg = [vv(sq.h[:, 0:2, :].rearrange("p a n -> p (a n)").rearrange("p (j m) -> p j m", j=8), sq[:]),
                   vv(sq.h[:, 2:4, :].rearrange("p a n -> p (a n)").rearrange("p (j m) -> p j m", j=8), sq[:])]
            for jj in range(8):
                for (ri, src, pb) in ((0, o_re, ps[0]), (1, o_im, ps[1])):
                    transpose(pb[:, 0:128], vv(src.ap[:, jj, :], src), identf[:])
                    act(vv(stg[ri].ap[:, jj, :], sq[:]), pb[:, 0:128], AF.Copy)
            for ri in range(2):
                dma("sp", Bscr_d[:, ri, js, :], stg[ri], reads=[sq[:]], writes=[BSCR])

        def rmsnorm(n, gcol0):
            for grp in range(4):
                src = xT.multi(range(4 * grp, 4 * grp + 4), (slice(None), slice(4 * grp, 4 * grp + 4), slice(0, n)))
                act(sq[:, :, 0:n], src, AF.Square)
                for i in range(4):
                    mm(ps[6][:, 0:n], onesb[:], sq[:, i, 0:n], start=(grp == 0 and i == 0), stop=(grp == 3 and i == 3))
            act(rstd[:, 0:n], ps[6][:, 0:n], AF.Sqrt, scale=1.0 / D, bias=EPS)
            recip(rstd[:, 0:n], rstd[:, 0:n])
            for j in range(NKC):
                stt(hT.sub(j, (slice(None), j, slice(0, n))), xT.sub(j, (slice(None), j, slice(0, n))),
                    nrm[:, gcol0 + j:gcol0 + j + 1], rstd[:, 0:n], ALU.mult, ALU.mult)

        def ffn(n, wg_d, wu_d, wd_d):
            hall = lambda kc: hT[:, kc, 0:n]
            for s in range(NFF // 2):
                sl = wload([(0, 256, NKC, wsrc(wg_d, NKC, 256 * s, 256 * s + 256)),
                            (4096, 256, NKC, wsrc(wu_d, NKC, 256 * s, 256 * s + 256))])
                for half in range(2):
                    f = 2 * s + half
                    pg = ps[2 * (f % 2)]; pu = ps[2 * (f % 2) + 1]
                    for kc in range(NKC):
                        mm(pg[:, 0:n], wview(sl, 0, 256, kc, 128 * half, 128 * half + 128), hall(kc), kc == 0, kc == NKC - 1)
                    for kc in range(NKC):
                        mm(pu[:, 0:n], wview(sl, 4096, 256, kc, 128 * half, 128 * half + 128), hall(kc), kc == 0, kc == NKC - 1)
                    tmp = tA if f % 2 == 0 else tB
                    act(tmp[:, 0:n], pg[:, 0:n], AF.Silu)
                    tt(slot(f, n), tmp[:, 0:n], pu[:, 0:n], ALU.mult)
            for j in range(NKC):
                sl = wload([(0, 128, NFF, wsrc(wd_d, NFF, 128 * j, 128 * j + 128))])
                pd = ps[4 + (j % 2)]
                for kc in range(NFF):
                    mm(pd[:, 0:n], wview(sl, 0, 128, kc, 0, 128), slot(kc, n), kc == 0, kc == NFF - 1)
                xj = xT.sub(j, (slice(None), j, slice(0, n)))
                stt(xj, pd[:, 0:n], 0.5, xj, ALU.mult, ALU.add)

        def proj_pair(sl, t0, n, pa, pb):
            for (tix, pp) in ((t0, pa), (t0 + 1, pb)):
                for kc in range(NKC):
                    mm(pp[:, 0:n], wview(sl, 0, 512, kc, 128 * tix, 128 * tix + 128), hT[:, kc, 0:n], kc == 0, kc == NKC - 1)

        def mix_slab(i):
            return wload([(0, 512, NKC, wsrc(wmix_d, NKC, 512 * i, 512 * i + 512))])

        def mix(blk, n, c0):
            sample = blk == NBLK
            rmsnorm(n, 16)
            cosK = tE[:, 0:n]; sinK = tF[:, 0:n]; cosI = tE[:, 0:n]; sinI = tF[:, 0:n]
            dma("sp", tE[:, 0:n], rope_d[0, :, c0:c0 + n], writes=[tE[:]])
            dma("sp", tF[:, 0:n], rope_d[1, :, c0:c0 + n], writes=[tF[:]])
            for i in range(2):
                sl = mix_slab(i)
                for t in range(4):
                    pp = ps[t % 4]
                    for kc in range(NKC):
                        mm(pp[:, 0:n], wview(sl, 0, 512, kc, 128 * t, 128 * t + 128), hT[:, kc, 0:n], kc == 0, kc == NKC - 1)
                    act(slot(A_U + 4 * i + t, n), pp[:, 0:n], AF.Copy)

            def normrope(pa, pb, gcol, out_bf, out_f32=None):
                act(sq[:, 0, 0:n], pa[:, 0:n], AF.Square)
                mm(ps[6][:, 0:n], onesb[:], sq[:, 0, 0:n], True, True)
                act(rstd[:, 0:n], ps[6][:, 0:n], AF.Sqrt, scale=1.0 / 128, bias=EPS)
                recip(rstd[:, 0:n], rstd[:, 0:n])
                stt(tC[:, 0:n], pa[:, 0:n], qkn[:, gcol:gcol + 1], rstd[:, 0:n], ALU.mult, ALU.mult)
                tt(tC[:, 0:n], tC[:, 0:n], cosK, ALU.mult)
                stt(tD[:, 0:n], pb[:, 0:n], qkn[:, gcol + 1:gcol + 2], rstd[:, 0:n], ALU.mult, ALU.mult)
                tt(tD[:, 0:n], tD[:, 0:n], sinK, ALU.mult)
                if out_f32 is not None:
                    tt(out_f32, tC[:, 0:n], tD[:, 0:n], ALU.add)
                    act(out_bf, out_f32, AF.Copy)
                else:
                    tt(out_bf, tC[:, 0:n], tD[:, 0:n], ALU.add)

            if KSUB < 2:
                return False
            for i in range(4):
                sl = mix_slab(2 + i)
                for hh in range(2):
                    h = 2 * i + hh
                    pa, pb = ps[2 * hh], ps[2 * hh + 1]
                    proj_pair(sl, 2 * hh, n, pa, pb)
                    normrope(pa, pb, 0, slot(A_Q + h, n))
            if KSUB < 3:
                return False
            sl = mix_slab(6)
            for g in range(2):
                pa, pb = ps[2 * g], ps[2 * g + 1]
                proj_pair(sl, 2 * g, n, pa, pb)
                if not sample:
                    kdst = kTs.sub(blk, (slice(None), g, slice(c0, c0 + n)))
                else:
                    kdst = knew.sub(g, (slice(None), g, slice(0, n)))
                normrope(pa, pb, 2, kdst, out_f32=kst[:, 0:n])
                dma("sp", kT_o[128 * g:128 * g + 128, c0:c0 + n], kst[:, 0:n], reads=[kst[:]])
            if KSUB < 4:
                return False
            dma("sp", tE[:, 0:n], rope_d[2, :, c0:c0 + n], writes=[tE[:]])
            dma("sp", tF[:, 0:n], rope_d[3, :, c0:c0 + n], writes=[tF[:]])
            for i in range(4):
                sl = mix_slab(7 + i)
                for hh in range(2):
                    t = 2 * i + hh
                    pa, pb = ps[2 * hh], ps[2 * hh + 1]
                    if not sample:
                        proj_pair(sl, 2 * hh, n, pa, pb)
                        tt(tC[:, 0:n], pa[:, 0:n], cosI, ALU.mult)
                        tt(tD[:, 0:n], pb[:, 0:n], sinI, ALU.mult)
                        tt(slot(A_QI + t, n), tC[:, 0:n], tD[:, 0:n], ALU.add)
                    else:
                        for a in range(2):
                            for (tix, pp) in ((2 * hh, pa), (2 * hh + 1, pb)):
                                for kc in range(NKC):
                                    mm(pp[0:64, 0:n], wview(sl, 0, 512, kc, 128 * tix + 64 * a, 128 * tix + 64 * a + 64),
                                       hT[:, kc, 0:n], kc == 0, kc == NKC - 1)
                            tt(tC[0:64, 0:n], pa[0:64, 0:n], tE[0:64, 0:n], ALU.mult)
                            tt(tD[0:64, 0:n], pb[0:64, 0:n], tF[0:64, 0:n], ALU.mult)
                            tt(sQI[0:64, 2 * t + a, 0:n], tC[0:64, 0:n], tD[0:64, 0:n], ALU.add)
            if KSUB < 5:
                return False
            sl = wload([(0, 256, NKC, wsrc(wmix_d, NKC, 44 * 128, 46 * 128))])
            pa, pb = ps[0], ps[1]
            for (tix, pp) in ((0, pa), (1, pb)):
                for kc in range(NKC):
                    mm(pp[:, 0:n], wview(sl, 0, 256, kc, 128 * tix, 128 * tix + 128), hT[:, kc, 0:n], kc == 0, kc == NKC - 1)
            tt(tC[:, 0:n], pa[:, 0:n], cosI, ALU.mult)
            tt(tD[:, 0:n], pb[:, 0:n], sinI, ALU.mult)
            tt(kst[:, 0:n], tC[:, 0:n], tD[:, 0:n], ALU.add)
            dma("sp", kiT_o[:, c0:c0 + n], kst[:, 0:n], reads=[kst[:]])
            if not sample:
                ts(kiA.sub(blk, (slice(None), slice(c0, c0 + n))), kst[:, 0:n], hmask[:, 0:1], None, ALU.mult)
                ts(kiB.sub(blk, (slice(None), slice(c0, c0 + n))), kst[:, 0:n], hmask[:, 1:2], None, ALU.mult)
            else:
                ts(kinA[:, 0:n], kst[:, 0:n], hmask[:, 0:1], None, ALU.mult)
                ts(kinB[:, 0:n], kst[:, 0:n], hmask[:, 1:2], None, ALU.mult)
            if KSUB < 6:
                return False
            sl = wload([(0, 512, NKC, wsrc(wtok_d, NKC, 0, 512))])
            ntt = (n + 127) // 128
            for g in range(2):
                pp = ps[g]
                for kc in range(NKC):
                    mm(pp[:, 0:n], wview(sl, 0, 512, kc, 128 * g, 128 * g + 128), hT[:, kc, 0:n], kc == 0, kc == NKC - 1)
                cp(kst[:, 0:n], pp[:, 0:n])
                dma("sp", vT_o[128 * g:128 * g + 128, c0:c0 + n], kst[:, 0:n], reads=[kst[:]])
                if KATT and (KV & 1):
                    act(hre[:, 0:n], pp[:, 0:n], AF.Copy)
                    for tti in range(ntt):
                        m = max(32, min(128, n - 128 * tti))
                        transpose(vv(ps6b[0:m, 128 * tti:128 * tti + 128], ps[7][:]), hre[:, 128 * tti:128 * tti + m], identb[:])
                    if not sample:
                        for tti in range(ntt):
                            gt = blk * 4 + tti
                            act(vS.sub(gt, (slice(None), gt, slice(128 * g, 128 * g + 128))),
                                vv(ps6b[:, 128 * tti:128 * tti + 128], ps[7][:]), AF.Copy)
                    else:
                        act(vnew[0:n, 128 * g:128 * g + 128], vv(ps6b[0:n, 0:128], ps[7][:]), AF.Copy)
            if KATT and (KV & 2):
                pw = ps[2]
                if n < 32:
                    memset(tC[0:32, 0:32], 0.0)
                for kc in range(NKC):
                    mm(pw[0:32, 0:n], wview(sl, 0, 512, kc, 256, 288), hT[:, kc, 0:n], kc == 0, kc == NKC - 1)
                ts(tC[0:32, 0:n], pw[0:32, 0:n], 0.25, None, ALU.mult)
                for tti in range(ntt):
                    m = max(32, min(128, n - 128 * tti))
                    transpose(ps[3][0:m, 32 * tti:32 * tti + 32], tC[0:32, 128 * tti:128 * tti + m], identf[0:32, 0:32])
                for tti in range(ntt):
                    m = min(128, n - 128 * tti)
                    cp(wtk.sub(tti, (slice(0, m), tti, slice(None))), ps[3][0:m, 32 * tti:32 * tti + 16])
            if stage_limit < 3:
                return False
            if not sample:
                ssm_prompt(n)
            else:
                ssm_sample()
            if stage_limit < 4:
                return False
            glu(n)
            if stage_limit < 5:
                return False
            if not sample and KATT:
                for qt in range(4):
                    attn_prompt(blk, qt)
            else:
                attn_sample()
            if stage_limit < 6:
                return False
            merge_out(n)
            return True

        def glu(n):
            sl = wload([(0, 1024, 8, wsrc(glw_d, 8, 0, 1024))])
            for j in range(8):
                pp = ps[j % 4]
                for kc in range(8):
                    mm(pp[:, 0:n], wview(sl, 0, 1024, kc, 128 * j, 128 * j + 128), slot(A_ZA + kc, n), kc == 0, kc == 7)
                tmp = tA if j % 2 == 0 else tB
                act(tmp[:, 0:n], pp[:, 0:n], AF.Sigmoid, bias=glb[:, j:j + 1])
                tt(slot(A_U + j, n), tmp[:, 0:n], slot(A_ZA + j, n), ALU.mult)

        def merge_out(n):
            for j in range(16):
                sl = wload([(0, 128, 8, wsrc(wa_d, 8, 128 * j, 128 * j + 128)),
                            (1024, 128, 8, wsrc(wb_d, 8, 128 * j, 128 * j + 128)),
                            (2048, 128, 16, wsrc(wgab_d, 16, 128 * j, 128 * j + 128)),
                            (4096, 128, 16, wsrc(wgab_d, 16, 2048 + 128 * j, 2048 + 128 * j + 128))])
                b0 = 0
                pA, pB, pga, pgb = ps[b0], ps[b0 + 1], ps[b0 + 2], ps[b0 + 3]
                for kc in range(8):
                    mm(pA[:, 0:n], wview(sl, 0, 128, kc, 0, 128), slot(A_U + kc, n), kc == 0, kc == 7)
                for kc in range(8):
                    mm(pB[:, 0:n], wview(sl, 1024, 128, kc, 0, 128), slot(A_BO + kc, n), kc == 0, kc == 7)
                for kc in range(16):
                    mm(pga[:, 0:n], wview(sl, 2048, 128, kc, 0, 128), hT[:, kc, 0:n], kc == 0, kc == 15)
                for kc in range(16):
                    mm(pgb[:, 0:n], wview(sl, 4096, 128, kc, 0, 128), hT[:, kc, 0:n], kc == 0, kc == 15)
                act(tA[:, 0:n], pga[:, 0:n], AF.Sigmoid)
                act(tB[:, 0:n], pgb[:, 0:n], AF.Sigmoid)
                tt(tC[:, 0:n], tA[:, 0:n], pA[:, 0:n], ALU.mult)
                tt(tD[:, 0:n], tB[:, 0:n], pB[:, 0:n], ALU.mult)
                tt(slot(A_Q + j, n), tC[:, 0:n], tD[:, 0:n], ALU.add)
            for i in range(4):
                sl = wload([(0, 512, NKC, wsrc(wo_d, NKC, 512 * i, 512 * i + 512))])
                for t in range(4):
                    j = 4 * i + t
                    pp = ps[j % 4]
                    for kc in range(NKC):
                        mm(pp[:, 0:n], wview(sl, 0, 512, kc, 128 * t, 128 * t + 128), slot(A_Q + kc, n), kc == 0, kc == NKC - 1)
                    xj = xT.sub(j, (slice(None), j, slice(0, n)))
                    tt(xj, pp[:, 0:n], xj, ALU.add)

        NIT = 18

        def bisect(L, lo, W, mid, cnt, gsel):
            pr = lo.ap.shape[0]
            for it in range(NIT):
                sc = float(2.0 ** -(it + 1))
                stt(mid, W, sc, lo, ALU.mult, ALU.add)
                ts(vv(mskAP[0:pr, 0:L], MSK_KEYS), vv(IscAP[0:pr, 0:L], ISC_KEYS), mid, None, ALU.is_ge, ALU.add, accum=cnt)
                ts(gsel, cnt, TOPK - 0.5, sc, ALU.is_ge, ALU.mult)
                stt(lo, gsel, W, lo, ALU.mult, ALU.add)

        def attn_prompt(blk, qt):
            G = 4 * blk + qt
            L = 128 * (G + 1)
            nkb = G + 1
            q0 = 128 * qt
            for h in range(16):
                ts(Dm.sub(h, (slice(None), h, slice(None))), identf[:], wtk[:, qt, h:h + 1], None, ALU.mult)
            for ch in range((L + 511) // 512):
                w = min(512, L - 512 * ch)
                pI = ps[2]
                for h in range(16):
                    psc = ps[h % 2]
                    kx = kiA if h % 2 == 0 else kiB
                    mm(psc[:, 0:w], slotc(A_QI + h // 2, q0, q0 + 128), kx[:, 512 * ch:512 * ch + w], True, True)
                    act(rl[h % 2][:, 0:w], psc[:, 0:w], AF.Relu, scale=0.125)
                    mm(pI[:, 0:w], Dm.sub(h, (slice(None), h, slice(None))), rl[h % 2][:, 0:w], h == 0, h == 15)
                act(Isc(512 * ch, 512 * ch + w), pI[:, 0:w], AF.Copy)
            hi, lo, W, mid, cnt, gsel = (bs[:, i:i + 1] for i in range(6))
            S.add("dve", lambda e: e.tensor_reduce(out=bs.h[:, 0:1], in_=IscAP[:, 0:L], axis=AX.X, op=ALU.max),
                  reads=[Isc(0, L)], writes=[hi])
            S.add("dve", lambda e: e.tensor_reduce(out=bs.h[:, 1:2], in_=IscAP[:, 0:L], axis=AX.X, op=ALU.min),
                  reads=[Isc(0, L)], writes=[lo])
            tt(Isc(128 * G, 128 * G + 128), Isc(128 * G, 128 * G + 128), tri[:], ALU.add)
            ts(lo, lo, -1.0, None, ALU.add)
            tt(W, hi, lo, ALU.subtract)
            ts(W, W, 1.0, None, ALU.add)
            bisect(L, lo, W, mid, cnt, gsel)
            ts(mskv(0, L), Isc(0, L), lo, None, ALU.is_ge)
            for k8 in range(0, nkb, 8):
                c8 = min(8, nkb - k8)
                for kb in range(k8, k8 + c8):
                    transpose(vv(ps6b[:, (kb - k8) * 128:(kb - k8) * 128 + 128], ps[7][:]), mskv(128 * kb, 128 * kb + 128), identb[:])
                act(vv(mskT.h[:, k8:k8 + c8, :].rearrange("p a b -> p (a b)"), mskT[:]),
                    vv(ps6b[:, 0:c8 * 128], ps[7][:]), AF.Copy)
            for hh in range(8):
                g = hh // 4
                pO, pD = ps[6], ps[3]
                for c4 in range(0, nkb, 4):
                    c = min(4, nkb - c4)
                    pS = ps[4 + ((c4 // 4) % 2)]
                    pex = pe_[(c4 // 4) % 2]
                    for kb in range(c4, c4 + c):
                        mm(pS[:, (kb - c4) * 128:(kb - c4) * 128 + 128], kTs[:, g, 128 * kb:128 * kb + 128],
                           slotc(A_Q + hh, q0, q0 + 128), True, True)
                    act(pex[:, 0:c * 128], pS[:, 0:c * 128], AF.Exp, scale=float(128 ** -0.5))
                    tt(pex[:, 0:c * 128], pex[:, 0:c * 128],
                       vv(mskT.h[:, c4:c4 + c, :].rearrange("p a b -> p (a b)"), mskT[:]), ALU.mult)
                    for kb in range(c4, c4 + c):
                        mm(pO[:, 0:128], vS[:, kb, 128 * g:128 * g + 128], pex[:, (kb - c4) * 128:(kb - c4) * 128 + 128],
                           kb == 0, kb == nkb - 1)
                        mm(pD[:, 0:128], onesb[:], pex[:, (kb - c4) * 128:(kb - c4) * 128 + 128], kb == 0, kb == nkb - 1)
                recip(tA[:, 0:128], pD[:, 0:128])
                tt(slotc(A_BO + hh, q0, q0 + 128), pO[:, 0:128], tA[:, 0:128], ALU.mult)

        def attn_sample():
            LS = PAST + NS
            arf = arena.h[:].rearrange("p s n -> p (s n)").bitcast(F32)
            ARK = arena[:]

            def IS(a, b_):
                return vv(arf[0:16, a:b_], ARK)
            junk = vv(kTs.h[:].rearrange("p a n -> p (a n)")[0:16, 0:2052], kTs[:])
            kic = vv(rl[0].h[:, :].rearrange("p (t d) -> p t d", t=8), rl[0][:])
            kiTc = vv(mskT.h[:].rearrange("p a b -> p (a b)")[0:64, 0:1024], mskT[:])
            Kc = vv(kiA.h[:, :].rearrange("p (t d) -> p t d", t=8), kiA[:])
            Vc = vv(kiB.h[:, :].rearrange("p (t d) -> p t d", t=8), kiB[:])
            KTc = vv(Dm.h[:].rearrange("p a b -> p (a b)").rearrange("p (g k) -> p g k", g=2), Dm[:])
            mTc = vv(pe_[0].h[:, 0:128].rearrange("p (t q) -> p t q", t=8), pe_[0][:])
            mskc = vv(vS.h[:].rearrange("p a b -> p (a b)")[0:16, 0:1024], vS[:])
            pexv = vv(pe_[1].h[:, 0:256].rearrange("p (g t c) -> p g t c", g=2, t=8), pe_[1][:])
            qs = vv(hre.h[:, 0:32].rearrange("p (h q) -> p h q", h=8), hre[:])
            for h in range(16):
                ts(DmS[:, h, :], identf[0:16, 0:16], wtk[0:16, 0, h:h + 1], None, ALU.mult)
            hi, lo, W, mid, cnt, gsel, c2 = (bs2[:, i:i + 1] for i in range(7))
            for bi in range(4):
                for ch in range(8):
                    S.add("pool", lambda e, ch=ch, bi=bi: e.indirect_dma_start(
                        out=rl[0].h[:, :], out_offset=None, in_=cik_d[:, :],
                        in_offset=bass.IndirectOffsetOnAxis(ap=idx8.h[:, bi, ch:ch + 1], axis=0)),
                        reads=[idx8[:]], writes=[rl[0][:]], dma=True)
                    for t in range(8):
                        transpose(vv(ps6b[0:64, 128 * t:128 * t + 128], ps[7][:]), vv(kic.ap[:, t, :], kic), identb[:])
                    act(kiTc, vv(ps6b[0:64, 0:1024], ps[7][:]), AF.Copy)
                    for sc in range(2):
                        pI = ps[2]
                        for h in range(16):
                            psc = ps[h % 2]
                            mm(psc[0:16, 0:512], sQI[0:64, h, 0:16], vv(kiTc.ap[:, 512 * sc:512 * sc + 512], kiTc), True, True)
                            act(pe_[h % 2][0:16, 0:512], psc[0:16, 0:512], AF.Relu, scale=0.125)
                            mm(pI[0:16, 0:512], DmS[:, h, :], pe_[h % 2][0:16, 0:512], h == 0, h == 15)
                        c0_ = 1024 * ch + 512 * sc
                        act(IS(c0_, c0_ + 512), pI[0:16, 0:512], AF.Copy)
                pI = ps[2]
                for h in range(16):
                    psc = ps[h % 2]
                    mm(psc[0:16, 0:16], sQI[0:64, h, 0:16], kinA[0:64, 0:16], True, True)
                    act(pe_[h % 2][0:16, 0:16], psc[0:16, 0:16], AF.Relu, scale=0.125)
                    mm(pI[0:16, 0:16], DmS[:, h, :], pe_[h % 2][0:16, 0:16], h == 0, h == 15)
                S.add("dve", lambda e: e.tensor_reduce(out=bs2.h[:, 0:1], in_=arf[0:16, 0:PAST], axis=AX.X, op=ALU.max),
                      reads=[IS(0, PAST)], writes=[hi])
                S.add("dve", lambda e: e.tensor_reduce(out=bs2.h[:, 1:2], in_=arf[0:16, 0:PAST], axis=AX.X, op=ALU.min),
                      reads=[IS(0, PAST)], writes=[lo])
                tt(IS(PAST, LS), pI[0:16, 0:16], mnew[:], ALU.add)
                ts(lo, lo, -64.0, None, ALU.add)
                ts(hi, hi, 64.0, None, ALU.add)
                tt(W, hi, lo, ALU.subtract)
                for it in range(NIT + 4):
                    scv = float(2.0 ** -(it + 1))
                    stt(mid, W, scv, lo, ALU.mult, ALU.add)
                    for q4 in range(4):
                        ts(junk, IS(2052 * q4, 2052 * q4 + 2052), mid, None, ALU.is_ge, ALU.add, accum=(cnt if q4 == 0 else c2))
                        if q4 > 0:
                            tt(cnt, cnt, c2, ALU.add)
                    ts(gsel, cnt, TOPK - 0.5, scv, ALU.is_ge, ALU.mult)
                    stt(lo, gsel, W, lo, ALU.mult, ALU.add)
                for hh in range(8):
                    cp(vv(qs.ap[:, hh, :], qs), slotc_s(A_Q + hh, 4 * bi, 4 * bi + 4))
                pO = [ps[2], ps[3]]
                pD = [ps[4], ps[5]]
                nblk_tot = 65
                for ch in range(9):
                    if ch < 8:
                        for (dst, src_d) in ((kiA, ck_d), (kiB, cv_d)):
                            S.add("pool", lambda e, ch=ch, bi=bi, dst=dst, src_d=src_d: e.indirect_dma_start(
                                out=dst.h[:, :], out_offset=None, in_=src_d[:, :],
                                in_offset=bass.IndirectOffsetOnAxis(ap=idx8.h[:, bi, ch:ch + 1], axis=0)),
                                reads=[idx8[:]], writes=[dst[:]], dma=True)
                        ts(mskc, IS(1024 * ch, 1024 * ch + 1024), lo, None, ALU.is_ge)
                        for t in range(8):
                            transpose(vv(ps6b[:, 16 * t:16 * t + 16], ps[7][:]), vv(mskc.ap[:, 128 * t:128 * t + 128], mskc), identb[0:16, 0:16])
                        act(vv(pe_[0].h[:, 0:128], pe_[0][:]), vv(ps6b[:, 0:128], ps[7][:]), AF.Copy)
                        for g in range(2):
                            for t in range(8):
                                transpose(vv(ps6b[:, 128 * t:128 * t + 128], ps[7][:]), vv(Kc.ap[:, t, 128 * g:128 * g + 128], Kc), identb[:])
                            act(vv(KTc.ap[:, g, :], KTc), vv(ps6b[:, 0:1024], ps[7][:]), AF.Copy)
                        pS = ps[0]
                        for g in range(2):
                            for t in range(8):
                                mm(pS[:, 128 * g + 16 * t:128 * g + 16 * t + 16], vv(KTc.ap[:, g, 128 * t:128 * t + 128], KTc),
                                   vv(qs.ap[:, 4 * g:4 * g + 4, :].rearrange("p h q -> p (h q)"), qs), True, True)
                        act(vv(pe_[1].h[:, 0:256], pe_[1][:]), pS[:, 0:256], AF.Exp, scale=float(128 ** -0.5))
                        for g in range(2):
                            pg_ = vv(pexv.ap[:, g, :, :].rearrange("p t (h q) -> p t h q", h=4), pexv)
                            mb_ = vv(mTc.ap[:, :, 4 * bi:4 * bi + 4].unsqueeze(2).to_broadcast([128, 8, 4, 4]), mTc)
                            tt(pg_, pg_, mb_, ALU.mult)
                        for g in range(2):
                            for t in range(8):
                                kb = 8 * ch + t
                                rhs_ = vv(pexv.ap[:, g, t, :], pexv)
                                mm(pO[g][:, 0:16], vv(Vc.ap[:, t, 128 * g:128 * g + 128], Vc), rhs_, kb == 0, False)
                                mm(pD[g][:, 0:16], onesb[:], rhs_, kb == 0, False)
                    else:
                        ts(vv(mskc.ap[:, 0:16], mskc), IS(PAST, LS), lo, None, ALU.is_ge)
                        transpose(vv(ps6b[0:16, 0:16], ps[7][:]), vv(mskc.ap[:, 0:16], mskc), identb[0:16, 0:16])
                        act(vv(pe_[0].h[0:16, 0:16], pe_[0][:]), vv(ps6b[0:16, 0:16], ps[7][:]), AF.Copy)
                        pS = ps[0]
                        for g in range(2):
                            mm(pS[0:16, 16 * g:16 * g + 16], knew[:, g, 0:16],
                               vv(qs.ap[:, 4 * g:4 * g + 4, :].rearrange("p h q -> p (h q)"), qs), True, True)
                        act(vv(pe_[1].h[0:16, 0:32], pe_[1][:]), pS[0:16, 0:32], AF.Exp, scale=float(128 ** -0.5))
                        for g in range(2):
                            pg_ = vv(pe_[1].h[0:16, 16 * g:16 * g + 16].rearrange("p (h q) -> p h q", h=4), pe_[1][:])
                            mb_ = vv(pe_[0].h[0:16, 4 * bi:4 * bi + 4].unsqueeze(1).to_broadcast([16, 4, 4]), pe_[0][:])
                            tt(pg_, pg_, mb_, ALU.mult)
                        for g in range(2):
                            rhs_ = vv(pe_[1].h[0:16, 16 * g:16 * g + 16], pe_[1][:])
                            mm(pO[g][:, 0:16], vnew[0:16, 128 * g:128 * g + 128], rhs_, False, True)
                            mm(pD[g][:, 0:16], onesb[0:16, :], rhs_, False, True)
                for g in range(2):
                    recip(tA[:, 0:16], pD[g][:, 0:16])
                    tt(tB[:, 0:16], pO[g][:, 0:16], tA[:, 0:16], ALU.mult)
                    outv = sA.multi(range(A_BO + 4 * g, A_BO + 4 * g + 4),
                                    (slice(None), slice(A_BO + 4 * g, A_BO + 4 * g + 4), slice(4 * bi, 4 * bi + 4)))
                    cp(outv, vv(tB.h[:, 0:16].rearrange("p (h q) -> p h q", h=4), tB[:]))

        def slotc_s(i, a, b_):
            return sA.sub(i, (slice(None), i, slice(a, b_)))

        def ssm_prompt(n):
            slB = wload([(0, 128, 32, Bscr_d[:, 0, :, :]), (4096, 128, 32, Bscr_d[:, 1, :, :])], extra_reads=[BSCR])
            slC = wload([(0, 128, 32, cre_d.rearrange("p (j m) -> p j m", j=32)),
                         (4096, 128, 32, cim_d.rearrange("p (j m) -> p j m", j=32))])
            for c in range(8):
                py = ps[4 + (c % 2)]
                for jj in range(4):
                    j = 4 * c + jj
                    pre, pim = ps[2 * (j % 2)], ps[2 * (j % 2) + 1]
                    ub = slot(A_U + c, n)
                    mm(pre[:, 0:n], wview(slB, 0, 128, j, 0, 128), ub, True, True)
                    mm(pim[:, 0:n], wview(slB, 4096, 128, j, 0, 128), ub, True, True)
                    fj = sfrq[:, j:j + 1]
                    ts(tA[:, 0:n], iota1[:, 0:n], fj, None, ALU.mult)
                    cp(tI[:, 0:n], tA[:, 0:n])
                    stt(tB[:, 0:n], tI[:, 0:n], -1.0, tA[:, 0:n], ALU.mult, ALU.add)
                    act(sinT[:, 0:n], tB[:, 0:n], AF.Sin, scale=float(2 * np.pi))
                    ts(tA[:, 0:n], tA[:, 0:n], 0.25, None, ALU.add)
                    cp(tI[:, 0:n], tA[:, 0:n])
                    stt(tB[:, 0:n], tI[:, 0:n], -1.0, tA[:, 0:n], ALU.mult, ALU.add)
                    act(cosT[:, 0:n], tB[:, 0:n], AF.Sin, scale=float(2 * np.pi))
                    tt(tA[:, 0:n], cosT[:, 0:n], pre[:, 0:n], ALU.mult)
                    tt(tB[:, 0:n], sinT[:, 0:n], pim[:, 0:n], ALU.mult)
                    tt(tC[:, 0:n], tA[:, 0:n], tB[:, 0:n], ALU.add)
                    tt(tA[:, 0:n], cosT[:, 0:n], pim[:, 0:n], ALU.mult)
                    tt(tB[:, 0:n], sinT[:, 0:n], pre[:, 0:n], ALU.mult)
                    tt(tD[:, 0:n], tA[:, 0:n], tB[:, 0:n], ALU.subtract)
                    dec = vv(sdec.h[:, j:j + 1].to_broadcast([128, n]), sdec[:])
                    for (cc, qq, ci) in ((tC, tE, 0), (tD, tF, 1)):
                        init = car[:, ci, j:j + 1]
                        S.add("dve", lambda e, cc=cc, qq=qq, init=init, dec=dec: e.tensor_tensor_scan(
                            out=qq.h[:, 0:n], data0=dec.ap, data1=cc.h[:, 0:n], initial=init.ap,
                            op0=ALU.mult, op1=ALU.add), reads=[cc[:], init, dec], writes=[qq[:]])
                    tt(tA[:, 0:n], cosT[:, 0:n], tE[:, 0:n], ALU.mult)
                    tt(tB[:, 0:n], sinT[:, 0:n], tF[:, 0:n], ALU.mult)
                    tt(tC[:, 0:n], tA[:, 0:n], tB[:, 0:n], ALU.subtract)
                    tt(tA[:, 0:n], sinT[:, 0:n], tE[:, 0:n], ALU.mult)
                    tt(tB[:, 0:n], cosT[:, 0:n], tF[:, 0:n], ALU.mult)
                    tt(tD[:, 0:n], tA[:, 0:n], tB[:, 0:n], ALU.add)
                    act(hre[:, 0:n], tC[:, 0:n], AF.Copy)
                    act(him[:, 0:n], tD[:, 0:n], AF.Copy, scale=-1.0)
                    cp(car[:, 0, j:j + 1], tC[:, n - 1:n])
                    cp(car[:, 1, j:j + 1], tD[:, n - 1:n])
                    mm(py[:, 0:n], wview(slC, 0, 128, j, 0, 128), hre[:, 0:n], jj == 0, False)
                    mm(py[:, 0:n], wview(slC, 4096, 128, j, 0, 128), him[:, 0:n], False, False)
                mm(py[:, 0:n], Dd.sub(c, (slice(None), c, slice(None))), slot(A_U + c, n), False, True)
                gelu_to(py, slot(A_ZA + c, n), n)

        def gelu_to(py, out, n):
            act(tA[:, 0:n], py[:, 0:n], AF.Square)
            ts(tA[:, 0:n], tA[:, 0:n], 0.044715, 1.0, ALU.mult, ALU.add)
            tt(tA[:, 0:n], tA[:, 0:n], py[:, 0:n], ALU.mult)
            act(tB[:, 0:n], tA[:, 0:n], AF.Sigmoid, scale=1.5957691216057308)
            tt(out, tB[:, 0:n], py[:, 0:n], ALU.mult)

        def ssm_sample():
            n = NS
            slB = wload([(0, 128, 32, Bscr_d[:, 0, :, :]), (4096, 128, 32, Bscr_d[:, 1, :, :])], extra_reads=[BSCR])
            slC = wload([(0, 128, 32, cre_d.rearrange("p (j m) -> p j m", j=32)),
                         (4096, 128, 32, cim_d.rearrange("p (j m) -> p j m", j=32))])
            for j in range(32):
                c = j // 4
                pre, pim = ps[2 * (j % 2)], ps[2 * (j % 2) + 1]
                mm(pre[:, 0:n], wview(slB, 0, 128, j, 0, 128), slot(A_U + c, n), True, True)
                mm(pim[:, 0:n], wview(slB, 4096, 128, j, 0, 128), slot(A_U + c, n), True, True)
                cp(tE[:, 16 * j:16 * j + 16], pre[:, 0:n])
                act(tF[:, 16 * j:16 * j + 16], pim[:, 0:n], AF.Copy)
            cp(hss[:], h0s[:])

            def v4(tX, t):
                return vv(tX.h[:].rearrange("p (j b t) -> p j b t", j=32, b=4)[:, :, :, t], tX[:])

            def v3(tX, a):
                return vv(tX.h[:, a:a + 128].rearrange("p (j b) -> p j b", j=32), tX[:])
            lr = vv(lbre.h[:, :].unsqueeze(2).to_broadcast([128, 32, 4]), lbre[:])
            li = vv(lbim.h[:, :].unsqueeze(2).to_broadcast([128, 32, 4]), lbim[:])
            A1, A2, A3, A4 = v3(tA, 0), v3(tA, 128), v3(tB, 0), v3(tB, 128)
            for t in range(4):
                hr = hss[:, 0, :, :]
                hi_ = hss[:, 1, :, :]
                tt(A1, hr, lr, ALU.mult)
                tt(A2, hi_, li, ALU.mult)
                tt(A1, A1, A2, ALU.subtract)
                tt(A3, hi_, lr, ALU.mult)
                tt(A4, hr, li, ALU.mult)
                tt(A3, A3, A4, ALU.add)
                tt(hr, A1, v4(tE, t), ALU.add)
                tt(hi_, A3, v4(tF, t), ALU.add)
                cp(v4(tC, t), hr)
                cp(v4(tD, t), hi_)
            act(hre[:], tC[:], AF.Copy)
            act(him[:], tD[:], AF.Copy, scale=-1.0)
            for c in range(8):
                py = ps[4 + (c % 2)]
                for jj in range(4):
                    j = 4 * c + jj
                    mm(py[:, 0:n], wview(slC, 0, 128, j, 0, 128), hre[:, 16 * j:16 * j + 16], jj == 0, False)
                    mm(py[:, 0:n], wview(slC, 4096, 128, j, 0, 128), him[:, 16 * j:16 * j + 16], False, False)
                mm(py[:, 0:n], Dd.sub(c, (slice(None), c, slice(None))), slot(A_U + c, n), False, True)
                gelu_to(py, slot(A_ZA + c, n), n)

        knew = sb("knew", [128, 2, NS], BF16)
        kinA = sb("kinA", [128, NS], BF16); kinB = sb("kinB", [128, NS], BF16)
        vnew = sb("vnew", [NS, 256], BF16)

        for blk in range(NBLK + 1):
            n = NB if blk < NBLK else NS
            cur["sample"] = (blk == NBLK)
            wblk["blk"] = blk
            wblk["k"] = 0
            c0 = blk * NB
            for j in range(NKC):
                dma("sp", xT.sub(j, (slice(None), j, slice(0, n))), xT_d[128 * j:128 * j + 128, c0:c0 + n],
                    writes=[xT.sub(j, (slice(None), j, slice(0, n)))])
            if stage_limit >= 1:
                rmsnorm(n, 0)
                ffn(n, w1g_d, w1u_d, w1d_d)
            if stage_limit >= 2:
                full = mix(blk, n, c0)
                if full and stage_limit >= 7:
                    rmsnorm(n, 32)
                    ffn(n, w2g_d, w2u_d, w2d_d)
            for j in range(NKC):
                dma("sp", yT_o[128 * j:128 * j + 128, c0:c0 + n], xT.sub(j, (slice(None), j, slice(0, n))),
                    reads=[xT.sub(j, (slice(None), j, slice(0, n)))])
        dma("sp", sp_o, vv(car.h[:].rearrange("p a j -> p (a j)"), car[:]), reads=[car[:]])
        dma("sp", ss_o, vv(hss.h[:].rearrange("p a j b -> p (a j b)"), hss[:]), reads=[hss[:]])

        S.emit_all(nc)
    return nc


def _rope_tables(pos):
    pos = np.asarray(pos, np.float32)
    T = pos.shape[0]
    out = np.zeros((4, 128, T), np.float32)
    out[0] = 1.0
    out[2] = 1.0
    half = 16
    fr = (np.float32(500000.0) ** (-np.arange(half, dtype=np.float32) * np.float32(2.0) / np.float32(32))).astype(np.float32)
    ang = (pos[:, None] * fr[None, :]).astype(np.float32)
    c, s = np.cos(ang).T.astype(np.float32), np.sin(ang).T.astype(np.float32)
    out[0, 0:16] = c; out[0, 16:32] = c
    out[1, 0:16] = -s; out[1, 16:32] = s
    half = 8
    fr = (np.float32(500000.0) ** (-np.arange(half, dtype=np.float32) * np.float32(2.0) / np.float32(16))).astype(np.float32)
    ang = (pos[:, None] * fr[None, :]).astype(np.float32)
    c, s = np.cos(ang).T.astype(np.float32), np.sin(ang).T.astype(np.float32)
    for b in (0, 64):
        out[2, b:b + 8] = c; out[2, b + 8:b + 16] = c
        out[3, b:b + 8] = -s; out[3, b + 8:b + 16] = s
    return out


def _perm_cols(w, width, half):
    n = w.shape[1]
    idx = np.arange(n)
    loc = idx % width
    src = idx.copy()
    src[loc < half] += half
    m = (loc >= half) & (loc < 2 * half)
    src[m] -= half
    return w[:, src]


_NC_CACHE = {}


def kernel(x_prompt, x_sample, cache_k, cache_v, cache_idx_k, state_ssm_re, state_ssm_im, page_table,
           ffn1_norm, ffn1_w_gate, ffn1_w_up, ffn1_w_down, mix_norm, w_in, q_norm, k_norm,
           ssm_lambda_re, ssm_lambda_im, ssm_b_re, ssm_b_im, ssm_c_re, ssm_c_im, ssm_d, ssm_log_dt,
           glu_w, glu_b, w_branch_a, w_branch_b, w_out, ffn2_norm, ffn2_w_gate, ffn2_w_up, ffn2_w_down):
    import os
    stage_limit = int(os.environ.get("KSTAGE", "99"))
    f32 = np.float32
    A = lambda a: np.ascontiguousarray(np.asarray(a), dtype=f32)
    win = A(w_in)[0]
    u_w = win[:, 0:1024]; q_w = win[:, 1024:2048]; k_w = win[:, 2048:2304]; v_w = win[:, 2304:2560]
    qi_w = win[:, 2560:3584]; ki_w = win[:, 3584:3648]; wi_w = win[:, 3648:3664]
    q_r = _perm_cols(q_w, 128, 16); k_r = _perm_cols(k_w, 128, 16)
    qi_r = _perm_cols(qi_w, 64, 8); ki_r = _perm_cols(ki_w, 64, 8)
    tiles = [u_w[:, 128 * i:128 * i + 128] for i in range(8)]
    for h in range(8):
        tiles += [q_w[:, 128 * h:128 * h + 128], q_r[:, 128 * h:128 * h + 128]]
    for g in range(2):
        tiles += [k_w[:, 128 * g:128 * g + 128], k_r[:, 128 * g:128 * g + 128]]
    for t in range(8):
        tiles += [qi_w[:, 128 * t:128 * t + 128], qi_r[:, 128 * t:128 * t + 128]]
    tiles += [np.concatenate([ki_w, ki_w], 1), np.concatenate([ki_r, ki_r], 1)]
    wmix = np.ascontiguousarray(np.concatenate(tiles, 1))
    assert wmix.shape[1] == WMIX_TILES * 128
    wtok = np.ascontiguousarray(np.concatenate([v_w, wi_w, np.zeros((D, 512 - 272), f32)], 1))
    wgab = np.ascontiguousarray(win[:, 3664:7760])
    pl = lambda g: np.ascontiguousarray(A(g).reshape(-1, 128).T)
    nrm = np.concatenate([pl(ffn1_norm[0]), pl(mix_norm[0]), pl(ffn2_norm[0])], 1)
    qn = A(q_norm)[0]; kn = A(k_norm)[0]
    qkn = np.stack([qn, _perm_cols(qn[None], 128, 16)[0], kn, _perm_cols(kn[None], 128, 16)[0]], 1)
    glb = pl(glu_b[0])
    ident = np.eye(128, dtype=f32)
    tri = np.where(np.arange(128)[None, :] <= np.arange(128)[:, None], 0.0, NEG).astype(f32)
    iota1 = np.broadcast_to(np.arange(1, NB + 1, dtype=f32)[None, :], (128, NB)).copy()
    def sl_(a):
        return np.ascontiguousarray(A(a).reshape(32, 2, 64).transpose(1, 2, 0).reshape(128, 32))
    ldt = np.repeat(A(ssm_log_dt)[0][:, None], 64, 1)
    ssmp = np.concatenate([sl_(ssm_lambda_re[0]), sl_(ssm_lambda_im[0]), sl_(ldt)], 1)
    ssmd = np.ascontiguousarray(A(ssm_d)[0].reshape(8, 128).T)
    def bs_(b):
        b = A(b).reshape(32, 2, 64, 16)
        o = np.zeros((2, 64, 32, 128), f32)
        for j in range(32):
            for g2 in range(2):
                k0 = 32 * (j % 4) + 16 * g2
                o[g2, :, j, k0:k0 + 16] = b[j, g2]
        return np.ascontiguousarray(o.reshape(128, 32 * 128))
    def cs_(c):
        c = A(c).reshape(32, 2, 16, 64)
        o = np.zeros((2, 64, 32, 128), f32)
        for j in range(32):
            for g2 in range(2):
                m0 = 32 * (j % 4) + 16 * g2
                o[g2, :, j, m0:m0 + 16] = c[j, g2].T
        return np.ascontiguousarray(o.reshape(128, 32 * 128))
    bsre = bs_(ssm_b_re[0]); bsim = bs_(ssm_b_im[0]); cre = cs_(ssm_c_re[0]); cim = cs_(ssm_c_im[0])

    xp = A(x_prompt); xs = A(x_sample)
    sre = A(state_ssm_re)[0]; sim = A(state_ssm_im)[0]
    shared = dict(w1g=A(ffn1_w_gate)[0], w1u=A(ffn1_w_up)[0], w1d=A(ffn1_w_down)[0],
                  w2g=A(ffn2_w_gate)[0], w2u=A(ffn2_w_up)[0], w2d=A(ffn2_w_down)[0],
                  wmix=wmix, wtok=wtok, wgab=wgab, glw=A(glu_w)[0], wa=A(w_branch_a)[0], wb=A(w_branch_b)[0],
                  wo=A(w_out)[0], nrm=np.ascontiguousarray(nrm), qkn=np.ascontiguousarray(qkn), glb=glb,
                  ident=ident, tri=tri, iota1=iota1, hmask=np.stack([(np.arange(128) < 64), (np.arange(128) >= 64)], 1).astype(f32), ssmp=np.ascontiguousarray(ssmp), ssmd=ssmd,
                  bsre=bsre, bsim=bsim, cre=cre, cim=cim)
    ckh = A(cache_k)[0].reshape(40960, 2048); cvh = A(cache_v)[0].reshape(40960, 2048)
    cikh = A(cache_idx_k)[0].reshape(40960, 512)
    shared.update(ckh=ckh, cvh=cvh, cikh=cikh)
    tq = np.arange(16)
    mnew = np.where((tq[None, :] // 4 == tq[:, None] // 4) & (tq[None, :] % 4 <= tq[:, None] % 4), 0.0, NEG).astype(f32)
    shared.update(mnew=mnew)
    pt_np = np.asarray(page_table).astype(np.int32)
    pos_s = np.tile(PAST + np.arange(4), 4)
    rope = _rope_tables(np.concatenate([np.arange(SEQ), pos_s]))
    in_maps = []
    for c in range(N_CORES):
        pb = c // 2
        sbs = slice(4 * c, 4 * c + 4)
        xT = np.concatenate([xp[pb].T, xs[sbs].reshape(NS, D).T], 1)
        def h0l(s):
            return s.reshape(4, 32, 2, 64).transpose(2, 3, 1, 0).reshape(128, 32 * 4)
        h0 = np.concatenate([h0l(sre[sbs]), h0l(sim[sbs])], 1)
        m = dict(shared)
        ptl_c = np.ascontiguousarray(np.concatenate([pt_np[sbs].T, pt_np[sbs].T], 0).astype(np.int32))
        m.update(xT=np.ascontiguousarray(xT), rope=rope, h0=np.ascontiguousarray(h0), ptl=ptl_c)
        in_maps.append(m)

    import time as _t
    _t0 = _t.time()
    key = stage_limit
    if key not in _NC_CACHE:
        _NC_CACHE[key] = build_nc(stage_limit)
    nc = _NC_CACHE[key]
    _t1 = _t.time()
    res = run_bass_kernel_spmd(nc, in_maps, core_ids=list(range(N_CORES)))
    if os.environ.get("KTIME"):
        print("[kernel] build %.1fs run %.1fs" % (_t1 - _t0, _t.time() - _t1), flush=True)
    R = res.results
    y_p = np.stack([R[2 * b]["yT"][:, :SEQ].T for b in range(4)])
    y_s = np.concatenate([R[c]["yT"][:, SEQ:].T.reshape(4, 4, D) for c in range(N_CORES)])
    k_p = np.stack([R[2 * b]["kTo"][:, :SEQ].T.reshape(SEQ, 2, 128) for b in range(4)])[None]
    v_p = np.stack([R[2 * b]["vTo"][:, :SEQ].T.reshape(SEQ, 2, 128) for b in range(4)])[None]
    ik_p = np.stack([R[2 * b]["kiTo"][:64, :SEQ].T for b in range(4)])[None]
    def st_p(r, a):
        return r["ssp"][:, 32 * a:32 * a + 32].reshape(2, 64, 32).transpose(2, 0, 1).reshape(64, 64)
    re_p = np.stack([st_p(R[2 * b], 0) for b in range(4)])[None]
    im_p = np.stack([st_p(R[2 * b], 1) for b in range(4)])[None]
    k_s = np.concatenate([R[c]["kTo"][:, SEQ:].T.reshape(4, 4, 2, 128) for c in range(N_CORES)])[None]
    v_s = np.concatenate([R[c]["vTo"][:, SEQ:].T.reshape(4, 4, 2, 128) for c in range(N_CORES)])[None]
    ik_s = np.concatenate([R[c]["kiTo"][:64, SEQ:].T.reshape(4, 4, 64) for c in range(N_CORES)])[None]
    def st_s(r, a):
        return r["sss"][:, 128 * a:128 * a + 128].reshape(2, 64, 32, 4).transpose(3, 2, 0, 1).reshape(4, 64, 64)
    re_s = np.concatenate([st_s(R[c], 0) for c in range(N_CORES)])[None]
    im_s = np.concatenate([st_s(R[c], 1) for c in range(N_CORES)])[None]
    outs = (y_p, y_s, k_p, v_p, ik_p, re_p, im_p, k_s, v_s, ik_s, re_s, im_s)
    return tuple(np.ascontiguousarray(o, dtype=np.float32) for o in outs)
```

```python
import contextlib
import numpy as np
import ml_dtypes
import concourse.bass as bass
import concourse.mybir as mybir
from concourse.bass_utils import run_bass_kernel_spmd

F32 = mybir.dt.float32
BF16 = mybir.dt.bfloat16
I32 = mybir.dt.int32
ALU = mybir.AluOpType
AF = mybir.ActivationFunctionType
AX = mybir.AxisListType

D = 2048
DFF = 5632
NKC = D // 128
NFF = DFF // 128
SEQ = 2048
NB = 512
NBLK = SEQ // NB
NS = 16
NTOK = SEQ + NS
EPS = 1e-6
NEG = -1.0e30
TOPK = 256
PAST = 8192
NPAGE = 64
N_CORES = 8


class V:
    __slots__ = ("ap", "keys")

    def __init__(self, ap, keys):
        self.ap = ap
        self.keys = keys


class Buf:
    def __init__(self, handle, name):
        self.h = handle
        self.name = name

    def __getitem__(self, idx):
        return V(self.h[idx], ((self.name, None),))

    def sub(self, k, idx):
        return V(self.h[idx], ((self.name, k),))

    def multi(self, ks, idx):
        return V(self.h[idx], tuple((self.name, k) for k in ks))


def vv(ap, *views):
    keys = tuple(k for v in views for k in v.keys)
    return V(ap, keys)


class Op:
    __slots__ = ("idx", "eng", "emit", "dma", "deps", "inc", "cnt", "sem", "val", "prev_val")

    def __init__(self, idx, eng, emit, dma):
        self.idx = idx
        self.eng = eng
        self.emit = emit
        self.dma = dma
        self.deps = set()
        self.inc = False
        self.cnt = 0
        self.sem = None
        self.val = 0
        self.prev_val = 0


class Sched:
    def __init__(self):
        self.ops = []
        self.last_writer = {}
        self.readers = {}

    @staticmethod
    def _conf(d, name, sub):
        e = d.get(name)
        if not e:
            return []
        if sub is None:
            return list(e.values())
        out = []
        if sub in e:
            out.append(e[sub])
        if None in e:
            out.append(e[None])
        return out

    def add(self, eng, emit, reads=(), writes=(), dma=False):
        op = Op(len(self.ops), eng, emit, dma)
        rk = [k for v in reads if isinstance(v, V) for k in v.keys]
        wk = [k for v in writes if isinstance(v, V) for k in v.keys]
        for (name, sub) in rk:
            for w in self._conf(self.last_writer, name, sub):
                op.deps.add(w)
            if name.startswith("ps"):
                for rs in self._conf(self.readers, name, sub):
                    op.deps |= {r for r in rs if r.eng != eng}
        for (name, sub) in wk:
            for w in self._conf(self.last_writer, name, sub):
                op.deps.add(w)
            for rs in self._conf(self.readers, name, sub):
                op.deps |= rs
        for (name, sub) in rk:
            self.readers.setdefault(name, {}).setdefault(sub, set()).add(op)
        for (name, sub) in wk:
            lw = self.last_writer.setdefault(name, {})
            rd = self.readers.setdefault(name, {})
            if sub is None:
                lw.clear()
                rd.clear()
            lw[sub] = op
            rd[sub] = set()
        op.deps.discard(op)
        self.ops.append(op)
        return op

    def emit_all(self, nc, nsem_dma=14):
        engs = ["pe", "act", "dve", "pool", "sp"]
        per = {e: [] for e in engs}
        for op in self.ops:
            per[op.eng].append(op)
        for op in self.ops:
            for d in op.deps:
                if d.dma:
                    continue
                if d.eng == "pe" and op.eng == "pe" and not op.dma:
                    continue
                d.inc = True
        cnt = {e: 0 for e in engs}
        for op in self.ops:
            if not op.dma and op.inc:
                cnt[op.eng] += 1
                op.cnt = cnt[op.eng]
        with contextlib.ExitStack() as st:
            csem = {e: st.enter_context(nc.semaphore("c_" + e)) for e in ["pe", "act", "dve", "pool"]}
            dsem = {e: [st.enter_context(nc.semaphore("d_%s%d" % (e, i))) for i in range(nsem_dma)]
                    for e in ["pool", "sp"]}
            duse = {e: [0] * nsem_dma for e in dsem}
            dnext = {e: 0 for e in dsem}
            for op in self.ops:
                if op.dma:
                    i = dnext[op.eng]
                    dnext[op.eng] = (i + 1) % nsem_dma
                    op.sem = dsem[op.eng][i]
                    op.prev_val = duse[op.eng][i]
                    duse[op.eng][i] += 16
                    op.val = duse[op.eng][i]
            block = st.enter_context(nc.Block())

            def run(engname, e):
                waited = {}

                def w(sem, val):
                    if val <= 0 or waited.get(sem.name, 0) >= val:
                        return
                    waited[sem.name] = val
                    e.wait_ge(sem, val)

                for op in per[engname]:
                    for d in sorted(op.deps, key=lambda o: o.idx):
                        if d.dma:
                            w(d.sem, d.val)
                        else:
                            if d.eng == "pe" and engname == "pe" and not op.dma:
                                continue
                            w(csem[d.eng], d.cnt)
                    if op.dma:
                        w(op.sem, op.prev_val)
                        ins = op.emit(e)
                        ins.then_inc(op.sem, 16)
                    else:
                        ins = op.emit(e)
                        if op.inc:
                            ins.then_inc(csem[engname], 1)
                if engname in dsem:
                    for i, s in enumerate(dsem[engname]):
                        w(s, duse[engname][i])

            @block.tensor
            def _(e):
                run("pe", e)

            @block.scalar
            def _(e):
                run("act", e)

            @block.vector
            def _(e):
                run("dve", e)

            @block.gpsimd
            def _(e):
                run("pool", e)

            @block.sync
            def _(e):
                run("sp", e)


WMIX_TILES = 46


def build_nc(stage_limit=99):
    import os as _os
    KSUB = int(_os.environ.get("KSUB", "99"))
    KATT = int(_os.environ.get("KATT", "1"))
    KV = int(_os.environ.get("KV", "3"))
    nc = bass.Bass("TRN2", target_bir_lowering=False)

    def din(name, shape, dt=F32):
        return nc.dram_tensor(name, list(shape), dt, kind="ExternalInput").ap()

    def dout(name, shape, dt=F32):
        return nc.dram_tensor(name, list(shape), dt, kind="ExternalOutput").ap()

    xT_d = din("xT", [D, NTOK])
    w1g_d = din("w1g", [D, DFF]); w1u_d = din("w1u", [D, DFF]); w1d_d = din("w1d", [DFF, D])
    w2g_d = din("w2g", [D, DFF]); w2u_d = din("w2u", [D, DFF]); w2d_d = din("w2d", [DFF, D])
    wmix_d = din("wmix", [D, WMIX_TILES * 128])
    wtok_d = din("wtok", [D, 512])
    wgab_d = din("wgab", [D, 4096])
    glw_d = din("glw", [1024, 1024]); wa_d = din("wa", [1024, D]); wb_d = din("wb", [1024, D]); wo_d = din("wo", [D, D])
    nrm_d = din("nrm", [128, 3 * 16])
    qkn_d = din("qkn", [128, 4])
    glb_d = din("glb", [128, 8])
    rope_d = din("rope", [4, 128, NTOK])
    ident_d = din("ident", [128, 128]); tri_d = din("tri", [128, 128]); iota_d = din("iota1", [128, NB])
    ssmp_d = din("ssmp", [128, 3 * 32])
    ssmd_d = din("ssmd", [128, 8])
    bsre_d = din("bsre", [128, 32 * 128]); bsim_d = din("bsim", [128, 32 * 128])
    cre_d = din("cre", [128, 32 * 128]); cim_d = din("cim", [128, 32 * 128])
    h0_d = din("h0", [128, 2 * 32 * 4])
    hmask_d = din("hmask", [128, 2])
    ck_d = din("ckh", [40960, 2048]); cv_d = din("cvh", [40960, 2048]); cik_d = din("cikh", [40960, 512])
    ptl_d = din("ptl", [128, 4], I32)
    mnew_d = din("mnew", [16, 16])

    yT_o = dout("yT", [D, NTOK])
    kT_o = dout("kTo", [256, NTOK])
    vT_o = dout("vTo", [256, NTOK])
    kiT_o = dout("kiTo", [128, NTOK])
    sp_o = dout("ssp", [128, 2 * 32])
    ss_o = dout("sss", [128, 2 * 32 * 4])

    S = Sched()
    with contextlib.ExitStack() as st:
        def sb(name, shape, dt):
            return Buf(st.enter_context(nc.sbuf_tensor("s_" + name, list(shape), dt)), name)

        def psb(name, shape, dt):
            return Buf(st.enter_context(nc.psum_tensor("p_" + name, list(shape), dt)), name)

        xT = sb("xTs", [128, NKC, NB], F32)
        hT = sb("hT", [128, NKC, NB], BF16)
        arena = sb("arena", [128, NFF, NB], BF16)
        wsl = [sb("wsl%d" % i, [128, 8192], BF16) for i in range(2)]
        ps = [psb("ps%d" % i, [128, 512], F32) for i in range(8)]
        kTs = sb("kTs", [128, 2, SEQ], BF16)
        vS = sb("vS", [128, 16, 256], BF16)
        kiA = sb("kiA", [128, SEQ], BF16)
        kiB = sb("kiB", [128, SEQ], BF16)
        Dd = sb("Dd", [128, 8, 128], BF16)
        identb = sb("identb", [128, 128], BF16); identf = sb("identf", [128, 128], F32)
        onesb = sb("onesb", [128, 128], BF16)
        tri = sb("tri", [128, 128], F32)
        hmask = sb("hmask", [128, 2], F32)
        iota1 = sb("iota1s", [128, NB], F32)
        nrm = sb("nrm", [128, 48], F32); qkn = sb("qkn", [128, 4], F32); glb = sb("glb", [128, 8], F32)
        ssmp = sb("ssmp", [128, 96], F32); ssmd = sb("ssmd", [128, 8], F32)
        sdec = sb("sdec", [128, 32], F32)
        sfrq = sb("sfrq", [128, 32], F32)
        lbre = sb("lbre", [128, 32], F32); lbim = sb("lbim", [128, 32], F32)
        gre = sb("gre", [128, 32], F32); gim = sb("gim", [128, 32], F32)
        stp = sb("stp", [128, 8, 32], F32)
        car = sb("car", [128, 2, 32], F32)
        h0s = sb("h0s", [128, 2, 32, 4], F32)
        hss = sb("hss", [128, 2, 32, 4], F32)
        sq = sb("sq", [128, 4, NB], BF16)
        rstd = sb("rstd", [128, NB], F32)
        tA = sb("tA", [128, NB], F32); tB = sb("tB", [128, NB], F32); tC = sb("tC", [128, NB], F32)
        tD = sb("tD", [128, NB], F32); tE = sb("tE", [128, NB], F32); tF = sb("tF", [128, NB], F32)
        tI = sb("tI", [128, NB], I32)
        cosT = sb("cosT", [128, NB], F32); sinT = sb("sinT", [128, NB], F32)
        hre = sb("hre", [128, NB], BF16); him = sb("him", [128, NB], BF16)
        mskT = sb("mskT", [128, 16, 128], BF16)
        rl = [sb("rl%d" % i, [128, NB], BF16) for i in range(2)]
        pe_ = [sb("pe%d" % i, [128, NB], BF16) for i in range(2)]
        Dm = sb("Dm", [128, 16, 128], BF16)
        wtk = sb("wtk", [128, 4, 16], F32)
        bs = sb("bs", [128, 8], F32)
        sA = sb("sA", [128, NFF, NS], BF16)
        sQI = sb("sQI", [128, 16, NS], BF16)
        DmS = sb("DmS", [16, 16, 16], BF16)
        ptl = sb("ptl", [128, 4], I32); idxh = sb("idxh", [128, 4], I32); idx8 = sb("idx8", [128, 4, 8], I32)
        mnew = sb("mnew", [16, 16], F32)
        bs2 = sb("bs2", [16, 8], F32)
        kst = sb("kst", [128, NB], F32)

        cur = {"sample": False}

        def slot(i, n=NB):
            if cur["sample"]:
                return sA.sub(i, (slice(None), i, slice(0, n)))
            return arena.sub(i, (slice(None), i, slice(0, n)))

        def slotc(i, a, b):
            return arena.sub(i, (slice(None), i, slice(a, b)))

        IscAP = arena.h[:, 24:32, :].rearrange("p s n -> p (s n)").bitcast(F32)
        mskAP = arena.h[:, 40:44, :].rearrange("p s n -> p (s n)")
        ISC_KEYS = arena.multi(range(24, 32), (slice(None), slice(24, 32), slice(None)))
        MSK_KEYS = arena.multi(range(40, 44), (slice(None), slice(40, 44), slice(None)))

        def Isc(a, b):
            return vv(IscAP[:, a:b], ISC_KEYS)

        def mskv(a, b):
            return vv(mskAP[:, a:b], MSK_KEYS)

        ps6b = ps[7].h[:].bitcast(BF16)

        A_U, A_Q, A_QI, A_ZA, A_BO = 0, 8, 16, 24, 32

        def dma(eng, out, in_, reads=(), writes=()):
            S.add(eng, lambda e: e.dma_start(out=out.ap if isinstance(out, V) else out,
                                             in_=in_.ap if isinstance(in_, V) else in_),
                  reads=list(reads), writes=list(writes), dma=True)

        def mm(out, lhsT, rhs, start, stop):
            S.add("pe", lambda e: e.matmul(out.ap, lhsT=lhsT.ap, rhs=rhs.ap, start=start, stop=stop),
                  reads=[lhsT, rhs], writes=[out])

        def act(out, in_, func, scale=1.0, bias=0.0, accum=None, extra_reads=()):
            kw = {}
            if accum is not None:
                kw["accum_out"] = accum.ap
            b = bias.ap if isinstance(bias, V) else bias
            sc = scale.ap if isinstance(scale, V) else scale
            S.add("act", lambda e: e.activation(out=out.ap, in_=in_.ap, func=func, bias=b, scale=sc, **kw),
                  reads=[in_, bias, scale] + list(extra_reads), writes=[out] + ([accum] if accum is not None else []))

        def tt(out, in0, in1, op, eng="dve"):
            S.add(eng, lambda e: e.tensor_tensor(out=out.ap, in0=in0.ap, in1=in1.ap, op=op),
                  reads=[in0, in1], writes=[out])

        def ts(out, in0, s1, s2, op0, op1=None, accum=None, eng="dve"):
            a1 = s1.ap if isinstance(s1, V) else s1
            a2 = s2.ap if isinstance(s2, V) else s2
            kw = {}
            if op1 is not None:
                kw["op1"] = op1
            if accum is not None:
                kw["accum_out"] = accum.ap
            S.add(eng, lambda e: e.tensor_scalar(out=out.ap, in0=in0.ap, scalar1=a1, scalar2=a2, op0=op0, **kw),
                  reads=[in0, s1, s2], writes=[out] + ([accum] if accum is not None else []))

        def stt(out, in0, scalar, in1, op0, op1, eng="dve"):
            a = scalar.ap if isinstance(scalar, V) else scalar
            S.add(eng, lambda e: e.scalar_tensor_tensor(out=out.ap, in0=in0.ap, scalar=a, in1=in1.ap, op0=op0, op1=op1),
                  reads=[in0, scalar, in1], writes=[out])

        def cp(out, in_, eng="dve"):
            S.add(eng, lambda e: e.tensor_copy(out=out.ap, in_=in_.ap), reads=[in_], writes=[out])

        def memset(out, val, eng="dve"):
            S.add(eng, lambda e: e.memset(out.ap, val), writes=[out])

        def recip(out, in_):
            S.add("dve", lambda e: e.reciprocal(out=out.ap, in_=in_.ap), reads=[in_], writes=[out])

        def transpose(out, in_, ident):
            S.add("pe", lambda e: e.transpose(out.ap, in_.ap, ident.ap), reads=[in_, ident], writes=[out])

        wstate = {"n": 0}

        NSLAB = 128
        Wscr_d = nc.dram_tensor("Wscr", [NSLAB, 128, 8192], BF16).ap()
        wblk = {"blk": 0, "k": 0}

        def aq():
            return "sp" if wblk["blk"] == 0 else "pool"

        def wload(parts, extra_reads=(), used=None):
            sl = wsl[wstate["n"] % 2]
            wstate["n"] += 1
            k = wblk["k"]
            wblk["k"] += 1
            assert k < NSLAB
            skey = V(None, (("Wscr", k),))
            if wblk["blk"] == 0:
                for (c0, ncols, nkc, src) in parts:
                    dst = sl.h[:, c0:c0 + nkc * ncols].rearrange("p (k c) -> p k c", k=nkc)
                    step = max(1, min(nkc, 8))
                    for k0 in range(0, nkc, step):
                        k1 = min(nkc, k0 + step)
                        S.add("pool", lambda e, d=(dst[:, k0:k1, :] if used is None else dst[:, k0:k1, 0:used]), s_=src[:, k0:k1, :]: e.dma_start(out=d, in_=s_),
                              reads=list(extra_reads), writes=[sl[:]], dma=True)
                S.add("sp", lambda e, sl=sl, k=k: e.dma_start(out=Wscr_d[k, :, :], in_=sl.h[:, :]), reads=[sl[:]], writes=[skey], dma=True)
            else:
                S.add("sp", lambda e, sl=sl, k=k: e.dma_start(out=sl.h[:, :], in_=Wscr_d[k, :, :]), reads=[skey], writes=[sl[:]], dma=True)
            return sl

        def wview(sl, c0, ncols, kc, a, b):
            o = c0 + kc * ncols
            return sl[:, o + a:o + b]

        def wsrc(w_d, nkc, c0, c1):
            return w_d.rearrange("(k p) c -> p k c", p=128)[:, :, c0:c1]

        dma("sp", identf[:], ident_d, writes=[identf[:]])
        dma("pool", identb[:], ident_d, writes=[identb[:]])
        dma("sp", tri[:], tri_d, writes=[tri[:]])
        dma("sp", iota1[:], iota_d, writes=[iota1[:]])
        dma("sp", nrm[:], nrm_d, writes=[nrm[:]])
        dma("sp", qkn[:], qkn_d, writes=[qkn[:]])
        dma("sp", glb[:], glb_d, writes=[glb[:]])
        dma("sp", hmask[:], hmask_d, writes=[hmask[:]])
        dma("sp", ptl[:], ptl_d, writes=[ptl[:]])
        dma("sp", mnew[:], mnew_d, writes=[mnew[:]])
        ts(idxh[:], ptl[:], 2.0, None, ALU.mult)
        ts(idxh[:], idxh[:], hmask[:, 1:2], None, ALU.add)
        for ch in range(8):
            ts(idx8[:, :, ch], idxh[:], 8.0, float(ch), ALU.mult, ALU.add)
        dma("sp", ssmp[:], ssmp_d, writes=[ssmp[:]])
        dma("sp", ssmd[:], ssmd_d, writes=[ssmd[:]])
        dma("sp", vv(h0s.h[:].rearrange("p a j b -> p (a j b)"), h0s[:]), h0_d, writes=[h0s[:]])
        memset(onesb[:], 1.0)
        memset(wsl[0][:], 0.0)
        memset(wsl[1][:], 0.0)
        memset(hre[:], 0.0)
        memset(him[:], 0.0)
        memset(car[:], 0.0)
        for c in range(8):
            ts(Dd.sub(c, (slice(None), c, slice(None))), identf[:], ssmd[:, c:c + 1], None, ALU.mult)

        lre = ssmp[:, 0:32]; lim = ssmp[:, 32:64]; ldt = ssmp[:, 64:96]

        def T(i):
            return stp.sub(i, (slice(None), i, slice(None)))
        act(T(0), ldt, AF.Exp)
        tt(T(1), lre, T(0), ALU.mult)
        tt(T(2), lim, T(0), ALU.mult)
        act(sdec[:], T(1), AF.Exp)
        ts(T(3), T(2), float(1.0 / (2 * np.pi)), None, ALU.mult)
        stpi = sb("stpi", [128, 32], I32)
        cp(stpi[:], T(3))
        stt(sfrq[:], stpi[:], -1.0, T(3), ALU.mult, ALU.add)
        act(T(4), sfrq[:], AF.Sin, scale=float(2 * np.pi))
        ts(T(5), sfrq[:], 0.25, None, ALU.add)
        cp(stpi[:], T(5))
        stt(T(5), stpi[:], -1.0, T(5), ALU.mult, ALU.add)
        act(T(5), T(5), AF.Sin, scale=float(2 * np.pi))
        tt(lbre[:], sdec[:], T(5), ALU.mult)
        tt(lbim[:], sdec[:], T(4), ALU.mult)
        ts(T(6), lbre[:], -1.0, None, ALU.add)
        tt(T(0), lre, lre, ALU.mult)
        tt(T(1), lim, lim, ALU.mult)
        tt(T(0), T(0), T(1), ALU.add)
        recip(T(0), T(0))
        tt(T(1), T(6), lre, ALU.mult)
        tt(T(2), lbim[:], lim, ALU.mult)
        tt(T(1), T(1), T(2), ALU.add)
        tt(gre[:], T(1), T(0), ALU.mult)
        tt(T(1), lbim[:], lre, ALU.mult)
        tt(T(2), T(6), lim, ALU.mult)
        tt(T(1), T(1), T(2), ALU.subtract)
        tt(gim[:], T(1), T(0), ALU.mult)
        Bscr_d = nc.dram_tensor("Bscr", [128, 2, 32, 128], BF16).ap()
        BSCR = V(None, (("Bscr", None),))
        for ch in range(4):
            js = slice(8 * ch, 8 * ch + 8)

            def f32v(s0):
                ks = list(range(s0, s0 + 4))
                v = arena.multi(ks, (slice(None), slice(s0, s0 + 4), slice(None)))
                return vv(v.ap.rearrange("p s n -> p (s n)").bitcast(F32).rearrange("p (j m) -> p j m", j=8), v)
            bre_raw = f32v(0); bim_raw = f32v(4); o_re = f32v(8); o_im = f32v(12); tmp = f32v(16)
            dma("sp", bre_raw, bsre_d.rearrange("p (j m) -> p j m", j=32)[:, js, :], writes=[bre_raw])
            dma("sp", bim_raw, bsim_d.rearrange("p (j m) -> p j m", j=32)[:, js, :], writes=[bim_raw])
            g_re_b = vv(gre.h[:, js].unsqueeze(2).to_broadcast([128, 8, 128]), gre[:])
            g_im_b = vv(gim.h[:, js].unsqueeze(2).to_broadcast([128, 8, 128]), gim[:])
            tt(o_re, bre_raw, g_re_b, ALU.mult)
            tt(tmp, bim_raw, g_im_b, ALU.mult)
            tt(o_re, o_re, tmp, ALU.subtract)
            tt(o_im, bim_raw, g_re_b, ALU.mult)
            tt(tmp, bre_raw, g_im_b, ALU.mult)
            tt(o_im, o_im, tmp, ALU.add)
            stg = [vv(sq.h[:, 0:2, :].rearrange("p a n -> p (a n)").rearrange("p (j m) -> p j m", j=8), sq[:]),
                   vv(sq.h[:, 2:4, :].rearrange("p a n -> p (a n)").rearrange("p (j m) -> p j m", j=8), sq[:])]
            for jj in range(8):
                for (ri, src, pb) in ((0, o_re, ps[0]), (1, o_im, ps[1])):
                    transpose(pb[:, 0:128], vv(src.ap[:, jj, :], src), identf[:])
                    act(vv(stg[ri].ap[:, jj, :], sq[:]), pb[:, 0:128], AF.Copy)
            for ri in range(2):
                dma("sp", Bscr_d[:, ri, js, :], stg[ri], reads=[sq[:]], writes=[BSCR])

        def rmsnorm(n, gcol0):
            for grp in range(4):
                src = xT.multi(range(4 * grp, 4 * grp + 4), (slice(None), slice(4 * grp, 4 * grp + 4), slice(0, n)))
                act(sq[:, :, 0:n], src, AF.Square)
                for i in range(4):
                    mm(ps[6][:, 0:n], onesb[:], sq[:, i, 0:n], start=(grp == 0 and i == 0), stop=(grp == 3 and i == 3))
            act(rstd[:, 0:n], ps[6][:, 0:n], AF.Sqrt, scale=1.0 / D, bias=EPS)
            recip(rstd[:, 0:n], rstd[:, 0:n])
            for j in range(NKC):
                stt(hT.sub(j, (slice(None), j, slice(0, n))), xT.sub(j, (slice(None), j, slice(0, n))),
                    nrm[:, gcol0 + j:gcol0 + j + 1], rstd[:, 0:n], ALU.mult, ALU.mult)

        def ffn(n, wg_d, wu_d, wd_d):
            hall = lambda kc: hT[:, kc, 0:n]
            for s in range(NFF // 2):
                sl = wload([(0, 256, NKC, wsrc(wg_d, NKC, 256 * s, 256 * s + 256)),
                            (4096, 256, NKC, wsrc(wu_d, NKC, 256 * s, 256 * s + 256))])
                for half in range(2):
                    f = 2 * s + half
                    pg = ps[2 * (f % 2)]; pu = ps[2 * (f % 2) + 1]
                    for kc in range(NKC):
                        mm(pg[:, 0:n], wview(sl, 0, 256, kc, 128 * half, 128 * half + 128), hall(kc), kc == 0, kc == NKC - 1)
                    for kc in range(NKC):
                        mm(pu[:, 0:n], wview(sl, 4096, 256, kc, 128 * half, 128 * half + 128), hall(kc), kc == 0, kc == NKC - 1)
                    tmp = tA if f % 2 == 0 else tB
                    act(tmp[:, 0:n], pg[:, 0:n], AF.Silu)
                    tt(slot(f, n), tmp[:, 0:n], pu[:, 0:n], ALU.mult)
            for j in range(NKC):
                sl = wload([(0, 128, NFF, wsrc(wd_d, NFF, 128 * j, 128 * j + 128))])
                pd = ps[4 + (j % 2)]
                for kc in range(NFF):
                    mm(pd[:, 0:n], wview(sl, 0, 128, kc, 0, 128), slot(kc, n), kc == 0, kc == NFF - 1)
                xj = xT.sub(j, (slice(None), j, slice(0, n)))
                stt(xj, pd[:, 0:n], 0.5, xj, ALU.mult, ALU.add)

        def proj_pair(sl, t0, n, pa, pb):
            for (tix, pp) in ((t0, pa), (t0 + 1, pb)):
                for kc in range(NKC):
                    mm(pp[:, 0:n], wview(sl, 0, 512, kc, 128 * tix, 128 * tix + 128), hT[:, kc, 0:n], kc == 0, kc == NKC - 1)

        def mix_slab(i):
            return wload([(0, 512, NKC, wsrc(wmix_d, NKC, 512 * i, 512 * i + 512))])

        def mix(blk, n, c0):
            sample = blk == NBLK
            rmsnorm(n, 16)
            cosK = tE[:, 0:n]; sinK = tF[:, 0:n]; cosI = tE[:, 0:n]; sinI = tF[:, 0:n]
            dma(aq(), tE[:, 0:n], rope_d[0, :, c0:c0 + n], writes=[tE[:]])
            dma(aq(), tF[:, 0:n], rope_d[1, :, c0:c0 + n], writes=[tF[:]])
            for i in range(2):
                sl = mix_slab(i)
                for t in range(4):
                    pp = ps[t % 4]
                    for kc in range(NKC):
                        mm(pp[:, 0:n], wview(sl, 0, 512, kc, 128 * t, 128 * t + 128), hT[:, kc, 0:n], kc == 0, kc == NKC - 1)
                    act(slot(A_U + 4 * i + t, n), pp[:, 0:n], AF.Copy)

            def normrope(pa, pb, gcol, out_bf, out_f32=None):
                act(sq[:, 0, 0:n], pa[:, 0:n], AF.Square)
                mm(ps[6][:, 0:n], onesb[:], sq[:, 0, 0:n], True, True)
                act(rstd[:, 0:n], ps[6][:, 0:n], AF.Sqrt, scale=1.0 / 128, bias=EPS)
                recip(rstd[:, 0:n], rstd[:, 0:n])
                stt(tC[:, 0:n], pa[:, 0:n], qkn[:, gcol:gcol + 1], rstd[:, 0:n], ALU.mult, ALU.mult)
                tt(tC[:, 0:n], tC[:, 0:n], cosK, ALU.mult)
                stt(tD[:, 0:n], pb[:, 0:n], qkn[:, gcol + 1:gcol + 2], rstd[:, 0:n], ALU.mult, ALU.mult)
                tt(tD[:, 0:n], tD[:, 0:n], sinK, ALU.mult)
                if out_f32 is not None:
                    tt(out_f32, tC[:, 0:n], tD[:, 0:n], ALU.add)
                    act(out_bf, out_f32, AF.Copy)
                else:
                    tt(out_bf, tC[:, 0:n], tD[:, 0:n], ALU.add)

            if KSUB < 2:
                return False
            for i in range(4):
                sl = mix_slab(2 + i)
                for hh in range(2):
                    h = 2 * i + hh
                    pa, pb = ps[2 * hh], ps[2 * hh + 1]
                    proj_pair(sl, 2 * hh, n, pa, pb)
                    normrope(pa, pb, 0, slot(A_Q + h, n))
            if KSUB < 3:
                return False
            sl = mix_slab(6)
            for g in range(2):
                pa, pb = ps[2 * g], ps[2 * g + 1]
                proj_pair(sl, 2 * g, n, pa, pb)
                if not sample:
                    kdst = kTs.sub(blk, (slice(None), g, slice(c0, c0 + n)))
                else:
                    kdst = knew.sub(g, (slice(None), g, slice(0, n)))
                normrope(pa, pb, 2, kdst, out_f32=kst[:, 0:n])
                dma(aq(), kT_o[128 * g:128 * g + 128, c0:c0 + n], kst[:, 0:n], reads=[kst[:]])
            if KSUB < 4:
                return False
            dma(aq(), tE[:, 0:n], rope_d[2, :, c0:c0 + n], writes=[tE[:]])
            dma(aq(), tF[:, 0:n], rope_d[3, :, c0:c0 + n], writes=[tF[:]])
            for i in range(4):
                sl = mix_slab(7 + i)
                for hh in range(2):
                    t = 2 * i + hh
                    pa, pb = ps[2 * hh], ps[2 * hh + 1]
                    if not sample:
                        proj_pair(sl, 2 * hh, n, pa, pb)
                        tt(tC[:, 0:n], pa[:, 0:n], cosI, ALU.mult)
                        tt(tD[:, 0:n], pb[:, 0:n], sinI, ALU.mult)
                        tt(slot(A_QI + t, n), tC[:, 0:n], tD[:, 0:n], ALU.add)
                    else:
                        for a in range(2):
                            for (tix, pp) in ((2 * hh, pa), (2 * hh + 1, pb)):
                                for kc in range(NKC):
                                    mm(pp[0:64, 0:n], wview(sl, 0, 512, kc, 128 * tix + 64 * a, 128 * tix + 64 * a + 64),
                                       hT[:, kc, 0:n], kc == 0, kc == NKC - 1)
                            tt(tC[0:64, 0:n], pa[0:64, 0:n], tE[0:64, 0:n], ALU.mult)
                            tt(tD[0:64, 0:n], pb[0:64, 0:n], tF[0:64, 0:n], ALU.mult)
                            tt(sQI[0:64, 2 * t + a, 0:n], tC[0:64, 0:n], tD[0:64, 0:n], ALU.add)
            if KSUB < 5:
                return False
            sl = wload([(0, 256, NKC, wsrc(wmix_d, NKC, 44 * 128, 46 * 128))])
            pa, pb = ps[0], ps[1]
            for (tix, pp) in ((0, pa), (1, pb)):
                for kc in range(NKC):
                    mm(pp[:, 0:n], wview(sl, 0, 256, kc, 128 * tix, 128 * tix + 128), hT[:, kc, 0:n], kc == 0, kc == NKC - 1)
            tt(tC[:, 0:n], pa[:, 0:n], cosI, ALU.mult)
            tt(tD[:, 0:n], pb[:, 0:n], sinI, ALU.mult)
            tt(kst[:, 0:n], tC[:, 0:n], tD[:, 0:n], ALU.add)
            dma(aq(), kiT_o[:, c0:c0 + n], kst[:, 0:n], reads=[kst[:]])
            if not sample:
                ts(kiA.sub(blk, (slice(None), slice(c0, c0 + n))), kst[:, 0:n], hmask[:, 0:1], None, ALU.mult)
                ts(kiB.sub(blk, (slice(None), slice(c0, c0 + n))), kst[:, 0:n], hmask[:, 1:2], None, ALU.mult)
            else:
                ts(kinA[:, 0:n], kst[:, 0:n], hmask[:, 0:1], None, ALU.mult)
                ts(kinB[:, 0:n], kst[:, 0:n], hmask[:, 1:2], None, ALU.mult)
            if KSUB < 6:
                return False
            sl = wload([(0, 512, NKC, wsrc(wtok_d, NKC, 0, 512))])
            ntt = (n + 127) // 128
            for g in range(2):
                pp = ps[g]
                for kc in range(NKC):
                    mm(pp[:, 0:n], wview(sl, 0, 512, kc, 128 * g, 128 * g + 128), hT[:, kc, 0:n], kc == 0, kc == NKC - 1)
                cp(kst[:, 0:n], pp[:, 0:n])
                dma(aq(), vT_o[128 * g:128 * g + 128, c0:c0 + n], kst[:, 0:n], reads=[kst[:]])
                if KATT and (KV & 1):
                    act(hre[:, 0:n], pp[:, 0:n], AF.Copy)
                    for tti in range(ntt):
                        m = max(32, min(128, n - 128 * tti))
                        transpose(vv(ps6b[0:m, 128 * tti:128 * tti + 128], ps[7][:]), hre[:, 128 * tti:128 * tti + m], identb[:])
                    if not sample:
                        for tti in range(ntt):
                            gt = blk * 4 + tti
                            act(vS.sub(gt, (slice(None), gt, slice(128 * g, 128 * g + 128))),
                                vv(ps6b[:, 128 * tti:128 * tti + 128], ps[7][:]), AF.Copy)
                    else:
                        act(vnew[0:n, 128 * g:128 * g + 128], vv(ps6b[0:n, 0:128], ps[7][:]), AF.Copy)
            if KATT and (KV & 2):
                pw = ps[2]
                if n < 32:
                    memset(tC[0:32, 0:32], 0.0)
                for kc in range(NKC):
                    mm(pw[0:32, 0:n], wview(sl, 0, 512, kc, 256, 288), hT[:, kc, 0:n], kc == 0, kc == NKC - 1)
                ts(tC[0:32, 0:n], pw[0:32, 0:n], 0.25, None, ALU.mult)
                for tti in range(ntt):
                    m = max(32, min(128, n - 128 * tti))
                    transpose(ps[3][0:m, 32 * tti:32 * tti + 32], tC[0:32, 128 * tti:128 * tti + m], identf[0:32, 0:32])
                for tti in range(ntt):
                    m = min(128, n - 128 * tti)
                    cp(wtk.sub(tti, (slice(0, m), tti, slice(None))), ps[3][0:m, 32 * tti:32 * tti + 16])
            if stage_limit < 3:
                return False
            if not sample:
                ssm_prompt(n)
            else:
                ssm_sample()
            if stage_limit < 4:
                return False
            glu(n)
            if stage_limit < 5:
                return False
            if not sample and KATT:
                for qt in range(4):
                    attn_prompt(blk, qt)
            else:
                attn_sample()
            if stage_limit < 6:
                return False
            merge_out(n)
            return True

        def glu(n):
            sl = wload([(0, 1024, 8, wsrc(glw_d, 8, 0, 1024))])
            for j in range(8):
                pp = ps[j % 4]
                for kc in range(8):
                    mm(pp[:, 0:n], wview(sl, 0, 1024, kc, 128 * j, 128 * j + 128), slot(A_ZA + kc, n), kc == 0, kc == 7)
                tmp = tA if j % 2 == 0 else tB
                act(tmp[:, 0:n], pp[:, 0:n], AF.Sigmoid, bias=glb[:, j:j + 1])
                tt(slot(A_U + j, n), tmp[:, 0:n], slot(A_ZA + j, n), ALU.mult)

        def merge_out(n):
            for j in range(16):
                sl = wload([(0, 128, 8, wsrc(wa_d, 8, 128 * j, 128 * j + 128)),
                            (1024, 128, 8, wsrc(wb_d, 8, 128 * j, 128 * j + 128)),
                            (2048, 128, 16, wsrc(wgab_d, 16, 128 * j, 128 * j + 128)),
                            (4096, 128, 16, wsrc(wgab_d, 16, 2048 + 128 * j, 2048 + 128 * j + 128))])
                b0 = 0
                pA, pB, pga, pgb = ps[b0], ps[b0 + 1], ps[b0 + 2], ps[b0 + 3]
                for kc in range(8):
                    mm(pA[:, 0:n], wview(sl, 0, 128, kc, 0, 128), slot(A_U + kc, n), kc == 0, kc == 7)
                for kc in range(8):
                    mm(pB[:, 0:n], wview(sl, 1024, 128, kc, 0, 128), slot(A_BO + kc, n), kc == 0, kc == 7)
                for kc in range(16):
                    mm(pga[:, 0:n], wview(sl, 2048, 128, kc, 0, 128), hT[:, kc, 0:n], kc == 0, kc == 15)
                for kc in range(16):
                    mm(pgb[:, 0:n], wview(sl, 4096, 128, kc, 0, 128), hT[:, kc, 0:n], kc == 0, kc == 15)
                act(tA[:, 0:n], pga[:, 0:n], AF.Sigmoid)
                act(tB[:, 0:n], pgb[:, 0:n], AF.Sigmoid)
                tt(tC[:, 0:n], tA[:, 0:n], pA[:, 0:n], ALU.mult)
                tt(tD[:, 0:n], tB[:, 0:n], pB[:, 0:n], ALU.mult)
                tt(slot(A_Q + j, n), tC[:, 0:n], tD[:, 0:n], ALU.add)
            for i in range(4):
                sl = wload([(0, 512, NKC, wsrc(wo_d, NKC, 512 * i, 512 * i + 512))])
                for t in range(4):
                    j = 4 * i + t
                    pp = ps[j % 4]
                    for kc in range(NKC):
                        mm(pp[:, 0:n], wview(sl, 0, 512, kc, 128 * t, 128 * t + 128), slot(A_Q + kc, n), kc == 0, kc == NKC - 1)
                    xj = xT.sub(j, (slice(None), j, slice(0, n)))
                    tt(xj, pp[:, 0:n], xj, ALU.add)

        NIT = 18

        def bisect(L, lo, W, mid, cnt, gsel):
            pr = lo.ap.shape[0]
            for it in range(NIT):
                sc = float(2.0 ** -(it + 1))
                stt(mid, W, sc, lo, ALU.mult, ALU.add)
                ts(vv(mskAP[0:pr, 0:L], MSK_KEYS), vv(IscAP[0:pr, 0:L], ISC_KEYS), mid, None, ALU.is_ge, ALU.add, accum=cnt)
                ts(gsel, cnt, TOPK - 0.5, sc, ALU.is_ge, ALU.mult)
                stt(lo, gsel, W, lo, ALU.mult, ALU.add)

        def attn_prompt(blk, qt):
            G = 4 * blk + qt
            L = 128 * (G + 1)
            nkb = G + 1
            q0 = 128 * qt
            for h in range(16):
                ts(Dm.sub(h, (slice(None), h, slice(None))), identf[:], wtk[:, qt, h:h + 1], None, ALU.mult)
            for ch in range((L + 511) // 512):
                w = min(512, L - 512 * ch)
                pI = ps[2]
                for h in range(16):
                    psc = ps[h % 2]
                    kx = kiA if h % 2 == 0 else kiB
                    mm(psc[:, 0:w], slotc(A_QI + h // 2, q0, q0 + 128), kx[:, 512 * ch:512 * ch + w], True, True)
                    act(rl[h % 2][:, 0:w], psc[:, 0:w], AF.Relu, scale=0.125)
                    mm(pI[:, 0:w], Dm.sub(h, (slice(None), h, slice(None))), rl[h % 2][:, 0:w], h == 0, h == 15)
                act(Isc(512 * ch, 512 * ch + w), pI[:, 0:w], AF.Copy)
            hi, lo, W, mid, cnt, gsel = (bs[:, i:i + 1] for i in range(6))
            S.add("dve", lambda e: e.tensor_reduce(out=bs.h[:, 0:1], in_=IscAP[:, 0:L], axis=AX.X, op=ALU.max),
                  reads=[Isc(0, L)], writes=[hi])
            S.add("dve", lambda e: e.tensor_reduce(out=bs.h[:, 1:2], in_=IscAP[:, 0:L], axis=AX.X, op=ALU.min),
                  reads=[Isc(0, L)], writes=[lo])
            tt(Isc(128 * G, 128 * G + 128), Isc(128 * G, 128 * G + 128), tri[:], ALU.add)
            ts(lo, lo, -1.0, None, ALU.add)
            tt(W, hi, lo, ALU.subtract)
            ts(W, W, 1.0, None, ALU.add)
            bisect(L, lo, W, mid, cnt, gsel)
            ts(mskv(0, L), Isc(0, L), lo, None, ALU.is_ge)
            for k8 in range(0, nkb, 8):
                c8 = min(8, nkb - k8)
                for kb in range(k8, k8 + c8):
                    transpose(vv(ps6b[:, (kb - k8) * 128:(kb - k8) * 128 + 128], ps[7][:]), mskv(128 * kb, 128 * kb + 128), identb[:])
                act(vv(mskT.h[:, k8:k8 + c8, :].rearrange("p a b -> p (a b)"), mskT[:]),
                    vv(ps6b[:, 0:c8 * 128], ps[7][:]), AF.Copy)
            for hh in range(8):
                g = hh // 4
                pO, pD = ps[6], ps[3]
                for c4 in range(0, nkb, 4):
                    c = min(4, nkb - c4)
                    pS = ps[4 + ((c4 // 4) % 2)]
                    pex = pe_[(c4 // 4) % 2]
                    for kb in range(c4, c4 + c):
                        mm(pS[:, (kb - c4) * 128:(kb - c4) * 128 + 128], kTs[:, g, 128 * kb:128 * kb + 128],
                           slotc(A_Q + hh, q0, q0 + 128), True, True)
                    act(pex[:, 0:c * 128], pS[:, 0:c * 128], AF.Exp, scale=float(128 ** -0.5))
                    tt(pex[:, 0:c * 128], pex[:, 0:c * 128],
                       vv(mskT.h[:, c4:c4 + c, :].rearrange("p a b -> p (a b)"), mskT[:]), ALU.mult)
                    for kb in range(c4, c4 + c):
                        mm(pO[:, 0:128], vS[:, kb, 128 * g:128 * g + 128], pex[:, (kb - c4) * 128:(kb - c4) * 128 + 128],
                           kb == 0, kb == nkb - 1)
                        mm(pD[:, 0:128], onesb[:], pex[:, (kb - c4) * 128:(kb - c4) * 128 + 128], kb == 0, kb == nkb - 1)
                recip(tA[:, 0:128], pD[:, 0:128])
                tt(slotc(A_BO + hh, q0, q0 + 128), pO[:, 0:128], tA[:, 0:128], ALU.mult)

        def attn_sample():
            LS = PAST + NS
            arf = arena.h[:].rearrange("p s n -> p (s n)").bitcast(F32)
            ARK = arena[:]

            def IS(a, b_):
                return vv(arf[0:16, a:b_], ARK)
            junk = vv(kTs.h[:].rearrange("p a n -> p (a n)")[0:16, 0:2052], kTs[:])
            kic = vv(rl[0].h[:, :].rearrange("p (t d) -> p t d", t=8), rl[0][:])
            kiTc = vv(mskT.h[:].rearrange("p a b -> p (a b)")[0:64, 0:1024], mskT[:])
            Kc = vv(kiA.h[:, :].rearrange("p (t d) -> p t d", t=8), kiA[:])
            Vc = vv(kiB.h[:, :].rearrange("p (t d) -> p t d", t=8), kiB[:])
            KTc = vv(Dm.h[:].rearrange("p a b -> p (a b)").rearrange("p (g k) -> p g k", g=2), Dm[:])
            mTc = vv(pe_[0].h[:, 0:128].rearrange("p (t q) -> p t q", t=8), pe_[0][:])
            mskc = vv(vS.h[:].rearrange("p a b -> p (a b)")[0:16, 0:1024], vS[:])
            pexv = vv(pe_[1].h[:, 0:256].rearrange("p (g t c) -> p g t c", g=2, t=8), pe_[1][:])
            qs = vv(hre.h[:, 0:32].rearrange("p (h q) -> p h q", h=8), hre[:])
            for h in range(16):
                ts(DmS[:, h, :], identf[0:16, 0:16], wtk[0:16, 0, h:h + 1], None, ALU.mult)
            hi, lo, W, mid, cnt, gsel, c2 = (bs2[:, i:i + 1] for i in range(7))
            for bi in range(4):
                for ch in range(8):
                    S.add("pool", lambda e, ch=ch, bi=bi: e.indirect_dma_start(
                        out=rl[0].h[:, :], out_offset=None, in_=cik_d[:, :],
                        in_offset=bass.IndirectOffsetOnAxis(ap=idx8.h[:, bi, ch:ch + 1], axis=0)),
                        reads=[idx8[:]], writes=[rl[0][:]], dma=True)
                    for t in range(8):
                        transpose(vv(ps6b[0:64, 128 * t:128 * t + 128], ps[7][:]), vv(kic.ap[:, t, :], kic), identb[:])
                    act(kiTc, vv(ps6b[0:64, 0:1024], ps[7][:]), AF.Copy)
                    for sc in range(2):
                        pI = ps[2]
                        for h in range(16):
                            psc = ps[h % 2]
                            mm(psc[0:16, 0:512], sQI[0:64, h, 0:16], vv(kiTc.ap[:, 512 * sc:512 * sc + 512], kiTc), True, True)
                            act(pe_[h % 2][0:16, 0:512], psc[0:16, 0:512], AF.Relu, scale=0.125)
                            mm(pI[0:16, 0:512], DmS[:, h, :], pe_[h % 2][0:16, 0:512], h == 0, h == 15)
                        c0_ = 1024 * ch + 512 * sc
                        act(IS(c0_, c0_ + 512), pI[0:16, 0:512], AF.Copy)
                pI = ps[2]
                for h in range(16):
                    psc = ps[h % 2]
                    mm(psc[0:16, 0:16], sQI[0:64, h, 0:16], kinA[0:64, 0:16], True, True)
                    act(pe_[h % 2][0:16, 0:16], psc[0:16, 0:16], AF.Relu, scale=0.125)
                    mm(pI[0:16, 0:16], DmS[:, h, :], pe_[h % 2][0:16, 0:16], h == 0, h == 15)
                S.add("dve", lambda e: e.tensor_reduce(out=bs2.h[:, 0:1], in_=arf[0:16, 0:PAST], axis=AX.X, op=ALU.max),
                      reads=[IS(0, PAST)], writes=[hi])
                S.add("dve", lambda e: e.tensor_reduce(out=bs2.h[:, 1:2], in_=arf[0:16, 0:PAST], axis=AX.X, op=ALU.min),
                      reads=[IS(0, PAST)], writes=[lo])
                tt(IS(PAST, LS), pI[0:16, 0:16], mnew[:], ALU.add)
                ts(lo, lo, -64.0, None, ALU.add)
                ts(hi, hi, 64.0, None, ALU.add)
                tt(W, hi, lo, ALU.subtract)
                for it in range(NIT + 4):
                    scv = float(2.0 ** -(it + 1))
                    stt(mid, W, scv, lo, ALU.mult, ALU.add)
                    for q4 in range(4):
                        ts(junk, IS(2052 * q4, 2052 * q4 + 2052), mid, None, ALU.is_ge, ALU.add, accum=(cnt if q4 == 0 else c2))
                        if q4 > 0:
                            tt(cnt, cnt, c2, ALU.add)
                    ts(gsel, cnt, TOPK - 0.5, scv, ALU.is_ge, ALU.mult)
                    stt(lo, gsel, W, lo, ALU.mult, ALU.add)
                for hh in range(8):
                    cp(vv(qs.ap[:, hh, :], qs), slotc_s(A_Q + hh, 4 * bi, 4 * bi + 4))
                pO = [ps[2], ps[3]]
                pD = [ps[4], ps[5]]
                nblk_tot = 65
                for ch in range(9):
                    if ch < 8:
                        for (dst, src_d) in ((kiA, ck_d), (kiB, cv_d)):
                            S.add("pool", lambda e, ch=ch, bi=bi, dst=dst, src_d=src_d: e.indirect_dma_start(
                                out=dst.h[:, :], out_offset=None, in_=src_d[:, :],
                                in_offset=bass.IndirectOffsetOnAxis(ap=idx8.h[:, bi, ch:ch + 1], axis=0)),
                                reads=[idx8[:]], writes=[dst[:]], dma=True)
                        ts(mskc, IS(1024 * ch, 1024 * ch + 1024), lo, None, ALU.is_ge)
                        for t in range(8):
                            transpose(vv(ps6b[:, 16 * t:16 * t + 16], ps[7][:]), vv(mskc.ap[:, 128 * t:128 * t + 128], mskc), identb[0:16, 0:16])
                        act(vv(pe_[0].h[:, 0:128], pe_[0][:]), vv(ps6b[:, 0:128], ps[7][:]), AF.Copy)
                        for g in range(2):
                            for t in range(8):
                                transpose(vv(ps6b[:, 128 * t:128 * t + 128], ps[7][:]), vv(Kc.ap[:, t, 128 * g:128 * g + 128], Kc), identb[:])
                            act(vv(KTc.ap[:, g, :], KTc), vv(ps6b[:, 0:1024], ps[7][:]), AF.Copy)
                        pS = ps[0]
                        for g in range(2):
                            for t in range(8):
                                mm(pS[:, 128 * g + 16 * t:128 * g + 16 * t + 16], vv(KTc.ap[:, g, 128 * t:128 * t + 128], KTc),
                                   vv(qs.ap[:, 4 * g:4 * g + 4, :].rearrange("p h q -> p (h q)"), qs), True, True)
                        act(vv(pe_[1].h[:, 0:256], pe_[1][:]), pS[:, 0:256], AF.Exp, scale=float(128 ** -0.5))
                        for g in range(2):
                            pg_ = vv(pexv.ap[:, g, :, :].rearrange("p t (h q) -> p t h q", h=4), pexv)
                            mb_ = vv(mTc.ap[:, :, 4 * bi:4 * bi + 4].unsqueeze(2).to_broadcast([128, 8, 4, 4]), mTc)
                            tt(pg_, pg_, mb_, ALU.mult)
                        for g in range(2):
                            for t in range(8):
                                kb = 8 * ch + t
                                rhs_ = vv(pexv.ap[:, g, t, :], pexv)
                                mm(pO[g][:, 0:16], vv(Vc.ap[:, t, 128 * g:128 * g + 128], Vc), rhs_, kb == 0, False)
                                mm(pD[g][:, 0:16], onesb[:], rhs_, kb == 0, False)
                    else:
                        ts(vv(mskc.ap[:, 0:16], mskc), IS(PAST, LS), lo, None, ALU.is_ge)
                        transpose(vv(ps6b[0:16, 0:16], ps[7][:]), vv(mskc.ap[:, 0:16], mskc), identb[0:16, 0:16])
                        act(vv(pe_[0].h[0:16, 0:16], pe_[0][:]), vv(ps6b[0:16, 0:16], ps[7][:]), AF.Copy)
                        pS = ps[0]
                        for g in range(2):
                            mm(pS[0:16, 16 * g:16 * g + 16], knew[:, g, 0:16],
                               vv(qs.ap[:, 4 * g:4 * g + 4, :].rearrange("p h q -> p (h q)"), qs), True, True)
                        act(vv(pe_[1].h[0:16, 0:32], pe_[1][:]), pS[0:16, 0:32], AF.Exp, scale=float(128 ** -0.5))
                        for g in range(2):
                            pg_ = vv(pe_[1].h[0:16, 16 * g:16 * g + 16].rearrange("p (h q) -> p h q", h=4), pe_[1][:])
                            mb_ = vv(pe_[0].h[0:16, 4 * bi:4 * bi + 4].unsqueeze(1).to_broadcast([16, 4, 4]), pe_[0][:])
                            tt(pg_, pg_, mb_, ALU.mult)
                        for g in range(2):
                            rhs_ = vv(pe_[1].h[0:16, 16 * g:16 * g + 16], pe_[1][:])
                            mm(pO[g][:, 0:16], vnew[0:16, 128 * g:128 * g + 128], rhs_, False, True)
                            mm(pD[g][:, 0:16], onesb[0:16, :], rhs_, False, True)
                for g in range(2):
                    recip(tA[:, 0:16], pD[g][:, 0:16])
                    tt(tB[:, 0:16], pO[g][:, 0:16], tA[:, 0:16], ALU.mult)
                    outv = sA.multi(range(A_BO + 4 * g, A_BO + 4 * g + 4),
                                    (slice(None), slice(A_BO + 4 * g, A_BO + 4 * g + 4), slice(4 * bi, 4 * bi + 4)))
                    cp(outv, vv(tB.h[:, 0:16].rearrange("p (h q) -> p h q", h=4), tB[:]))

        def slotc_s(i, a, b_):
            return sA.sub(i, (slice(None), i, slice(a, b_)))

        def ssm_prompt(n):
            slB = wload([(0, 128, 32, Bscr_d[:, 0, :, :]), (4096, 128, 32, Bscr_d[:, 1, :, :])], extra_reads=[BSCR])
            slC = wload([(0, 128, 32, cre_d.rearrange("p (j m) -> p j m", j=32)),
                         (4096, 128, 32, cim_d.rearrange("p (j m) -> p j m", j=32))])
            for c in range(8):
                py = ps[4 + (c % 2)]
                for jj in range(4):
                    j = 4 * c + jj
                    pre, pim = ps[2 * (j % 2)], ps[2 * (j % 2) + 1]
                    ub = slot(A_U + c, n)
                    mm(pre[:, 0:n], wview(slB, 0, 128, j, 0, 128), ub, True, True)
                    mm(pim[:, 0:n], wview(slB, 4096, 128, j, 0, 128), ub, True, True)
                    fj = sfrq[:, j:j + 1]
                    ts(tA[:, 0:n], iota1[:, 0:n], fj, None, ALU.mult)
                    cp(tI[:, 0:n], tA[:, 0:n])
                    stt(tB[:, 0:n], tI[:, 0:n], -1.0, tA[:, 0:n], ALU.mult, ALU.add)
                    act(sinT[:, 0:n], tB[:, 0:n], AF.Sin, scale=float(2 * np.pi))
                    ts(tA[:, 0:n], tA[:, 0:n], 0.25, None, ALU.add)
                    cp(tI[:, 0:n], tA[:, 0:n])
                    stt(tB[:, 0:n], tI[:, 0:n], -1.0, tA[:, 0:n], ALU.mult, ALU.add)
                    act(cosT[:, 0:n], tB[:, 0:n], AF.Sin, scale=float(2 * np.pi))
                    tt(tA[:, 0:n], cosT[:, 0:n], pre[:, 0:n], ALU.mult)
                    tt(tB[:, 0:n], sinT[:, 0:n], pim[:, 0:n], ALU.mult)
                    tt(tC[:, 0:n], tA[:, 0:n], tB[:, 0:n], ALU.add)
                    tt(tA[:, 0:n], cosT[:, 0:n], pim[:, 0:n], ALU.mult)
                    tt(tB[:, 0:n], sinT[:, 0:n], pre[:, 0:n], ALU.mult)
                    tt(tD[:, 0:n], tA[:, 0:n], tB[:, 0:n], ALU.subtract)
                    dec = vv(sdec.h[:, j:j + 1].to_broadcast([128, n]), sdec[:])
                    for (cc, qq, ci) in ((tC, tE, 0), (tD, tF, 1)):
                        init = car[:, ci, j:j + 1]
                        S.add("dve", lambda e, cc=cc, qq=qq, init=init, dec=dec: e.tensor_tensor_scan(
                            out=qq.h[:, 0:n], data0=dec.ap, data1=cc.h[:, 0:n], initial=init.ap,
                            op0=ALU.mult, op1=ALU.add), reads=[cc[:], init, dec], writes=[qq[:]])
                    tt(tA[:, 0:n], cosT[:, 0:n], tE[:, 0:n], ALU.mult)
                    tt(tB[:, 0:n], sinT[:, 0:n], tF[:, 0:n], ALU.mult)
                    tt(tC[:, 0:n], tA[:, 0:n], tB[:, 0:n], ALU.subtract)
                    tt(tA[:, 0:n], sinT[:, 0:n], tE[:, 0:n], ALU.mult)
                    tt(tB[:, 0:n], cosT[:, 0:n], tF[:, 0:n], ALU.mult)
                    tt(tD[:, 0:n], tA[:, 0:n], tB[:, 0:n], ALU.add)
                    act(hre[:, 0:n], tC[:, 0:n], AF.Copy)
                    act(him[:, 0:n], tD[:, 0:n], AF.Copy, scale=-1.0)
                    cp(car[:, 0, j:j + 1], tC[:, n - 1:n])
                    cp(car[:, 1, j:j + 1], tD[:, n - 1:n])
                    mm(py[:, 0:n], wview(slC, 0, 128, j, 0, 128), hre[:, 0:n], jj == 0, False)
                    mm(py[:, 0:n], wview(slC, 4096, 128, j, 0, 128), him[:, 0:n], False, False)
                mm(py[:, 0:n], Dd.sub(c, (slice(None), c, slice(None))), slot(A_U + c, n), False, True)
                gelu_to(py, slot(A_ZA + c, n), n)

        def gelu_to(py, out, n):
            act(tA[:, 0:n], py[:, 0:n], AF.Square)
            ts(tA[:, 0:n], tA[:, 0:n], 0.044715, 1.0, ALU.mult, ALU.add)
            tt(tA[:, 0:n], tA[:, 0:n], py[:, 0:n], ALU.mult)
            act(tB[:, 0:n], tA[:, 0:n], AF.Sigmoid, scale=1.5957691216057308)
            tt(out, tB[:, 0:n], py[:, 0:n], ALU.mult)

        def ssm_sample():
            n = NS
            slB = wload([(0, 128, 32, Bscr_d[:, 0, :, :]), (4096, 128, 32, Bscr_d[:, 1, :, :])], extra_reads=[BSCR])
            slC = wload([(0, 128, 32, cre_d.rearrange("p (j m) -> p j m", j=32)),
                         (4096, 128, 32, cim_d.rearrange("p (j m) -> p j m", j=32))])
            for j in range(32):
                c = j // 4
                pre, pim = ps[2 * (j % 2)], ps[2 * (j % 2) + 1]
                mm(pre[:, 0:n], wview(slB, 0, 128, j, 0, 128), slot(A_U + c, n), True, True)
                mm(pim[:, 0:n], wview(slB, 4096, 128, j, 0, 128), slot(A_U + c, n), True, True)
                cp(tE[:, 16 * j:16 * j + 16], pre[:, 0:n])
                act(tF[:, 16 * j:16 * j + 16], pim[:, 0:n], AF.Copy)
            cp(hss[:], h0s[:])

            def v4(tX, t):
                return vv(tX.h[:].rearrange("p (j b t) -> p j b t", j=32, b=4)[:, :, :, t], tX[:])

            def v3(tX, a):
                return vv(tX.h[:, a:a + 128].rearrange("p (j b) -> p j b", j=32), tX[:])
            lr = vv(lbre.h[:, :].unsqueeze(2).to_broadcast([128, 32, 4]), lbre[:])
            li = vv(lbim.h[:, :].unsqueeze(2).to_broadcast([128, 32, 4]), lbim[:])
            A1, A2, A3, A4 = v3(tA, 0), v3(tA, 128), v3(tB, 0), v3(tB, 128)
            for t in range(4):
                hr = hss[:, 0, :, :]
                hi_ = hss[:, 1, :, :]
                tt(A1, hr, lr, ALU.mult)
                tt(A2, hi_, li, ALU.mult)
                tt(A1, A1, A2, ALU.subtract)
                tt(A3, hi_, lr, ALU.mult)
                tt(A4, hr, li, ALU.mult)
                tt(A3, A3, A4, ALU.add)
                tt(hr, A1, v4(tE, t), ALU.add)
                tt(hi_, A3, v4(tF, t), ALU.add)
                cp(v4(tC, t), hr)
                cp(v4(tD, t), hi_)
            act(hre[:], tC[:], AF.Copy)
            act(him[:], tD[:], AF.Copy, scale=-1.0)
            for c in range(8):
                py = ps[4 + (c % 2)]
                for jj in range(4):
                    j = 4 * c + jj
                    mm(py[:, 0:n], wview(slC, 0, 128, j, 0, 128), hre[:, 16 * j:16 * j + 16], jj == 0, False)
                    mm(py[:, 0:n], wview(slC, 4096, 128, j, 0, 128), him[:, 16 * j:16 * j + 16], False, False)
                mm(py[:, 0:n], Dd.sub(c, (slice(None), c, slice(None))), slot(A_U + c, n), False, True)
                gelu_to(py, slot(A_ZA + c, n), n)

        knew = sb("knew", [128, 2, NS], BF16)
        kinA = sb("kinA", [128, NS], BF16); kinB = sb("kinB", [128, NS], BF16)
        vnew = sb("vnew", [NS, 256], BF16)

        for blk in range(NBLK + 1):
            n = NB if blk < NBLK else NS
            cur["sample"] = (blk == NBLK)
            wblk["blk"] = blk
            wblk["k"] = 0
            c0 = blk * NB
            for j in range(NKC):
                dma(aq(), xT.sub(j, (slice(None), j, slice(0, n))), xT_d[128 * j:128 * j + 128, c0:c0 + n],
                    writes=[xT.sub(j, (slice(None), j, slice(0, n)))])
            if stage_limit >= 1:
                rmsnorm(n, 0)
                ffn(n, w1g_d, w1u_d, w1d_d)
            if stage_limit >= 2:
                full = mix(blk, n, c0)
                if full and stage_limit >= 7:
                    rmsnorm(n, 32)
                    ffn(n, w2g_d, w2u_d, w2d_d)
            for j in range(NKC):
                dma(aq(), yT_o[128 * j:128 * j + 128, c0:c0 + n], xT.sub(j, (slice(None), j, slice(0, n))),
                    reads=[xT.sub(j, (slice(None), j, slice(0, n)))])
        dma("sp", sp_o, vv(car.h[:].rearrange("p a j -> p (a j)"), car[:]), reads=[car[:]])
        dma("sp", ss_o, vv(hss.h[:].rearrange("p a j b -> p (a j b)"), hss[:]), reads=[hss[:]])

        S.emit_all(nc)
    return nc


def _rope_tables(pos):
    pos = np.asarray(pos, np.float32)
    T = pos.shape[0]
    out = np.zeros((4, 128, T), np.float32)
    out[0] = 1.0
    out[2] = 1.0
    half = 16
    fr = (np.float32(500000.0) ** (-np.arange(half, dtype=np.float32) * np.float32(2.0) / np.float32(32))).astype(np.float32)
    ang = (pos[:, None] * fr[None, :]).astype(np.float32)
    c, s = np.cos(ang).T.astype(np.float32), np.sin(ang).T.astype(np.float32)
    out[0, 0:16] = c; out[0, 16:32] = c
    out[1, 0:16] = -s; out[1, 16:32] = s
    half = 8
    fr = (np.float32(500000.0) ** (-np.arange(half, dtype=np.float32) * np.float32(2.0) / np.float32(16))).astype(np.float32)
    ang = (pos[:, None] * fr[None, :]).astype(np.float32)
    c, s = np.cos(ang).T.astype(np.float32), np.sin(ang).T.astype(np.float32)
    for b in (0, 64):
        out[2, b:b + 8] = c; out[2, b + 8:b + 16] = c
        out[3, b:b + 8] = -s; out[3, b + 8:b + 16] = s
    return out


def _perm_cols(w, width, half):
    n = w.shape[1]
    idx = np.arange(n)
    loc = idx % width
    src = idx.copy()
    src[loc < half] += half
    m = (loc >= half) & (loc < 2 * half)
    src[m] -= half
    return w[:, src]


_NC_CACHE = {}


def kernel(x_prompt, x_sample, cache_k, cache_v, cache_idx_k, state_ssm_re, state_ssm_im, page_table,
           ffn1_norm, ffn1_w_gate, ffn1_w_up, ffn1_w_down, mix_norm, w_in, q_norm, k_norm,
           ssm_lambda_re, ssm_lambda_im, ssm_b_re, ssm_b_im, ssm_c_re, ssm_c_im, ssm_d, ssm_log_dt,
           glu_w, glu_b, w_branch_a, w_branch_b, w_out, ffn2_norm, ffn2_w_gate, ffn2_w_up, ffn2_w_down):
    import os
    stage_limit = int(os.environ.get("KSTAGE", "99"))
    f32 = np.float32
    A = lambda a: np.ascontiguousarray(np.asarray(a), dtype=f32)
    win = A(w_in)[0]
    u_w = win[:, 0:1024]; q_w = win[:, 1024:2048]; k_w = win[:, 2048:2304]; v_w = win[:, 2304:2560]
    qi_w = win[:, 2560:3584]; ki_w = win[:, 3584:3648]; wi_w = win[:, 3648:3664]
    q_r = _perm_cols(q_w, 128, 16); k_r = _perm_cols(k_w, 128, 16)
    qi_r = _perm_cols(qi_w, 64, 8); ki_r = _perm_cols(ki_w, 64, 8)
    tiles = [u_w[:, 128 * i:128 * i + 128] for i in range(8)]
    for h in range(8):
        tiles += [q_w[:, 128 * h:128 * h + 128], q_r[:, 128 * h:128 * h + 128]]
    for g in range(2):
        tiles += [k_w[:, 128 * g:128 * g + 128], k_r[:, 128 * g:128 * g + 128]]
    for t in range(8):
        tiles += [qi_w[:, 128 * t:128 * t + 128], qi_r[:, 128 * t:128 * t + 128]]
    tiles += [np.concatenate([ki_w, ki_w], 1), np.concatenate([ki_r, ki_r], 1)]
    wmix = np.ascontiguousarray(np.concatenate(tiles, 1))
    assert wmix.shape[1] == WMIX_TILES * 128
    wtok = np.ascontiguousarray(np.concatenate([v_w, wi_w, np.zeros((D, 512 - 272), f32)], 1))
    wgab = np.ascontiguousarray(win[:, 3664:7760])
    pl = lambda g: np.ascontiguousarray(A(g).reshape(-1, 128).T)
    nrm = np.concatenate([pl(ffn1_norm[0]), pl(mix_norm[0]), pl(ffn2_norm[0])], 1)
    qn = A(q_norm)[0]; kn = A(k_norm)[0]
    qkn = np.stack([qn, _perm_cols(qn[None], 128, 16)[0], kn, _perm_cols(kn[None], 128, 16)[0]], 1)
    glb = pl(glu_b[0])
    ident = np.eye(128, dtype=f32)
    tri = np.where(np.arange(128)[None, :] <= np.arange(128)[:, None], 0.0, NEG).astype(f32)
    iota1 = np.broadcast_to(np.arange(1, NB + 1, dtype=f32)[None, :], (128, NB)).copy()
    def sl_(a):
        return np.ascontiguousarray(A(a).reshape(32, 2, 64).transpose(1, 2, 0).reshape(128, 32))
    ldt = np.repeat(A(ssm_log_dt)[0][:, None], 64, 1)
    ssmp = np.concatenate([sl_(ssm_lambda_re[0]), sl_(ssm_lambda_im[0]), sl_(ldt)], 1)
    ssmd = np.ascontiguousarray(A(ssm_d)[0].reshape(8, 128).T)
    def bs_(b):
        b = A(b).reshape(32, 2, 64, 16)
        o = np.zeros((2, 64, 32, 128), f32)
        for j in range(32):
            for g2 in range(2):
                k0 = 32 * (j % 4) + 16 * g2
                o[g2, :, j, k0:k0 + 16] = b[j, g2]
        return np.ascontiguousarray(o.reshape(128, 32 * 128))
    def cs_(c):
        c = A(c).reshape(32, 2, 16, 64)
        o = np.zeros((2, 64, 32, 128), f32)
        for j in range(32):
            for g2 in range(2):
                m0 = 32 * (j % 4) + 16 * g2
                o[g2, :, j, m0:m0 + 16] = c[j, g2].T
        return np.ascontiguousarray(o.reshape(128, 32 * 128))
    bsre = bs_(ssm_b_re[0]); bsim = bs_(ssm_b_im[0]); cre = cs_(ssm_c_re[0]); cim = cs_(ssm_c_im[0])

    xp = A(x_prompt); xs = A(x_sample)
    sre = A(state_ssm_re)[0]; sim = A(state_ssm_im)[0]
    shared = dict(w1g=A(ffn1_w_gate)[0], w1u=A(ffn1_w_up)[0], w1d=A(ffn1_w_down)[0],
                  w2g=A(ffn2_w_gate)[0], w2u=A(ffn2_w_up)[0], w2d=A(ffn2_w_down)[0],
                  wmix=wmix, wtok=wtok, wgab=wgab, glw=A(glu_w)[0], wa=A(w_branch_a)[0], wb=A(w_branch_b)[0],
                  wo=A(w_out)[0], nrm=np.ascontiguousarray(nrm), qkn=np.ascontiguousarray(qkn), glb=glb,
                  ident=ident, tri=tri, iota1=iota1, hmask=np.stack([(np.arange(128) < 64), (np.arange(128) >= 64)], 1).astype(f32), ssmp=np.ascontiguousarray(ssmp), ssmd=ssmd,
                  bsre=bsre, bsim=bsim, cre=cre, cim=cim)
    ckh = A(cache_k)[0].reshape(40960, 2048); cvh = A(cache_v)[0].reshape(40960, 2048)
    cikh = A(cache_idx_k)[0].reshape(40960, 512)
    shared.update(ckh=ckh, cvh=cvh, cikh=cikh)
    tq = np.arange(16)
    mnew = np.where((tq[None, :] // 4 == tq[:, None] // 4) & (tq[None, :] % 4 <= tq[:, None] % 4), 0.0, NEG).astype(f32)
    shared.update(mnew=mnew)
    pt_np = np.asarray(page_table).astype(np.int32)
    pos_s = np.tile(PAST + np.arange(4), 4)
    rope = _rope_tables(np.concatenate([np.arange(SEQ), pos_s]))
    in_maps = []
    for c in range(N_CORES):
        pb = c // 2
        sbs = slice(4 * c, 4 * c + 4)
        xT = np.concatenate([xp[pb].T, xs[sbs].reshape(NS, D).T], 1)
        def h0l(s):
            return s.reshape(4, 32, 2, 64).transpose(2, 3, 1, 0).reshape(128, 32 * 4)
        h0 = np.concatenate([h0l(sre[sbs]), h0l(sim[sbs])], 1)
        m = dict(shared)
        ptl_c = np.ascontiguousarray(np.concatenate([pt_np[sbs].T, pt_np[sbs].T], 0).astype(np.int32))
        m.update(xT=np.ascontiguousarray(xT), rope=rope, h0=np.ascontiguousarray(h0), ptl=ptl_c)
        in_maps.append(m)

    import time as _t
    _t0 = _t.time()
    key = stage_limit
    if key not in _NC_CACHE:
        _NC_CACHE[key] = build_nc(stage_limit)
    nc = _NC_CACHE[key]
    _t1 = _t.time()
    res = run_bass_kernel_spmd(nc, in_maps, core_ids=list(range(N_CORES)))
    if os.environ.get("KTIME"):
        print("[kernel] build %.1fs run %.1fs" % (_t1 - _t0, _t.time() - _t1), flush=True)
    R = res.results
    y_p = np.stack([R[2 * b]["yT"][:, :SEQ].T for b in range(4)])
    y_s = np.concatenate([R[c]["yT"][:, SEQ:].T.reshape(4, 4, D) for c in range(N_CORES)])
    k_p = np.stack([R[2 * b]["kTo"][:, :SEQ].T.reshape(SEQ, 2, 128) for b in range(4)])[None]
    v_p = np.stack([R[2 * b]["vTo"][:, :SEQ].T.reshape(SEQ, 2, 128) for b in range(4)])[None]
    ik_p = np.stack([R[2 * b]["kiTo"][:64, :SEQ].T for b in range(4)])[None]
    def st_p(r, a):
        return r["ssp"][:, 32 * a:32 * a + 32].reshape(2, 64, 32).transpose(2, 0, 1).reshape(64, 64)
    re_p = np.stack([st_p(R[2 * b], 0) for b in range(4)])[None]
    im_p = np.stack([st_p(R[2 * b], 1) for b in range(4)])[None]
    k_s = np.concatenate([R[c]["kTo"][:, SEQ:].T.reshape(4, 4, 2, 128) for c in range(N_CORES)])[None]
    v_s = np.concatenate([R[c]["vTo"][:, SEQ:].T.reshape(4, 4, 2, 128) for c in range(N_CORES)])[None]
    ik_s = np.concatenate([R[c]["kiTo"][:64, SEQ:].T.reshape(4, 4, 64) for c in range(N_CORES)])[None]
    def st_s(r, a):
        return r["sss"][:, 128 * a:128 * a + 128].reshape(2, 64, 32, 4).transpose(3, 2, 0, 1).reshape(4, 64, 64)
    re_s = np.concatenate([st_s(R[c], 0) for c in range(N_CORES)])[None]
    im_s = np.concatenate([st_s(R[c], 1) for c in range(N_CORES)])[None]
    outs = (y_p, y_s, k_p, v_p, ik_p, re_p, im_p, k_s, v_s, ik_s, re_s, im_s)
    return tuple(np.ascontiguousarray(o, dtype=np.float32) for o in outs)
```
